# Optimizing a Trainium2 kernel written in Bass

```python
import math
import jax, jax.numpy as jnp
from jax import lax
import numpy as np

D_MODEL = 1024
BATCH = 8
SEQ = 4096
DEPTH = 1

D_MIX = 2 * D_MODEL
ATTN_WIDTH = D_MIX // 2
SB_HEAD_DIM = 64
SB_HEADS = ATTN_WIDTH // SB_HEAD_DIM
SB_BLOCK = 128
SSD_WIDTH = D_MIX - ATTN_WIDTH
SSD_HEAD_DIM = 64
SSD_HEADS = SSD_WIDTH // SSD_HEAD_DIM
SSD_GROUPS = 2
SSD_STATE = 128
SSD_CONV_K = 4
SSD_CHUNK = 128
SSD_CONV_CH = SSD_WIDTH + 2 * SSD_GROUPS * SSD_STATE
IN_SPLITS = (ATTN_WIDTH, ATTN_WIDTH, ATTN_WIDTH, SSD_WIDTH, SSD_WIDTH,
             SSD_GROUPS * SSD_STATE, SSD_GROUPS * SSD_STATE, SSD_HEADS)
D_IN = sum(IN_SPLITS)
MEM_LEN = 256
XA_HEADS = 4
XA_HEAD_DIM = D_MODEL // XA_HEADS
D_FF = 2816
EPS = 1e-6

kernel_name = "hybrid_sb_ssd_macaron_layer"


def rmsnorm(x, g):
    x32 = x.astype(jnp.float32)
    y = x32 * lax.rsqrt(jnp.mean(x32 * x32, axis=-1, keepdims=True) + EPS)
    return (y * g.astype(jnp.float32)).astype(x.dtype)


def swiglu(h, w_gu, w_down):
    gu = h @ w_gu
    g, u = jnp.split(gu, 2, axis=-1)
    return (jax.nn.silu(g) * u) @ w_down


def stick_breaking_attention(q, k, v):
    b_, h_, s_, d_ = q.shape
    scale = 1.0 / math.sqrt(d_)
    outs = []
    for i in range(s_ // SB_BLOCK):
        t0, t1 = i * SB_BLOCK, (i + 1) * SB_BLOCK
        qb = q[:, :, t0:t1].astype(jnp.float32)
        kb = k[:, :, :t1].astype(jnp.float32)
        vb = v[:, :, :t1]
        z = jnp.einsum('bhtd,bhsd->bhts', qb, kb) * scale
        t_idx = t0 + jnp.arange(SB_BLOCK)
        s_idx = jnp.arange(t1)
        causal = s_idx[None, :] < t_idx[:, None]
        log_1mb = jnp.where(causal, jax.nn.log_sigmoid(-z), 0.0)
        suffix = lax.cumsum(log_1mb, axis=3, reverse=True) - log_1mb
        log_a = jnp.where(causal, jax.nn.log_sigmoid(z) + suffix, -jnp.inf)
        a = jnp.exp(log_a)
        outs.append(jnp.einsum('bhts,bhsd->bhtd', a, vb.astype(jnp.float32)))
    return jnp.concatenate(outs, axis=2).astype(v.dtype)


def ssd_chunked(xs, dt, a, bmat, cmat):
    b_, s_, h_, p_ = xs.shape
    L, G, N = SSD_CHUNK, SSD_GROUPS, SSD_STATE
    R = h_ // G
    nc = s_ // L
    x = (xs.astype(jnp.float32) * dt[..., None]).reshape(b_, nc, L, G, R, p_)
    adt = (dt * a).reshape(b_, nc, L, G, R).transpose(0, 3, 4, 1, 2)
    a_cs = jnp.cumsum(adt, axis=-1)
    bc = bmat.astype(jnp.float32).reshape(b_, nc, L, G, N)
    cc = cmat.astype(jnp.float32).reshape(b_, nc, L, G, N)
    tri = jnp.tril(jnp.ones((L, L), dtype=bool))
    diff = a_cs[..., :, None] - a_cs[..., None, :]
    lmat = jnp.exp(jnp.where(tri, diff, -jnp.inf))
    cb = jnp.einsum('bclgn,bcsgn->bcgls', cc, bc)
    y_diag = jnp.einsum('bcgls,bgrcls,bcsgrp->bclgrp', cb, lmat, x)
    decay_states = jnp.exp(a_cs[..., -1:] - a_cs)
    states = jnp.einsum('bclgn,bgrcl,bclgrp->bcgrpn', bc, decay_states, x)
    chunk_decay = jnp.exp(a_cs[..., -1])

    def step(h, inp):
        st, dec = inp
        return h * dec[..., None, None] + st, h

    h0 = jnp.zeros_like(states[:, 0])
    _, states_in = lax.scan(step, h0, (jnp.moveaxis(states, 1, 0),
                                       jnp.moveaxis(chunk_decay, -1, 0)))
    states_in = jnp.moveaxis(states_in, 0, 1)
    y_off = jnp.einsum('bclgn,bcgrpn,bgrcl->bclgrp', cc, states_in, jnp.exp(a_cs))
    return (y_diag + y_off).reshape(b_, s_, h_, p_)


def hybrid_mixer(u, w_in, conv_w, conv_b, dt_bias, a_log, d_skip,
                 ssd_norm_g, attn_norm_g, w_out):
    b_, s_, _ = u.shape
    proj = u @ w_in
    q, k, v, z, xs, bm, cm, dt_raw = jnp.split(proj, np.cumsum(IN_SPLITS)[:-1], axis=-1)
    def heads(t):
        return t.reshape(b_, s_, SB_HEADS, SB_HEAD_DIM).transpose(0, 2, 1, 3)
    o_att = stick_breaking_attention(heads(q), heads(k), heads(v))
    o_att = o_att.transpose(0, 2, 1, 3).reshape(b_, s_, ATTN_WIDTH)
    o_att = rmsnorm(o_att, attn_norm_g)
    xbc = jnp.concatenate([xs, bm, cm], axis=-1)
    xbc = lax.conv_general_dilated(
        xbc, conv_w.astype(xbc.dtype)[:, None, :], window_strides=(1,),
        padding=[(SSD_CONV_K - 1, 0)], dimension_numbers=('NWC', 'WIO', 'NWC'),
        feature_group_count=SSD_CONV_CH)
    xbc = jax.nn.silu(xbc + conv_b)
    xs, bm, cm = jnp.split(xbc, [SSD_WIDTH, SSD_WIDTH + SSD_GROUPS * SSD_STATE], axis=-1)
    xs = xs.reshape(b_, s_, SSD_HEADS, SSD_HEAD_DIM)
    bm = bm.reshape(b_, s_, SSD_GROUPS, SSD_STATE)
    cm = cm.reshape(b_, s_, SSD_GROUPS, SSD_STATE)
    dt = jax.nn.softplus(dt_raw.astype(jnp.float32) + dt_bias.astype(jnp.float32))
    a = -jnp.exp(a_log.astype(jnp.float32))
    y = ssd_chunked(xs, dt, a, bm, cm)
    y = y + d_skip.astype(jnp.float32)[:, None] * xs.astype(jnp.float32)
    y = y.reshape(b_, s_, SSD_WIDTH) * jax.nn.silu(z.astype(jnp.float32))
    y = rmsnorm(y.reshape(b_, s_, SSD_GROUPS, SSD_WIDTH // SSD_GROUPS),
                ssd_norm_g.reshape(SSD_GROUPS, SSD_WIDTH // SSD_GROUPS))
    o_ssd = y.reshape(b_, s_, SSD_WIDTH).astype(u.dtype)
    return jnp.concatenate([o_att, o_ssd], axis=-1) @ w_out


def memory_cross_attention(h, mem_n, wq, wkv, wo):
    b_, s_, _ = h.shape
    q = (h @ wq).reshape(b_, s_, XA_HEADS, XA_HEAD_DIM)
    kv = mem_n @ wkv
    k, v = jnp.split(kv, 2, axis=-1)
    k = k.reshape(b_, MEM_LEN, XA_HEADS, XA_HEAD_DIM)
    v = v.reshape(b_, MEM_LEN, XA_HEADS, XA_HEAD_DIM)
    sc = jnp.einsum('bshd,bmhd->bhsm', q.astype(jnp.float32), k.astype(jnp.float32))
    p = jax.nn.softmax(sc / math.sqrt(XA_HEAD_DIM), axis=-1)
    o = jnp.einsum('bhsm,bmhd->bshd', p, v.astype(jnp.float32)).astype(h.dtype)
    return o.reshape(b_, s_, D_MODEL) @ wo


def setup_inputs(seed: int = 0) -> dict:
    key = jax.random.key(seed)
    ks = iter(jax.random.split(key, 40))

    def w(shape, fan_in):
        return jax.random.normal(next(ks), (DEPTH,) + shape, jnp.float32) * fan_in ** -0.5

    def gain(n):
        return 1.0 + 0.02 * jax.random.normal(next(ks), (DEPTH, n), jnp.float32)

    x = jax.random.normal(next(ks), (BATCH, SEQ, D_MODEL), jnp.float32)
    mem = jax.random.normal(next(ks), (BATCH, MEM_LEN, D_MODEL), jnp.float32)
    dt0 = jnp.exp(jax.random.uniform(next(ks), (DEPTH, SSD_HEADS), jnp.float32,
                                     math.log(1e-3), math.log(1e-1)))
    dt_bias = dt0 + jnp.log(-jnp.expm1(-dt0))
    a_log = jnp.log(jax.random.uniform(next(ks), (DEPTH, SSD_HEADS), jnp.float32, 1.0, 16.0))
    return {
        "x": x, "mem": mem,
        "ffn1_pre_g": gain(D_MODEL),
        "ffn1_w_gu": w((D_MODEL, 2 * D_FF), D_MODEL),
        "ffn1_w_down": w((D_FF, D_MODEL), D_FF),
        "ffn1_post_g": gain(D_MODEL),
        "mix_pre_g": gain(D_MODEL),
        "w_in": w((D_MODEL, D_IN), D_MODEL),
        "conv_w": w((SSD_CONV_K, SSD_CONV_CH), SSD_CONV_K),
        "conv_b": 0.02 * jax.random.normal(next(ks), (DEPTH, SSD_CONV_CH), jnp.float32),
        "dt_bias": dt_bias,
        "a_log": a_log,
        "d_skip": 1.0 + 0.1 * jax.random.normal(next(ks), (DEPTH, SSD_HEADS), jnp.float32),
        "ssd_norm_g": gain(SSD_WIDTH),
        "attn_norm_g": gain(ATTN_WIDTH),
        "w_out": w((D_MIX, D_MODEL), D_MIX),
        "mix_post_g": gain(D_MODEL),
        "xa_pre_g": gain(D_MODEL),
        "mem_g": gain(D_MODEL),
        "xa_wq": w((D_MODEL, D_MODEL), D_MODEL),
        "xa_wkv": w((D_MODEL, 2 * D_MODEL), D_MODEL),
        "xa_wo": w((D_MODEL, D_MODEL), D_MODEL),
        "xa_post_g": gain(D_MODEL),
        "ffn2_pre_g": gain(D_MODEL),
        "ffn2_w_gu": w((D_MODEL, 2 * D_FF), D_MODEL),
        "ffn2_w_down": w((D_FF, D_MODEL), D_FF),
        "ffn2_post_g": gain(D_MODEL),
    }


def reference(x, mem, ffn1_pre_g, ffn1_w_gu, ffn1_w_down, ffn1_post_g,
              mix_pre_g, w_in, conv_w, conv_b, dt_bias, a_log, d_skip,
              ssd_norm_g, attn_norm_g, w_out, mix_post_g,
              xa_pre_g, mem_g, xa_wq, xa_wkv, xa_wo, xa_post_g,
              ffn2_pre_g, ffn2_w_gu, ffn2_w_down, ffn2_post_g):
    h = x
    for l in range(DEPTH):
        f = swiglu(rmsnorm(h, ffn1_pre_g[l]), ffn1_w_gu[l], ffn1_w_down[l])
        h = h + 0.5 * rmsnorm(f, ffn1_post_g[l])
        m = hybrid_mixer(rmsnorm(h, mix_pre_g[l]), w_in[l], conv_w[l], conv_b[l],
                         dt_bias[l], a_log[l], d_skip[l], ssd_norm_g[l],
                         attn_norm_g[l], w_out[l])
        h = h + rmsnorm(m, mix_post_g[l])
        c = memory_cross_attention(rmsnorm(h, xa_pre_g[l]), rmsnorm(mem, mem_g[l]),
                                   xa_wq[l], xa_wkv[l], xa_wo[l])
        h = h + rmsnorm(c, xa_post_g[l])
        f = swiglu(rmsnorm(h, ffn2_pre_g[l]), ffn2_w_gu[l], ffn2_w_down[l])
        h = h + 0.5 * rmsnorm(f, ffn2_post_g[l])
    return h
```

```python
import math
from contextlib import ExitStack

import numpy as np
import concourse.bass as bass
import concourse.mybir as mybir
from concourse.bass_utils import run_bass_kernel_spmd

F32 = mybir.dt.float32
BF16 = mybir.dt.bfloat16
AF = mybir.ActivationFunctionType
ALU = mybir.AluOpType
AX = mybir.AxisListType

ENGS = ("pe", "act", "dve", "pool", "sp")


class Buf:
    def __init__(self, name, excl=False):
        self.name = name
        self.excl = excl
        self.w = None
        self.r = []


class Op:
    __slots__ = ("eng", "fn", "deps", "flag", "val", "dma_sem", "dma_val", "ndma", "tag", "phase")

    def __init__(self, eng, fn):
        self.phase = 0
        self.eng = eng
        self.fn = fn
        self.deps = []
        self.flag = False
        self.val = None
        self.dma_sem = None
        self.dma_val = None
        self.ndma = 0
        self.tag = None


class Sched:
    def __init__(self, nc):
        self.nc = nc
        self.handles = {"pe": nc.tensor, "act": nc.scalar, "dve": nc.vector,
                        "pool": nc.gpsimd, "sp": nc.sync}
        self.sem = {e: nc.alloc_semaphore("s_" + e) for e in ("pe", "act", "dve", "pool")}
        self.count = {e: 0 for e in self.sem}
        self.ops = {e: [] for e in ENGS}
        self.last = {e: None for e in ENGS}
        self.dmasems = {}
        self.waited = {e: {} for e in ENGS}
        self.n_inst = 0
        self.prev_tokens = []
        self.phase = 0
        self.defer = False
        self.replaying = False
        self.deferq = []

    def _dep(self, op, p):
        if p is None or p is op or p.phase != self.phase:
            return
        if p.eng == "pe" and op.eng == "pe" and p.dma_sem is None:
            return
        op.deps.append(p)
        if p.dma_sem is None:
            p.flag = True

    def replay(self, n):
        q = self.deferq
        k = 0
        self.replaying = True
        while q and k < n:
            kind, args, kw = q.pop(0)
            if kind == "op":
                self.op(*args, **kw)
            else:
                self.dma(*args, **kw)
            k += 1
        self.replaying = False
        return len(q)

    def op(self, eng, fn, r=(), w=()):
        if self.defer and not self.replaying:
            self.deferq.append(("op", (eng, fn), {"r": list(r), "w": list(w)}))
            return None
        o = Op(eng, fn)
        o.phase = self.phase
        w = list(w) + [b for b in r if b.excl]
        r = [b for b in r if not b.excl]
        for b in r:
            self._dep(o, b.w)
        for b in w:
            self._dep(o, b.w)
            for x in b.r:
                self._dep(o, x)
        for b in r:
            b.r.append(o)
        for b in w:
            b.w = o
            b.r = []
        self.ops[eng].append(o)
        self.last[eng] = o
        return o

    def dma(self, q, pairs, r=(), w=(), key=None, **kw):
        if self.defer and not self.replaying:
            kw2 = dict(kw)
            kw2.update({"r": list(r), "w": list(w), "key": key})
            self.deferq.append(("dma", (q, pairs), kw2))
            return None
        if key is None:
            key = (w[0].name if w else r[0].name) + ("_ld" if w else "_st")
        if key not in self.dmasems:
            self.dmasems[key] = [self.nc.alloc_semaphore("d_" + str(len(self.dmasems))), 0]
        ent = self.dmasems[key]
        sem = ent[0]

        def fn(e, pairs=pairs, sem=sem, kw=kw):
            for (o_, i_) in pairs:
                e.dma_start(out=o_, in_=i_, **kw).then_inc(sem, 16)
            return None

        o = self.op(q, fn, r=r, w=w)
        ent[1] += 16 * len(pairs)
        o.dma_sem = sem
        o.dma_val = ent[1]
        o.ndma = len(pairs)
        return o

    def deferred_flush(self):
        if not hasattr(self, "phases"):
            self.phases = []
        self.phases.append(self.ops)
        self.ops = {e: [] for e in ENGS}
        self.phase += 1

    def emit_all(self):
        comp = ("pe", "act", "dve", "pool")
        sched = self
        plan = []
        prev = []
        for ops in self.phases:
            for e in comp:
                for o in reversed(ops[e]):
                    if o.dma_sem is None and o.fn is not None:
                        o.flag = True
                        break
            for e in comp:
                c = self.count[e]
                for o in ops[e]:
                    if o.dma_sem is None and o.flag:
                        c += 1
                        o.val = c
                self.count[e] = c
            plan.append((ops, list(prev)))
            prev = [(self.sem[e], self.count[e]) for e in comp if self.count[e] > 0]
            dm = {}
            for e in ENGS:
                for o in ops[e]:
                    if o.dma_sem is not None:
                        dm[id(o.dma_sem)] = (o.dma_sem, max(o.dma_val, dm.get(id(o.dma_sem), (None, 0))[1]))
            self._dm_seen = getattr(self, "_dm_seen", {})
            self._dm_seen.update(dm)
            prev += list(self._dm_seen.values())
        final_toks = list(prev)

        def emit_waits(eh, waited, toks):
            for sem, val in toks:
                k = id(sem)
                if waited.get(k, 0) >= val:
                    continue
                eh.wait_ge(sem, val)
                waited[k] = val

        def emit(ename, eh):
            waited = sched.waited[ename]
            for ops, prevt in plan:
                emit_waits(eh, waited, prevt)
                for o in ops[ename]:
                    toks = []
                    for p in o.deps:
                        if p.dma_sem is not None:
                            toks.append((p.dma_sem, p.dma_val))
                        else:
                            toks.append((sched.sem[p.eng], p.val))
                    emit_waits(eh, waited, toks)
                    ins = o.fn(eh)
                    if o.dma_sem is None and o.flag:
                        ins.then_inc(sched.sem[ename], 1)
            emit_waits(eh, waited, final_toks)

        with self.nc.Block() as block:
            @block.tensor
            def _(e):
                emit("pe", e)

            @block.scalar
            def _(e):
                emit("act", e)

            @block.vector
            def _(e):
                emit("dve", e)

            @block.gpsimd
            def _(e):
                emit("pool", e)

            @block.sync
            def _(e):
                emit("sp", e)

    def flush(self, final=False):
        comp = ("pe", "act", "dve", "pool")
        for e in comp:
            for o in reversed(self.ops[e]):
                if o.dma_sem is None and o.fn is not None:
                    o.flag = True
                    break
        for e in comp:
            c = self.count[e]
            for o in self.ops[e]:
                if o.dma_sem is None and o.flag:
                    c += 1
                    o.val = c
            self.count[e] = c
        sched = self
        prev = list(self.prev_tokens)

        def emit_waits(eh, waited, toks):
            for sem, val in toks:
                k = id(sem)
                if waited.get(k, 0) >= val:
                    continue
                eh.wait_ge(sem, val)
                waited[k] = val

        def emit(ename, eh):
            waited = sched.waited[ename]
            emit_waits(eh, waited, prev)
            for o in sched.ops[ename]:
                toks = []
                for p in o.deps:
                    if p.dma_sem is not None:
                        toks.append((p.dma_sem, p.dma_val))
                    else:
                        assert p.val is not None
                        toks.append((sched.sem[p.eng], p.val))
                emit_waits(eh, waited, toks)
                ins = o.fn(eh)
                sched.n_inst += 1
                if o.dma_sem is None and o.flag:
                    ins.then_inc(sched.sem[ename], 1)

        def run_block(fn_for):
            with self.nc.Block() as block:
                @block.tensor
                def _(e):
                    fn_for("pe", e)

                @block.scalar
                def _(e):
                    fn_for("act", e)

                @block.vector
                def _(e):
                    fn_for("dve", e)

                @block.gpsimd
                def _(e):
                    fn_for("pool", e)

                @block.sync
                def _(e):
                    fn_for("sp", e)

        run_block(emit)
        self.ops = {e: [] for e in ENGS}
        self.phase += 1
        self.prev_tokens = [(self.sem[e], self.count[e]) for e in comp if self.count[e] > 0]
        self.prev_tokens += [(s, c) for (s, c) in self.dmasems.values() if c > 0]
        if final:
            toks = list(self.prev_tokens)

            def fin(ename, eh):
                emit_waits(eh, sched.waited[ename], toks)

            run_block(fin)


D = 1024
SEQ = 4096
NT = SEQ // 128
DFF = 2816
NJ = DFF // 128
D_IN = 5648
MEM = 256
EPS = 1e-6
NEG = -30000.0
N_WARM = 0
SSD_1BANK = False
OVERLAP_SSD = True
SIDE_EVERY = 4


class Ctx:
    pass


def build_nc(debug=False, stages=("ffn1", "inproj", "attn", "ssd", "outxa", "ffn2")):
    nc = bass.Bass("TRN2", target_bir_lowering=False)
    S = Sched(nc)
    C = Ctx()
    C.nc, C.S = nc, S

    def din(name, shape):
        return nc.dram_tensor(name, list(shape), F32, kind="ExternalInput")

    I = {}
    I["x"] = din("x", [SEQ, D])
    I["mem"] = din("mem", [MEM, D])
    for nm, shp in [("ffn1_pre_g", [1, D]), ("ffn1_w_gu", [D, 2 * DFF]), ("ffn1_w_down", [DFF, D]),
                    ("ffn1_post_g", [1, D]), ("mix_pre_g", [1, D]), ("w_in", [D, D_IN]),
                    ("conv_w", [4, 1536]), ("conv_b", [1, 1536]), ("dt_bias", [1, 16]),
                    ("a_log", [1, 16]), ("d_skip", [1, 16]), ("ssd_norm_g", [1, D]),
                    ("attn_norm_g", [1, D]), ("w_out", [2 * D, D]), ("mix_post_g", [1, D]),
                    ("xa_pre_g", [1, D]), ("mem_g", [1, D]), ("xa_wq", [D, D]), ("xa_wkv", [D, 2 * D]),
                    ("xa_wo", [D, D]), ("xa_post_g", [1, D]), ("ffn2_pre_g", [1, D]),
                    ("ffn2_w_gu", [D, 2 * DFF]), ("ffn2_w_down", [DFF, D]), ("ffn2_post_g", [1, D])]:
        I[nm] = din(nm, shp)
    out = nc.dram_tensor("out", [SEQ, D], F32, kind="ExternalOutput")
    kd = "ExternalOutput" if debug else "Internal"
    W = {}
    W["h1"] = nc.dram_tensor("h1", [SEQ, D], F32, kind=kd)
    W["qT"] = nc.dram_tensor("qT", [D, SEQ], BF16, kind=kd)
    W["kT"] = nc.dram_tensor("kT", [D, SEQ], BF16, kind=kd)
    W["v"] = nc.dram_tensor("v", [SEQ, D], BF16, kind=kd)
    W["z"] = nc.dram_tensor("z", [SEQ, D], F32, kind=kd)
    W["xbcT"] = nc.dram_tensor("xbcT", [1536, SEQ], F32, kind=kd)
    W["dtr"] = nc.dram_tensor("dtr", [SEQ, 16], F32, kind=kd)
    W["oattT"] = nc.dram_tensor("oattT", [D, SEQ], F32, kind=kd)
    W["ossdT"] = nc.dram_tensor("ossdT", [D, SEQ], BF16, kind=kd)
    W["h3"] = nc.dram_tensor("h3", [SEQ, D], F32, kind=kd)
    C.I, C.W, C.out = I, W, out

    with ExitStack() as es0:
        def sb0(name, shape, dt):
            return es0.enter_context(nc.sbuf_tensor(name, shape, dt))

        C.PP = [es0.enter_context(nc.psum_tensor("pp%d" % i, [128, 1024], F32)) for i in range(4)]
        C.PB = [[Buf("pb%d_%d" % (i, h), excl=True) for h in range(2)] for i in range(4)]

        cst = {}
        bc = Buf("consts")
        C.bc = bc
        onesF = sb0("onesF", [128, 128], F32)
        negF = sb0("negF", [128, 128], F32)
        bigF = sb0("bigF", [128, 128], F32)
        S.op("pool", lambda e: e.memset(onesF[:, :], 1.0), w=[bc])
        S.op("pool", lambda e: e.memset(negF[:, :], -1.0), w=[bc])
        S.op("pool", lambda e: e.memset(bigF[:, :], NEG), w=[bc])

        scratchF = sb0("cscratch_f", [128, 128], F32)

        def mk(name, src, cm, step, cmp, dt=F32):
            tf = sb0(name + "_f", [128, 128], F32) if dt == F32 else scratchF
            S.op("pool", lambda e: e.affine_select(out=tf[:, :], in_=src[:, :], pattern=[[step, 128]],
                                                   compare_op=cmp, fill=0.0, base=0, channel_multiplier=cm),
                 r=[bc], w=[bc])
            if dt == F32:
                return tf
            tb = sb0(name + "_b", [128, 128], BF16)
            S.op("pool", lambda e: e.tensor_copy(out=tb[:, :], in_=tf[:, :]), r=[bc], w=[bc])
            return tb

        C.onesF = onesF
        C.identF = mk("ident", onesF, 1, -1, ALU.is_equal)
        C.identB = mk("identb", onesF, 1, -1, ALU.is_equal, BF16)
        C.triNegB = mk("trineg", negF, 1, -1, ALU.is_ge, BF16)
        C.mask01 = mk("mask01", onesF, -1, 1, ALU.is_gt)
        C.negMaskB = mk("negmask", bigF, 1, -1, ALU.is_ge, BF16)
        C.triInclF = mk("triincl", onesF, -1, 1, ALU.is_ge)
        C.triUpF = mk("triup", onesF, 1, -1, ALU.is_gt)
        C.triUpB = mk("triupb", onesF, 1, -1, ALU.is_gt, BF16)
        C.negMaskSB = mk("negmasks", bigF, 1, -1, ALU.is_gt, BF16)
        onesNegB = sb0("onesNegB", [128, 128], BF16)
        S.op("pool", lambda e: e.tensor_copy(out=onesNegB[:, :], in_=negF[:, :]), r=[bc], w=[bc])
        C.onesNegB = onesNegB
        onesB = sb0("onesB", [128, 128], BF16)
        S.op("pool", lambda e: e.tensor_copy(out=onesB[:, :], in_=onesF[:, :]), r=[bc], w=[bc])
        C.onesB = onesB
        S.deferred_flush()

        if "ffn1" in stages:
            ffn_phase(C, I["x"], W["h1"], I["ffn1_w_gu"], I["ffn1_w_down"], I["ffn1_pre_g"], I["ffn1_post_g"])
        if "inproj" in stages:
            inproj_phase(C)
        if "attn" in stages and "ssd" in stages and OVERLAP_SSD:
            attn_phase(C, side=ssd_steps(C))
        else:
            if "attn" in stages:
                attn_phase(C)
            if "ssd" in stages:
                ssd_phase_1bank(C) if SSD_1BANK else ssd_phase(C)
        if "outxa" in stages:
            outxa_phase(C)
        if "ffn2" in stages:
            ffn_phase(C, W["h3"], out, I["ffn2_w_gu"], I["ffn2_w_down"], I["ffn2_pre_g"], I["ffn2_post_g"])
        S.deferred_flush()
        S.emit_all()
    return nc


class Rot:
    def __init__(self, items):
        self.items = items
        self.i = 0

    def next(self):
        it = self.items[self.i % len(self.items)]
        self.i += 1
        return it


def norm_A(C, pool, x_ap, bx):
    S = C.S
    ss, b_ss = pool["ss"].next()
    junk, b_junk = pool["junk"]
    xn, b_xn = pool["xn"]
    S.op("act", lambda e: e.activation(out=junk[:, :], in_=x_ap, func=AF.Square, accum_out=ss[:, 0:1]),
         r=[bx], w=[b_junk, b_ss])
    S.op("act", lambda e: e.activation(out=ss[:, 1:2], in_=ss[:, 0:1], func=AF.Ln, scale=1.0 / D, bias=EPS),
         r=[b_ss], w=[b_ss])
    S.op("act", lambda e: e.activation(out=ss[:, 2:3], in_=ss[:, 1:2], func=AF.Exp, scale=-0.5),
         r=[b_ss], w=[b_ss])
    S.op("dve", lambda e: e.tensor_scalar(out=xn[:, :], in0=x_ap, scalar1=ss[:, 2:3], scalar2=None, op0=ALU.mult),
         r=[bx, b_ss], w=[b_xn])


def norm_B(C, pool, gT, b_gT, dstT, col0, b_dst, pp, pb):
    S = C.S
    xn, b_xn = pool["xn"]
    for k in range(8):
        S.op("pe", lambda e, k=k: e.transpose(out=pp[:, k * 128:(k + 1) * 128], in_=xn[:, k * 128:(k + 1) * 128],
                                              identity=C.identF[:, :]),
             r=[b_xn, C.bc], w=[pb[k // 4]])
    for h in range(2):
        S.op("dve", lambda e, h=h: e.tensor_tensor(
            out=dstT[:, h * 4:(h + 1) * 4, col0:col0 + 128],
            in0=pp[:, h * 512:(h + 1) * 512].rearrange("p (k t) -> p k t", k=4),
            in1=gT[:, h * 4:(h + 1) * 4].unsqueeze(2).to_broadcast([128, 4, 128]), op=ALU.mult),
            r=[pb[h], b_gT], w=[b_dst])


def norm_T(C, pool, x_ap, bx, gT, b_gT, dstT, col0, b_dst, pp, pb):
    norm_A(C, pool, x_ap, bx)
    norm_B(C, pool, gT, b_gT, dstT, col0, b_dst, pp, pb)


def load_w_bf16(C, dst, w_dram, row0, nk, col_groups, bufs, dcol0=0):
    S = C.S
    for (c0, c1), b in zip(col_groups, bufs):
        pairs = []
        for k in range(nk):
            for s0 in range(c0, c1, 2048):
                s1 = min(c1, s0 + 2048)
                pairs.append((dst[:, k, dcol0 + s0:dcol0 + s1],
                              w_dram[row0 + k * 128:row0 + (k + 1) * 128, s0:s1]))
        S.dma("pool", pairs, w=[b])


def load_gT(C, dst, g_dram, b, n=8):
    C.S.dma("sp", [(dst[:, 0:n], g_dram[0, :].rearrange("(k p) -> p k", p=128))], w=[b],
            allow_slow_non_contiguous=True)


def load_bcast(C, dst, g_dram, b, n):
    C.S.dma("sp", [(dst[:, 0:n], g_dram[0:1, :].partition_broadcast(128))], w=[b])


def ffn_phase(C, src, dst, w_gu, w_dn, g_pre, g_post):
    nc, S = C.nc, C.S
    PP, PB = C.PP, C.PB
    with ExitStack() as es:
        def sb(name, shape, dt):
            return es.enter_context(nc.sbuf_tensor(name + "_p%d" % S.phase, shape, dt))
        wgu = sb("wgu", [128, 8, 2 * DFF], BF16)
        b_wgu = [Buf("wgu%d" % i) for i in range(8)]
        wd = sb("wd", [128, NJ, D], BF16)
        b_wd = [Buf("wd0"), Buf("wd1")]
        gT = sb("gT", [128, 8], F32); b_gT = Buf("gT")
        gb = sb("gb", [128, D], F32); b_gb = Buf("gb")
        load_gT(C, gT, g_pre, b_gT)
        load_bcast(C, gb, g_post, b_gb, D)
        grp = []
        for q4 in range(4):
            grp += [(q4 * 704, (q4 + 1) * 704), (DFF + q4 * 704, DFF + (q4 + 1) * 704)]
        load_w_bf16(C, wgu, w_gu, 0, 8, grp, [b_wgu[c0 // 704] for (c0, c1) in grp])
        for half in range(2):
            pairs = [(wd[:, j, :], w_dn[j * 128:(j + 1) * 128, :]) for j in range(half * 11, half * 11 + 11)]
            S.dma("pool", pairs, w=[b_wd[half]])

        def wgu_bufs(col):
            return sorted({b_wgu[col // 704], b_wgu[(col + 127) // 704]}, key=id)

        xin = Rot([(sb("xin%d" % i, [128, D], F32), Buf("xin%d" % i)) for i in range(2)])
        xres = Rot([(sb("xres%d" % i, [128, D], F32), Buf("xres%d" % i)) for i in range(1)])
        pool = {"ss": Rot([(sb("ss%d" % i, [128, 4], F32), Buf("ss%d" % i)) for i in range(4)]),
                "junk": (sb("junk", [128, D], BF16), Buf("junk")),
                "xn": (sb("xn", [128, D], F32), Buf("xn"))}
        t1, b_t1 = sb("t1", [128, D], F32), Buf("t1")
        xnT2 = [sb("xnT%d" % i, [128, 8, 512], BF16) for i in range(2)]
        b_xnT2 = [[Buf("xnT%d_%d" % (i, t)) for t in range(4)] for i in range(2)]
        actT = sb("actT", [128, NJ, 512], BF16)
        b_actT = [Buf("actT%d" % j) for j in range(NJ)]
        sg = Rot([(sb("sg%d" % i, [128, 512], F32), Buf("sg%d" % i)) for i in range(2)])

        NB = SEQ // 512
        xcur = {}

        def pre_load(blk, t):
            tt = blk * 4 + t
            xi, bxi = xin.next()
            S.dma("sp", [(xi[:, :], src[tt * 128:(tt + 1) * 128, :])], w=[bxi])
            xcur[(blk, t)] = (xi, bxi)

        def pre_A(blk, t):
            xi, bxi = xcur.pop((blk, t))
            norm_A(C, pool, xi[:, :], bxi)

        def pre_B(blk, t):
            tt = blk * 4 + t
            norm_B(C, pool, gT, b_gT, xnT2[blk % 2], t * 128, b_xnT2[blk % 2][t], PP[2 + tt % 2], PB[2 + tt % 2])

        for t in range(4):
            pre_load(0, t) if t < 2 else None
        for t in range(4):
            if t + 2 < 4:
                pass
            if t >= 2:
                pre_load(0, t)
            pre_A(0, t)
            pre_B(0, t)
        for blk in range(NB):
            xnT = xnT2[blk % 2]
            b_xnT = b_xnT2[blk % 2]
            nxt = blk + 1 if blk + 1 < NB else None
            for j in range(NJ):
                if nxt is not None:
                    for t in range(4):
                        if j == 5 * t:
                            pre_load(nxt, t)
                        if j == 5 * t + 2:
                            pre_A(nxt, t)
                        if j == 5 * t + 4:
                            pre_B(nxt, t)
                pa, pba = PP[j % 2], PB[j % 2]
                for half, cbase in ((0, j * 128), (1, DFF + j * 128)):
                    for k in range(8):
                        S.op("pe", lambda e, k=k, half=half, cbase=cbase, pa=pa, xnT=xnT: e.matmul(
                            pa[:, half * 512:(half + 1) * 512], lhsT=wgu[:, k, cbase:cbase + 128],
                            rhs=xnT[:, k, :], start=(k == 0), stop=(k == 7)),
                            r=wgu_bufs(cbase) + b_xnT, w=[pba[half]])
                sgt, b_sg = sg.next()
                S.op("act", lambda e, pa=pa, sgt=sgt: e.activation(out=sgt[:, :], in_=pa[:, 0:512], func=AF.Silu),
                     r=[pba[0]], w=[b_sg])
                S.op("dve", lambda e, pa=pa, sgt=sgt, j=j: e.tensor_tensor(
                    out=actT[:, j, :], in0=sgt[:, :], in1=pa[:, 512:1024], op=ALU.mult),
                    r=[b_sg, pba[1]], w=[b_actT[j]])
            for t in range(4):
                tt = blk * 4 + t
                pf, pbf = PP[2 + t % 2], PB[2 + t % 2]
                for n in range(2):
                    for j in range(NJ):
                        S.op("pe", lambda e, n=n, j=j, t=t, pf=pf: e.matmul(
                            pf[:, n * 512:(n + 1) * 512], lhsT=actT[:, j, t * 128:(t + 1) * 128],
                            rhs=wd[:, j, n * 512:(n + 1) * 512], start=(j == 0), stop=(j == NJ - 1)),
                            r=[b_actT[j], b_wd[j // 11]], w=[pbf[n]])
                xr, bxr = xres.next()
                S.dma("sp", [(xr[:, :], src[tt * 128:(tt + 1) * 128, :])], w=[bxr])
                post_norm_residual(C, pool, pf, pbf, gb, b_gb, xr, bxr, t1, b_t1, 0.5)
                S.dma("pool", [(dst[tt * 128:(tt + 1) * 128, :], xr[:, :])], r=[bxr])
        S.deferred_flush()


def post_norm_residual(C, pool, pf, pbf, gb, b_gb, xr, bxr, t1, b_t1, coef):
    S = C.S
    ss, b_ss = pool["ss"].next()
    junk, b_junk = pool["junk"]
    S.op("act", lambda e: e.activation(out=junk[:, :], in_=pf[:, :], func=AF.Square, accum_out=ss[:, 0:1]),
         r=pbf, w=[b_junk, b_ss])
    S.op("act", lambda e: e.activation(out=ss[:, 1:2], in_=ss[:, 0:1], func=AF.Ln, scale=1.0 / D, bias=EPS),
         r=[b_ss], w=[b_ss])
    S.op("act", lambda e: e.activation(out=ss[:, 2:3], in_=ss[:, 1:2], func=AF.Exp, scale=-0.5),
         r=[b_ss], w=[b_ss])
    S.op("dve", lambda e: e.scalar_tensor_tensor(out=t1[:, :], in0=pf[:, :], scalar=ss[:, 2:3], in1=gb[:, :],
                                                 op0=ALU.mult, op1=ALU.mult),
         r=pbf + [b_ss, b_gb], w=[b_t1])
    S.op("dve", lambda e: e.scalar_tensor_tensor(out=xr[:, :], in0=t1[:, :], scalar=float(coef), in1=xr[:, :],
                                                 op0=ALU.mult, op1=ALU.add),
         r=[b_t1, bxr], w=[bxr])


PARAM_NAMES = ["ffn1_pre_g", "ffn1_w_gu", "ffn1_w_down", "ffn1_post_g", "mix_pre_g", "w_in", "conv_w", "conv_b",
               "dt_bias", "a_log", "d_skip", "ssd_norm_g", "attn_norm_g", "w_out", "mix_post_g", "xa_pre_g",
               "mem_g", "xa_wq", "xa_wkv", "xa_wo", "xa_post_g", "ffn2_pre_g", "ffn2_w_gu", "ffn2_w_down",
               "ffn2_post_g"]


def core_inputs(inputs, b):
    m = {"x": np.ascontiguousarray(inputs["x"][b], dtype=np.float32),
         "mem": np.ascontiguousarray(inputs["mem"][b], dtype=np.float32)}
    for nm in PARAM_NAMES:
        a = np.asarray(inputs[nm], dtype=np.float32)[0]
        if a.ndim == 1:
            a = a[None, :]
        m[nm] = np.ascontiguousarray(a)
    return m


def kernel(**inputs):
    nc = build_nc(debug=False)
    in_maps = [core_inputs(inputs, b) for b in range(8)]
    res = run_bass_kernel_spmd(nc, in_maps, core_ids=list(range(8)))
    return np.stack([np.asarray(r["out"], dtype=np.float32) for r in res.results], axis=0)


def inproj_phase(C):
    nc, S, I, W = C.nc, C.S, C.I, C.W
    PP, PB = C.PP, C.PB
    with ExitStack() as es:
        def sb(name, shape, dt):
            return es.enter_context(nc.sbuf_tensor(name + "_p%d" % S.phase, shape, dt))
        win = sb("win", [128, 8, D_IN], BF16)
        groups = [(4096, 5120), (0, 1024), (1024, 2048), (5120, 5648), (2048, 3072), (3072, 4096)]
        b_win = {g: Buf("win%d" % i) for i, g in enumerate(groups)}
        gT = sb("gT", [128, 8], F32); b_gT = Buf("gT")
        load_gT(C, gT, I["mix_pre_g"], b_gT)
        load_w_bf16(C, win, I["w_in"], 0, 8, groups, [b_win[g] for g in groups])

        def win_buf(col):
            for g in groups:
                if g[0] <= col < g[1]:
                    return b_win[g]

        xin = Rot([(sb("xin%d" % i, [128, D], F32), Buf("xin%d" % i)) for i in range(4)])
        pool = {"ss": Rot([(sb("ss%d" % i, [128, 4], F32), Buf("ss%d" % i)) for i in range(4)]),
                "junk": (sb("junk", [128, D], F32), Buf("junk")),
                "xn": (sb("xn", [128, D], F32), Buf("xn"))}
        uT2 = [sb("uT%d" % i, [128, 8, 512], BF16) for i in range(2)]
        b_uT2 = [[Buf("uT%d_%d" % (i, t)) for t in range(4)] for i in range(2)]
        bkc = Buf("convconst")
        cw4 = sb("cw4", [128, 12, 4], F32)
        cbt = sb("cbt", [128, 12], F32)
        for k in range(4):
            S.dma("sp", [(cw4[:, :, k], I["conv_w"][k, :].rearrange("(ch p) -> p ch", p=128))], w=[bkc],
                  allow_slow_non_contiguous=True, key="convc")
        S.dma("sp", [(cbt[:, :], I["conv_b"][0, :].rearrange("(ch p) -> p ch", p=128))], w=[bkc],
              allow_slow_non_contiguous=True, key="convc")
        hal = [(sb("hal%d" % i, [128, 4], F32), Buf("hal%d" % i)) for i in range(12)]
        xp_r = Rot([(sb("xp%d" % i, [128, 516], F32), Buf("xp%d" % i)) for i in range(3)])
        acc_r = Rot([(sb("acc%d" % i, [128, 512], F32), Buf("acc%d" % i)) for i in range(3)])
        stb = Rot([(sb("stb%d" % i, [128, 512], BF16), Buf("stb%d" % i)) for i in range(3)])
        stf = Rot([(sb("stf%d" % i, [128, 512], F32), Buf("stf%d" % i)) for i in range(3)])
        vst = Rot([(sb("vst%d" % i, [128, D], BF16), Buf("vst%d" % i)) for i in range(2)])
        zst = Rot([(sb("zst%d" % i, [128, D], F32), Buf("zst%d" % i)) for i in range(2)])
        dst_ = Rot([(sb("dts%d" % i, [128, 16], F32), Buf("dts%d" % i)) for i in range(2)])
        banks = Rot([(PP[i][:, h * 512:(h + 1) * 512], PB[i][h]) for i in range(2) for h in range(2)])
        ev = [0]
        pend = []

        def evac(out_ap, in_ap, r, w, scale=None):
            ev[0] += 1
            if scale is not None or ev[0] % 2 == 0:
                S.op("act", lambda e: e.activation(out=out_ap, in_=in_ap, func=AF.Copy,
                                                   scale=(1.0 if scale is None else scale)), r=r, w=w)
            else:
                S.op("dve", lambda e: e.tensor_copy(out=out_ap, in_=in_ap), r=r, w=w)

        NB = SEQ // 512
        xcur = {}

        def pre_load(blk, t):
            tt = blk * 4 + t
            xi, bxi = xin.next()
            S.dma("sp", [(xi[:, :], W["h1"][tt * 128:(tt + 1) * 128, :])], w=[bxi])
            xcur[(blk, t)] = (xi, bxi)

        def pre_A(blk, t):
            xi, bxi = xcur.pop((blk, t))
            norm_A(C, pool, xi[:, :], bxi)

        def pre_B(blk, t):
            tt = blk * 4 + t
            norm_B(C, pool, gT, b_gT, uT2[blk % 2], t * 128, b_uT2[blk % 2][t], PP[2 + tt % 2], PB[2 + tt % 2])

        for t in range(4):
            pre_load(0, t)
        for t in range(4):
            pre_A(0, t)
            pre_B(0, t)
        order = []
        for i in range(12):
            order += [16 + i, i]
        order += [12, 13, 14, 15]
        for blk in range(NB):
            tok0 = blk * 512
            uT = uT2[blk % 2]
            b_uT = b_uT2[blk % 2]
            nxt = blk + 1 if blk + 1 < NB else None
            for ci, c in enumerate(order):
                if nxt is not None:
                    for t in range(4):
                        if ci == 6 * t:
                            pre_load(nxt, t)
                        if ci == 6 * t + 2:
                            pre_A(nxt, t)
                        if ci == 6 * t + 4:
                            pre_B(nxt, t)
                col0 = c * 128 if c < 16 else 4096 + (c - 16) * 128
                pa, pba = banks.next()
                for k in range(8):
                    S.op("pe", lambda e, k=k, col0=col0, pa=pa, uT=uT: e.matmul(
                        pa, lhsT=win[:, k, col0:col0 + 128], rhs=uT[:, k, :], start=(k == 0), stop=(k == 7)),
                        r=[win_buf(col0)] + b_uT, w=[pba])
                if c < 16:
                    st, b_st = stb.next()
                    evac(st[:, :], pa, [pba], [b_st], scale=(0.125 if c < 8 else 1.0))
                    dram = W["qT"] if c < 8 else W["kT"]
                    r0 = (c % 8) * 128
                    S.dma("pool", [(dram[r0:r0 + 128, tok0:tok0 + 512], st[:, :])], r=[b_st])
                else:
                    cc = c - 16
                    xp, b_xp = xp_r.next()
                    hl, b_hl = hal[cc]
                    if blk == 0:
                        S.op("dve", lambda e, xp=xp: e.memset(xp[:, 0:3], 0.0), w=[b_xp])
                    else:
                        S.op("dve", lambda e, xp=xp, hl=hl: e.tensor_copy(out=xp[:, 0:3], in_=hl[:, 0:3]),
                             r=[b_hl], w=[b_xp])
                    S.op("act", lambda e, xp=xp, pa=pa: e.activation(out=xp[:, 3:515], in_=pa, func=AF.Copy),
                         r=[pba], w=[b_xp])
                    while pend:
                        pend.pop(0)()
                    S.op("dve", lambda e, xp=xp, hl=hl: e.tensor_copy(out=hl[:, 0:3], in_=xp[:, 512:515]),
                         r=[b_xp], w=[b_hl])
                    ac, b_ac = acc_r.next()
                    S.op("dve", lambda e, xp=xp, ac=ac, cc=cc: e.tensor_scalar(
                        out=ac[:, :], in0=xp[:, 0:512], scalar1=cw4[:, cc, 0:1], scalar2=cbt[:, cc:cc + 1],
                        op0=ALU.mult, op1=ALU.add), r=[b_xp, bkc], w=[b_ac])
                    for k in range(1, 4):
                        S.op("dve", lambda e, xp=xp, ac=ac, cc=cc, k=k: e.scalar_tensor_tensor(
                            out=ac[:, :], in0=xp[:, k:k + 512], scalar=cw4[:, cc, k:k + 1], in1=ac[:, :],
                            op0=ALU.mult, op1=ALU.add), r=[b_xp, bkc, b_ac], w=[b_ac])
                    def fin(ac=ac, b_ac=b_ac, cc=cc, tok0=tok0):
                        st, b_st = stf.next()
                        S.op("act", lambda e: e.activation(out=st[:, :], in_=ac[:, :], func=AF.Silu),
                             r=[b_ac], w=[b_st])
                        r0 = cc * 128
                        S.dma("pool", [(W["xbcT"][r0:r0 + 128, tok0:tok0 + 512], st[:, :])], r=[b_st])
                    pend.append(fin)
            while pend:
                pend.pop(0)()
            for t in range(4):
                tt = blk * 4 + t
                vs, b_vs = vst.next()
                zs, b_zs = zst.next()
                for n in range(4):
                    col0 = 2048 + n * 512
                    pa, pba = banks.next()
                    for k in range(8):
                        S.op("pe", lambda e, k=k, col0=col0, pa=pa, t=t, uT=uT: e.matmul(
                            pa, lhsT=uT[:, k, t * 128:(t + 1) * 128], rhs=win[:, k, col0:col0 + 512],
                            start=(k == 0), stop=(k == 7)),
                            r=[win_buf(col0), b_uT[t]], w=[pba])
                    if n < 2:
                        evac(vs[:, n * 512:(n + 1) * 512], pa, [pba], [b_vs])
                    else:
                        S.op("act", lambda e, zs=zs, n=n, pa=pa: e.activation(
                            out=zs[:, (n - 2) * 512:(n - 1) * 512], in_=pa, func=AF.Silu), r=[pba], w=[b_zs])
                S.dma("pool", [(W["v"][tt * 128:(tt + 1) * 128, :], vs[:, :])], r=[b_vs])
                S.dma("pool", [(W["z"][tt * 128:(tt + 1) * 128, :], zs[:, :])], r=[b_zs])
                pa, pba = banks.next()
                for k in range(8):
                    S.op("pe", lambda e, k=k, pa=pa, t=t, uT=uT: e.matmul(
                        pa[:, 0:16], lhsT=uT[:, k, t * 128:(t + 1) * 128], rhs=win[:, k, 5632:5648],
                        start=(k == 0), stop=(k == 7)),
                        r=[win_buf(5632), b_uT[t]], w=[pba])
                ds, b_ds = dst_.next()
                evac(ds[:, :], pa[:, 0:16], [pba], [b_ds])
                S.dma("pool", [(W["dtr"][tt * 128:(tt + 1) * 128, :], ds[:, :])], r=[b_ds])
        S.deferred_flush()


def attn_phase(C, side=None):
    nc, S, W = C.nc, C.S, C.W
    PP, PB = C.PP, C.PB
    with ExitStack() as es:
        def sb(name, shape, dt):
            return es.enter_context(nc.sbuf_tensor(name + "_p%d" % S.phase, shape, dt))
        hk = Rot([(sb("hk%d" % i, [64, SEQ], BF16), Buf("hk%d" % i)) for i in range(2)])
        hq = Rot([(sb("hq%d" % i, [64, SEQ], BF16), Buf("hq%d" % i)) for i in range(2)])
        hv = Rot([(sb("hv%d" % i, [128, NT, 64], BF16), Buf("hv%d" % i)) for i in range(2)])
        ebuf = Rot([(sb("eb%d" % i, [128, 1024], F32), Buf("eb%d" % i)) for i in range(2)])
        spb = Rot([(sb("sp%d" % i, [128, 1024], BF16), Buf("sp%d" % i)) for i in range(4)])
        abuf = Rot([(sb("ab%d" % i, [128, 1024], BF16), Buf("ab%d" % i)) for i in range(3)])
        Rb = [(sb("R%d" % i, [128, 512], BF16), Buf("R%d" % i)) for i in range(2)]
        ost = Rot([(sb("ost%d" % i, [64, 512], F32), Buf("ost%d" % i)) for i in range(2)])
        Aregs = Rot([(PP[i], PB[i]) for i in range(3)])
        zb, b_zb = sb("zb", [128, 512], BF16), Buf("zb")
        S.op("dve", lambda e: e.memset(zb[:, :], 0.0), w=[b_zb])
        Obanks = Rot([(PP[3][:, h * 512:(h + 1) * 512], PB[3][h]) for h in range(1 if side is not None else 2)])

        class T:
            pass

        HD = {}

        def issue_loads(h):
            k_t, b_k = hk.next()
            q_t, b_q = hq.next()
            v_t, b_v = hv.next()
            S.dma("sp", [(k_t[:, :], W["kT"][h * 64:(h + 1) * 64, :])], w=[b_k])
            S.dma("sp", [(q_t[:, :], W["qT"][h * 64:(h + 1) * 64, :])], w=[b_q])
            S.dma("sp", [(v_t[:, :, :], W["v"][:, h * 64:(h + 1) * 64].rearrange("(kb p) d -> p kb d", p=128))],
                  w=[b_v])
            HD[h] = (k_t, b_k, q_t, b_q, v_t, b_v)

        units = []
        for h in range(16):
            uidx = 0
            for G in range(8):
                rcur = 0
                nkb = 4 * G + 4
                chain = []
                for idx, kb in enumerate(range(nkb - 1, -1, -1)):
                    t = T()
                    t.h, t.G, t.kb = h, G, kb
                    t.first = (idx == 0)
                    t.last = (kb == 0)
                    i = kb - 4 * G
                    t.c0 = 128 * i if i >= 0 else 0
                    t.diag = (i >= 0)
                    t.Rin = Rb[rcur]
                    t.Rout = Rb[1 - rcur]
                    rcur = 1 - rcur
                    chain.append(t)
                j = 0
                while j < len(chain):
                    u = T()
                    if chain[j].diag:
                        u.tiles = [chain[j]]
                        j += 1
                    else:
                        u.tiles = [chain[j], chain[j + 1]]
                        j += 2
                    for o, t in enumerate(u.tiles):
                        t.off = 512 * o
                    u.h, u.G = h, G
                    u.uidx = uidx
                    uidx += 1
                    u.lo = u.tiles[0].c0
                    u.hi = 512 * len(u.tiles)
                    units.append(u)

        OB = {}

        def stage1(u):
            if u.uidx == 0 and u.h == 0:
                issue_loads(0)
            if u.uidx == 3 and u.h + 1 < 16:
                issue_loads(u.h + 1)
            u.k_t, u.b_k, u.q_t, u.b_q, u.v_t, u.b_v = HD[u.h]
            if u.tiles[0].first:
                OB[(u.h, u.G)] = Obanks.next()
            u.o_ps, u.b_o = OB[(u.h, u.G)]
            u.A, u.b_A = Aregs.next()
            A = u.A
            for t in u.tiles:
                for _ in range(N_WARM):
                    S.op("pe", lambda e, t=t: e.matmul(A[:, t.off:t.off + 512], lhsT=C.onesB[:, :], rhs=zb[:, :],
                                                       start=True, stop=True),
                         r=[C.bc, b_zb], w=[u.b_A[t.off // 512]])
            for t in u.tiles:
                S.op("pe", lambda e, t=t: e.matmul(
                    A[:, t.off + t.c0:t.off + 512], lhsT=u.k_t[:, t.kb * 128:(t.kb + 1) * 128],
                    rhs=u.q_t[:, t.G * 512 + t.c0:(t.G + 1) * 512], start=True, stop=True),
                    r=[u.b_k, u.b_q], w=[u.b_A[t.off // 512]])
            nb = len(u.tiles)
            u.e, u.b_e = ebuf.next()
            u.sp, u.b_sp = spb.next()
            e_, sp_ = u.e, u.sp
            lo, hi = u.lo, u.hi
            S.op("act", lambda e: e.activation(out=e_[:, lo:hi], in_=A[:, lo:hi], func=AF.Exp),
                 r=u.b_A[0:nb], w=[u.b_e])
            S.op("act", lambda e: e.activation(out=sp_[:, lo:hi], in_=e_[:, lo:hi], func=AF.Ln, bias=1.0),
                 r=[u.b_e], w=[u.b_sp])
            t0 = u.tiles[0]
            if t0.diag:
                c0 = t0.c0
                S.op("dve", lambda e: e.tensor_tensor(out=sp_[:, c0:c0 + 128], in0=sp_[:, c0:c0 + 128],
                                                      in1=C.mask01[:, :], op=ALU.mult),
                     r=[u.b_sp, C.bc], w=[u.b_sp])

        def stage2(u):
            A, sp_ = u.A, u.sp
            for t in u.tiles:
                c0, off = t.c0, t.off
                bA = u.b_A[off // 512]
                Rin, b_Rin = t.Rin
                Rout, b_Rout = t.Rout
                S.op("pe", lambda e, c0=c0, off=off: e.matmul(
                    A[:, off + c0:off + 512], lhsT=C.triNegB[:, :], rhs=sp_[:, off + c0:off + 512],
                    start=False, stop=True, skip_group_check=True),
                    r=[u.b_sp, C.bc], w=[bA])
                if not t.first:
                    S.op("pe", lambda e, c0=c0, off=off, Rin=Rin: e.matmul(
                        A[:, off + c0:off + 512], lhsT=C.onesNegB[:, :], rhs=Rin[:, c0:512],
                        start=False, stop=True, skip_group_check=True),
                        r=[b_Rin, C.bc], w=[bA])
                if t.diag:
                    S.op("pe", lambda e, c0=c0, off=off: e.matmul(
                        A[:, off + c0:off + c0 + 128], lhsT=C.identB[:, :], rhs=C.negMaskB[:, :],
                        start=False, stop=True, skip_group_check=True),
                        r=[C.bc], w=[bA])
                if not t.last:
                    if t.first:
                        S.op("dve", lambda e, c0=c0, Rout=Rout: e.memset(Rout[:, 0:c0], 0.0), w=[b_Rout])
                        S.op("dve", lambda e, c0=c0, off=off, Rout=Rout: e.tensor_copy(
                            out=Rout[:, c0:512], in_=sp_[:, off + c0:off + 512]), r=[u.b_sp], w=[b_Rout])
                    else:
                        if c0 > 0:
                            S.op("dve", lambda e, c0=c0, Rout=Rout: e.memset(Rout[:, 0:c0], 0.0), w=[b_Rout])
                        S.op("dve", lambda e, c0=c0, off=off, Rin=Rin, Rout=Rout: e.tensor_tensor(
                            out=Rout[:, c0:512], in0=Rin[:, c0:512], in1=sp_[:, off + c0:off + 512], op=ALU.add),
                            r=[b_Rin, u.b_sp], w=[b_Rout])

        def stage3(u):
            u.a, u.b_a = abuf.next()
            a_, A = u.a, u.A
            lo, hi = u.lo, u.hi
            S.op("act", lambda e: e.activation(out=a_[:, lo:hi], in_=A[:, lo:hi], func=AF.Exp),
                 r=u.b_A[0:len(u.tiles)], w=[u.b_a])

        def stage4(u):
            a_ = u.a
            for t in u.tiles:
                c0, off = t.c0, t.off
                S.op("pe", lambda e, t=t, c0=c0, off=off: e.matmul(
                    u.o_ps[0:64, c0:512], lhsT=u.v_t[:, t.kb, :], rhs=a_[:, off + c0:off + 512],
                    start=t.first, stop=t.last, skip_group_check=True),
                    r=[u.b_a, u.b_v], w=[u.b_o])
                if t.last:
                    o_s, b_os = ost.next()
                    S.op("dve", lambda e, o_s=o_s: e.tensor_copy(out=o_s[:, :], in_=u.o_ps[0:64, :]),
                         r=[u.b_o], w=[b_os])
                    S.dma("pool", [(W["oattT"][t.h * 64:(t.h + 1) * 64, t.G * 512:(t.G + 1) * 512], o_s[:, :])],
                          r=[b_os])

        n = len(units)
        per_unit = 0
        if side is not None:
            S.defer = True
            for r_ in side:
                if r_ == "done":
                    break
            S.defer = False
            per_unit = -(-len(S.deferq) // max(1, n - 40))
        for i in range(n + 2):
            if i < n:
                stage1(units[i])
            if 0 <= i - 1 < n:
                stage2(units[i - 1])
                stage3(units[i - 1])
            if 0 <= i - 2 < n:
                stage4(units[i - 2])
            if side is not None:
                S.replay(per_unit)
        if side is not None:
            S.replay(10 ** 9)
        S.deferred_flush()
        if side is not None:
            for _ in side:
                pass


def ssd_phase(C):
    nc, S, I, W = C.nc, C.S, C.I, C.W
    PP, PB = C.PP, C.PB
    with ExitStack() as es:
        def sb(name, shape, dt):
            return es.enter_context(nc.sbuf_tensor(name + "_p%d" % S.phase, shape, dt))
        bk = Buf("ssdconst")
        dtb = sb("dtb", [128, 16], F32)
        abc = sb("abc", [128, 16], F32)
        dsk = sb("dsk", [128, 16], F32)
        gssd = sb("gssd", [128, D], F32)
        nm4 = sb("nm4", [128, 4, 128], BF16)
        S.dma("sp", [(dtb[:, :], I["dt_bias"][0:1, :].partition_broadcast(128))], w=[bk], key="ssdc")
        S.dma("sp", [(abc[:, :], I["a_log"][0:1, :].partition_broadcast(128))], w=[bk], key="ssdc")
        S.dma("sp", [(dsk[:, :], I["d_skip"][0:1, :].partition_broadcast(128))], w=[bk], key="ssdc")
        S.dma("sp", [(gssd[:, :], I["ssd_norm_g"][0:1, :].partition_broadcast(128))], w=[bk], key="ssdc")
        S.op("act", lambda e: e.activation(out=abc[:, :], in_=abc[:, :], func=AF.Exp), r=[bk], w=[bk])
        S.op("dve", lambda e: e.tensor_scalar(out=abc[:, :], in0=abc[:, :], scalar1=-1.0, scalar2=None,
                                              op0=ALU.mult), r=[bk], w=[bk])
        for i in range(4):
            S.op("dve", lambda e, i=i: e.tensor_copy(out=nm4[:, i, :], in_=C.negMaskSB[:, :]), r=[C.bc], w=[bk])

        def rot2(name, shape, dt, n=2):
            return Rot([(sb("%s%d" % (name, i), shape, dt), Buf("%s%d" % (name, i))) for i in range(n)])

        cv_r = rot2("cv", [128, 12, 128], F32, 3)
        dtr_t = rot2("dtr", [128, 16], F32)
        z_t = rot2("zt", [128, D], F32, 3)
        cvb_r = rot2("cvb", [128, 4, 128], BF16)
        xs_r = rot2("xs_tok", [128, D], F32)
        Bt_r = rot2("B_tok", [128, 256], BF16)
        sm_r = rot2("sm", [128, 8, 16], F32)
        edt_r = rot2("edt", [128, 48], F32)
        xbf_r = rot2("x_bf", [128, D], BF16)
        xdec_r = rot2("xdec", [128, D], BF16)
        y1_r = rot2("y1", [128, D], F32)
        rhs1_r = rot2("rhs1", [128, 2, 16, 128], BF16)
        smb, b_smb = sb("smb", [128, 2, 16], BF16), Buf("smb")
        cbs_r = rot2("cbs", [128, 256], F32)
        Mt = rot2("Mt", [128, 4, 128], BF16, 4)

        def rhs1_of(st):
            return st.rhs1

        def pcb_of(st):
            return st.cbs
        lm = rot2("lm", [128, 512], F32)
        yoffs, b_yoffs = sb("yoffs", [128, D], F32), Buf("yoffs")
        hstate, b_hst = sb("hstate", [128, D], F32), Buf("hstate")
        hsb, b_hsb = sb("hsb", [128, D], BF16), Buf("hsb")
        junk, b_junk = sb("junk", [128, D], F32), Buf("junk")
        ss_r = rot2("ss", [128, 8], F32)
        ot = rot2("ot", [128, 8, 128], BF16)
        Dbanks = Rot([(PP[2][:, h * 512:(h + 1) * 512], PB[2][h]) for h in range(2)])
        b_small = b_cb = PB[3][0]
        psm = PP[3][:, 0:48]
        pcb = PP[3][:, 128:384]
        pBt = PP[3][:, 512:768]
        xbc_v = W["xbcT"].rearrange("(ch p) t -> p ch t", p=128)
        ossd_v = W["ossdT"].rearrange("(k p) t -> p k t", p=128)

        def bc16(ap16, n0, n):
            return ap16[:, n0:n0 + n].unsqueeze(2).to_broadcast([128, n, 64])

        def h16(ap):
            return ap.rearrange("p (h d) -> p h d", d=64)

        ball = Buf("ssd_all")
        dtr_all = sb("dtr_all", [128, NT, 16], F32)
        dt_all = sb("dt_all", [128, NT, 16], F32)
        adt_all = sb("adt_all", [128, NT, 16], F32)
        adt_hi = sb("adt_hi", [128, NT, 16], BF16)
        adt_lo = sb("adt_lo", [128, NT, 16], F32)
        edt_all = sb("edt_all", [128, 3, NT, 16], F32)
        ddec_all = sb("ddec_all", [128, NT, 16], F32)
        S.dma("sp", [(dtr_all[:, :, :], W["dtr"].rearrange("(c p) h -> p c h", p=128))], w=[ball])

        def bcc(ap16):
            return ap16[:, :].unsqueeze(1).to_broadcast([128, NT, 16])

        S.op("dve", lambda e: e.tensor_tensor(out=dtr_all[:, :, :], in0=dtr_all[:, :, :], in1=bcc(dtb), op=ALU.add),
             r=[ball, bk], w=[ball])
        S.op("act", lambda e: e.activation(out=dtr_all[:, :, :], in_=dtr_all[:, :, :], func=AF.Exp),
             r=[ball], w=[ball])
        S.op("act", lambda e: e.activation(out=dt_all[:, :, :], in_=dtr_all[:, :, :], func=AF.Ln, bias=1.0),
             r=[ball], w=[ball])
        S.op("dve", lambda e: e.tensor_tensor(out=adt_all[:, :, :], in0=dt_all[:, :, :], in1=bcc(abc), op=ALU.mult),
             r=[ball, bk], w=[ball])
        S.op("dve", lambda e: e.tensor_copy(out=adt_hi[:, :, :], in_=adt_all[:, :, :]), r=[ball], w=[ball])
        S.op("dve", lambda e: e.tensor_tensor(out=adt_lo[:, :, :], in0=adt_all[:, :, :], in1=adt_hi[:, :, :],
                                              op=ALU.subtract), r=[ball], w=[ball])
        for i, lt in enumerate((C.triInclF, C.triUpF, C.onesF)):
            pq = PP[i // 2][:, (i % 2) * 512:(i % 2 + 1) * 512]
            S.op("pe", lambda e, lt=lt, pq=pq: e.matmul(pq, lhsT=lt[:, :],
                                                        rhs=adt_all[:, :, :].rearrange("p c h -> p (c h)"),
                                                        start=True, stop=True),
                 r=[ball, C.bc], w=[PB[i // 2][i % 2]])
            S.op("act", lambda e, i=i, pq=pq: e.activation(out=edt_all[:, i, :, :].rearrange("p c h -> p (c h)"),
                                                           in_=pq, func=AF.Exp),
                 r=[PB[i // 2][i % 2]], w=[ball])
        S.op("dve", lambda e: e.tensor_tensor(out=ddec_all[:, :, :], in0=dt_all[:, :, :], in1=edt_all[:, 1, :, :],
                                              op=ALU.mult), r=[ball], w=[ball])

        CH = {}

        class St:
            pass

        def front(c):
            st = St()
            CH[c] = st
            tok0 = c * 128
            st.rhs1, st.b_rhs1 = rhs1_r.next()
            rhs1, b_rhs1 = st.rhs1, st.b_rhs1
            for hl, src_ap, eng in ((0, adt_hi[:, c, :], "pool"), (1, adt_lo[:, c, :], "dve")):
                S.op(eng, lambda e, hl=hl, src_ap=src_ap: e.tensor_tensor(
                    out=rhs1[:, hl, :, :], in0=C.triInclF[:, :].unsqueeze(1).to_broadcast([128, 16, 128]),
                    in1=src_ap.unsqueeze(2).to_broadcast([128, 16, 128]), op=ALU.mult),
                    r=[ball, C.bc], w=[b_rhs1])
            cv, b_cv = cv_r.next()
            S.dma("sp", [(cv[:, :, :], xbc_v[:, :, tok0:tok0 + 128])], w=[b_cv])
            st.zt, st.b_zt = z_t.next()
            zt, b_zt = st.zt, st.b_zt
            S.dma("sp", [(zt[:, :], W["z"][tok0:tok0 + 128, :])], w=[b_zt])
            st.cvb, st.b_cvb = cvb_r.next()
            cvb, b_cvb = st.cvb, st.b_cvb
            S.op("dve", lambda e: e.tensor_copy(out=cvb[:, :, :], in_=cv[:, 8:12, :]), r=[b_cv], w=[b_cvb])
            for k in range(8):
                S.op("pe", lambda e, k=k: e.transpose(out=PP[0][:, k * 128:(k + 1) * 128], in_=cv[:, k, :],
                                                      identity=C.identF[:, :]),
                     r=[b_cv, C.bc], w=[PB[0][k // 4]])
            for g in range(2):
                S.op("pe", lambda e, g=g: e.transpose(out=pBt[:, g * 128:(g + 1) * 128], in_=cv[:, 8 + g, :],
                                                      identity=C.identF[:, :]),
                     r=[b_cv, C.bc], w=[PB[3][1]])
            st.xs, st.b_xs = xs_r.next()
            xs_tok, b_xs = st.xs, st.b_xs
            S.op("act", lambda e: e.activation(out=xs_tok[:, :], in_=PP[0][:, :], func=AF.Copy),
                 r=PB[0], w=[b_xs])
            st.Bt, st.b_Bt = Bt_r.next()
            B_tok, b_Bt = st.Bt, st.b_Bt
            S.op("dve", lambda e: e.tensor_copy(out=B_tok[:, :], in_=pBt), r=[PB[3][1]], w=[b_Bt])
            x_bf, b_xbf = xbf_r.next()
            S.op("pool", lambda e: e.tensor_tensor(out=h16(x_bf[:, :]), in0=h16(xs_tok[:, :]),
                                                   in1=bc16(dt_all[:, c, :], 0, 16), op=ALU.mult),
                 r=[b_xs, ball], w=[b_xbf])
            st.xdec, st.b_xdec = xdec_r.next()
            xdec, b_xdec = st.xdec, st.b_xdec
            S.op("dve", lambda e: e.tensor_tensor(out=h16(xdec[:, :]), in0=h16(xs_tok[:, :]),
                                                   in1=bc16(ddec_all[:, c, :], 0, 16), op=ALU.mult),
                 r=[b_xs, ball], w=[b_xdec])
            for g in range(2):
                S.op("pe", lambda e, g=g: e.matmul(pcb[:, g * 128:(g + 1) * 128], lhsT=cvb[:, g, :],
                                                   rhs=cvb[:, 2 + g, :], start=True, stop=True),
                     r=[b_cvb], w=[b_cb])
            st.cbs, st.b_cbs = cbs_r.next()
            cbs, b_cbs = st.cbs, st.b_cbs
            S.op("act", lambda e: e.activation(out=cbs[:, :], in_=pcb, func=AF.Copy), r=[b_cb], w=[b_cbs])
            st.M = {}
            st.x_bf, st.b_xbf = x_bf, b_xbf

        def QD(c, qd):
            st = CH[c]
            g = qd // 2
            dps, b_d = Dbanks.next()
            for hl in range(2):
                S.op("pe", lambda e, hl=hl: e.matmul(
                    dps, lhsT=C.triUpB[:, :], rhs=rhs1_of(st)[:, hl, qd * 4:(qd + 1) * 4, :],
                    start=(hl == 0), stop=False),
                    r=[st.b_rhs1, C.bc], w=[b_d])
            S.op("pe", lambda e: e.matmul(dps, lhsT=C.identB[:, :], rhs=nm4[:, :, :], start=False, stop=True),
                 r=[bk, C.bc], w=[b_d])
            lm_t, b_lm = lm.next()
            S.op("act", lambda e: e.activation(out=lm_t[:, :], in_=dps, func=AF.Exp), r=[b_d], w=[b_lm])
            M_t, b_M = Mt.next()
            S.op("dve", lambda e: e.tensor_tensor(
                out=M_t[:, :, :], in0=lm_t[:, :].rearrange("p (h l) -> p h l", h=4),
                in1=pcb_of(st)[:, g * 128:(g + 1) * 128].unsqueeze(1).to_broadcast([128, 4, 128]), op=ALU.mult),
                r=[b_lm, st.b_cbs], w=[b_M])
            st.M[qd] = (M_t, b_M)

        def QY(c, qd):
            st = CH[c]
            M_t, b_M = st.M[qd]
            for hh in range(4):
                h = qd * 4 + hh
                S.op("pe", lambda e, hh=hh, h=h: e.matmul(
                    PP[1][:, h * 64:(h + 1) * 64], lhsT=M_t[:, hh, :], rhs=st.x_bf[:, h * 64:(h + 1) * 64],
                    start=True, stop=True),
                    r=[b_M, st.b_xbf], w=[PB[1][h // 8]])

        def F3(c):
            st = CH[c]
            xs_tok, b_xs = st.xs, st.b_xs
            st.y1, st.b_y1 = y1_r.next()
            y1, b_y1 = st.y1, st.b_y1
            S.op("pool", lambda e: e.tensor_tensor(out=h16(y1[:, :]), in0=h16(xs_tok[:, :]),
                                                   in1=bc16(dsk, 0, 16), op=ALU.mult),
                 r=[b_xs, bk], w=[b_y1])
            S.op("dve", lambda e: e.tensor_tensor(out=y1[:, :], in0=y1[:, :], in1=PP[1][:, :], op=ALU.add),
                 r=[b_y1] + PB[1], w=[b_y1])

        def mid(c):
            st = CH[c]
            cvb, b_cvb = st.cvb, st.b_cvb
            if c > 0:
                for g in range(2):
                    yps, b_yp = Dbanks.next()
                    S.op("pe", lambda e, g=g, yps=yps: e.matmul(yps, lhsT=cvb[:, 2 + g, :],
                                                                rhs=hsb[:, g * 512:(g + 1) * 512],
                                                                start=True, stop=True),
                         r=[b_cvb, b_hsb], w=[b_yp])
                    S.op("dve", lambda e, g=g, yps=yps: e.tensor_tensor(
                        out=h16(yoffs[:, g * 512:(g + 1) * 512]), in0=h16(yps), in1=bc16(edt_all[:, 0, c, :], 8 * g, 8),
                        op=ALU.mult),
                        r=[b_yp, ball], w=[b_yoffs])
            if c < NT - 1:
                for g in range(2):
                    sps, b_sp = Dbanks.next()
                    S.op("pe", lambda e, g=g, sps=sps: e.matmul(sps, lhsT=st.Bt[:, g * 128:(g + 1) * 128],
                                                                rhs=st.xdec[:, g * 512:(g + 1) * 512],
                                                                start=True, stop=True),
                         r=[st.b_Bt, st.b_xdec], w=[b_sp])
                    hs_g = hstate[:, g * 512:(g + 1) * 512]
                    if c == 0:
                        S.op("dve", lambda e, hs_g=hs_g, sps=sps: e.tensor_copy(out=hs_g, in_=sps),
                             r=[b_sp], w=[b_hst])
                    else:
                        S.op("dve", lambda e, hs_g=hs_g, g=g: e.tensor_tensor(
                            out=h16(hs_g), in0=h16(hs_g), in1=bc16(edt_all[:, 2, c, :], 8 * g, 8), op=ALU.mult),
                            r=[b_hst, ball], w=[b_hst])
                        S.op("dve", lambda e, hs_g=hs_g, sps=sps: e.tensor_tensor(out=hs_g, in0=hs_g, in1=sps,
                                                                                   op=ALU.add),
                             r=[b_hst, b_sp], w=[b_hst])
                    S.op("act", lambda e, hs_g=hs_g, g=g: e.activation(out=hsb[:, g * 512:(g + 1) * 512], in_=hs_g,
                                                                      func=AF.Copy),
                         r=[b_hst], w=[b_hsb])

        def backA(c):
            st = CH[c]
            tok0 = c * 128
            y1, b_y1, zt, b_zt = st.y1, st.b_y1, st.zt, st.b_zt
            ss, b_ss = ss_r.next()
            if c > 0:
                S.op("dve", lambda e: e.tensor_tensor(out=y1[:, :], in0=y1[:, :], in1=yoffs[:, :], op=ALU.add),
                     r=[b_y1, b_yoffs], w=[b_y1])
            S.op("pool", lambda e: e.tensor_tensor(out=y1[:, :], in0=y1[:, :], in1=zt[:, :], op=ALU.mult),
                 r=[b_y1, b_zt], w=[b_y1])
            for g in range(2):
                S.op("act", lambda e, g=g: e.activation(out=junk[:, g * 512:(g + 1) * 512],
                                                        in_=y1[:, g * 512:(g + 1) * 512], func=AF.Square,
                                                        accum_out=ss[:, g:g + 1]),
                     r=[b_y1], w=[b_junk, b_ss])
            S.op("act", lambda e: e.activation(out=ss[:, 2:4], in_=ss[:, 0:2], func=AF.Ln, scale=1.0 / 512, bias=EPS),
                 r=[b_ss], w=[b_ss])
            S.op("act", lambda e: e.activation(out=ss[:, 4:6], in_=ss[:, 2:4], func=AF.Exp, scale=-0.5),
                 r=[b_ss], w=[b_ss])
            st.ss, st.b_ss = ss, b_ss

        def backB(c):
            st = CH[c]
            tok0 = c * 128
            y1, b_y1, ss, b_ss = st.y1, st.b_y1, st.ss, st.b_ss
            for g in range(2):
                S.op("dve", lambda e, g=g: e.scalar_tensor_tensor(
                    out=y1[:, g * 512:(g + 1) * 512], in0=y1[:, g * 512:(g + 1) * 512], scalar=ss[:, 4 + g:5 + g],
                    in1=gssd[:, g * 512:(g + 1) * 512], op0=ALU.mult, op1=ALU.mult),
                    r=[b_y1, b_ss, bk], w=[b_y1])
            for k in range(8):
                S.op("pe", lambda e, k=k: e.transpose(out=PP[0][:, k * 128:(k + 1) * 128],
                                                      in_=y1[:, k * 128:(k + 1) * 128], identity=C.identF[:, :]),
                     r=[b_y1, C.bc], w=[PB[0][k // 4]])
            o_t, b_ot = ot.next()
            S.op("act", lambda e: e.activation(out=o_t[:, :, :],
                                               in_=PP[0][:, :].rearrange("p (k t) -> p k t", k=8), func=AF.Copy),
                 r=PB[0], w=[b_ot])
            S.dma("act", [(ossd_v[:, :, tok0:tok0 + 128], o_t[:, :, :])], r=[b_ot])
            del CH[c]

        def whole_front(c):
            front(c)
            for qd in range(4):
                QD(c, qd)
                QY(c, qd)
            F3(c)

        whole_front(0)
        for c in range(NT):
            n = c + 1
            if n < NT:
                front(n)
                QD(n, 0)
            mid(c)
            if n < NT:
                QD(n, 1)
                QY(n, 0)
            backA(c)
            if n < NT:
                QD(n, 2)
                QY(n, 1)
            backB(c)
            if n < NT:
                QD(n, 3)
                QY(n, 2)
                QY(n, 3)
                F3(n)
        S.deferred_flush()


def ssd_steps(C):
    nc, S, I, W = C.nc, C.S, C.I, C.W
    PP, PB = C.PP, C.PB
    with ExitStack() as es:
        def sb(name, shape, dt):
            return es.enter_context(nc.sbuf_tensor(name + "_p%d" % S.phase, shape, dt))
        bk = Buf("ssdconst")
        dtb = sb("dtb", [128, 16], F32)
        abc = sb("abc", [128, 16], F32)
        dsk = sb("dsk", [128, 16], F32)
        gssd = sb("gssd", [128, D], F32)
        nm4 = sb("nm4", [128, 4, 128], BF16)
        S.dma("sp", [(dtb[:, :], I["dt_bias"][0:1, :].partition_broadcast(128))], w=[bk], key="ssdc")
        S.dma("sp", [(abc[:, :], I["a_log"][0:1, :].partition_broadcast(128))], w=[bk], key="ssdc")
        S.dma("sp", [(dsk[:, :], I["d_skip"][0:1, :].partition_broadcast(128))], w=[bk], key="ssdc")
        S.dma("sp", [(gssd[:, :], I["ssd_norm_g"][0:1, :].partition_broadcast(128))], w=[bk], key="ssdc")
        S.op("act", lambda e: e.activation(out=abc[:, :], in_=abc[:, :], func=AF.Exp), r=[bk], w=[bk])
        S.op("dve", lambda e: e.tensor_scalar(out=abc[:, :], in0=abc[:, :], scalar1=-1.0, scalar2=None,
                                              op0=ALU.mult), r=[bk], w=[bk])
        for i in range(4):
            S.op("dve", lambda e, i=i: e.tensor_copy(out=nm4[:, i, :], in_=C.negMaskSB[:, :]), r=[C.bc], w=[bk])

        def rot2(name, shape, dt, n=2):
            return Rot([(sb("%s%d" % (name, i), shape, dt), Buf("%s%d" % (name, i))) for i in range(n)])

        cv_r = rot2("cv", [128, 12, 128], F32, 3)
        dtr_t = rot2("dtr", [128, 16], F32)
        z_t = rot2("zt", [128, D], F32, 3)
        cvb_r = rot2("cvb", [128, 4, 128], BF16)
        xs_r = rot2("xs_tok", [128, D], F32)
        Bt_r = rot2("B_tok", [128, 256], BF16)
        sm_r = rot2("sm", [128, 8, 16], F32)
        edt_r = rot2("edt", [128, 48], F32)
        xbf_r = rot2("x_bf", [128, D], BF16)
        xdec_r = rot2("xdec", [128, D], BF16)
        y1_r = rot2("y1", [128, D], F32)
        rhs1_r = rot2("rhs1", [128, 2, 16, 128], BF16)
        smb, b_smb = sb("smb", [128, 2, 16], BF16), Buf("smb")
        cbs_r = rot2("cbs", [128, 256], F32)
        Mt = rot2("Mt", [128, 4, 128], BF16, 4)

        def rhs1_of(st):
            return st.rhs1

        def pcb_of(st):
            return st.cbs
        lm = rot2("lm", [128, 512], F32)
        yoffs, b_yoffs = sb("yoffs", [128, D], F32), Buf("yoffs")
        hstate, b_hst = sb("hstate", [128, D], F32), Buf("hstate")
        hsb, b_hsb = sb("hsb", [128, D], BF16), Buf("hsb")
        junk, b_junk = sb("junk", [128, D], F32), Buf("junk")
        ss_r = rot2("ss", [128, 8], F32)
        ot = rot2("ot", [128, 8, 128], BF16)
        SBK, bb = PP[3][:, 512:1024], PB[3][1]
        Dbanks = Rot([(SBK, bb)])
        b_cb = bb
        pcb = SBK[:, 0:256]
        pBt = SBK[:, 256:512]
        xbc_v = W["xbcT"].rearrange("(ch p) t -> p ch t", p=128)
        ossd_v = W["ossdT"].rearrange("(k p) t -> p k t", p=128)

        def bc16(ap16, n0, n):
            return ap16[:, n0:n0 + n].unsqueeze(2).to_broadcast([128, n, 64])

        def h16(ap):
            return ap.rearrange("p (h d) -> p h d", d=64)

        ball = Buf("ssd_all")
        dtr_all = sb("dtr_all", [128, NT, 16], F32)
        dt_all = sb("dt_all", [128, NT, 16], F32)
        adt_all = sb("adt_all", [128, NT, 16], F32)
        adt_hi = sb("adt_hi", [128, NT, 16], BF16)
        adt_lo = sb("adt_lo", [128, NT, 16], F32)
        edt_all = sb("edt_all", [128, 3, NT, 16], F32)
        ddec_all = sb("ddec_all", [128, NT, 16], F32)
        S.dma("sp", [(dtr_all[:, :, :], W["dtr"].rearrange("(c p) h -> p c h", p=128))], w=[ball])

        def bcc(ap16):
            return ap16[:, :].unsqueeze(1).to_broadcast([128, NT, 16])

        S.op("dve", lambda e: e.tensor_tensor(out=dtr_all[:, :, :], in0=dtr_all[:, :, :], in1=bcc(dtb), op=ALU.add),
             r=[ball, bk], w=[ball])
        S.op("act", lambda e: e.activation(out=dtr_all[:, :, :], in_=dtr_all[:, :, :], func=AF.Exp),
             r=[ball], w=[ball])
        S.op("act", lambda e: e.activation(out=dt_all[:, :, :], in_=dtr_all[:, :, :], func=AF.Ln, bias=1.0),
             r=[ball], w=[ball])
        S.op("dve", lambda e: e.tensor_tensor(out=adt_all[:, :, :], in0=dt_all[:, :, :], in1=bcc(abc), op=ALU.mult),
             r=[ball, bk], w=[ball])
        S.op("dve", lambda e: e.tensor_copy(out=adt_hi[:, :, :], in_=adt_all[:, :, :]), r=[ball], w=[ball])
        S.op("dve", lambda e: e.tensor_tensor(out=adt_lo[:, :, :], in0=adt_all[:, :, :], in1=adt_hi[:, :, :],
                                              op=ALU.subtract), r=[ball], w=[ball])
        for i, lt in enumerate((C.triInclF, C.triUpF, C.onesF)):
            pq = SBK
            S.op("pe", lambda e, lt=lt, pq=pq: e.matmul(pq, lhsT=lt[:, :],
                                                        rhs=adt_all[:, :, :].rearrange("p c h -> p (c h)"),
                                                        start=True, stop=True),
                 r=[ball, C.bc], w=[bb])
            S.op("act", lambda e, i=i, pq=pq: e.activation(out=edt_all[:, i, :, :].rearrange("p c h -> p (c h)"),
                                                           in_=pq, func=AF.Exp),
                 r=[bb], w=[ball])
        S.op("dve", lambda e: e.tensor_tensor(out=ddec_all[:, :, :], in0=dt_all[:, :, :], in1=edt_all[:, 1, :, :],
                                              op=ALU.mult), r=[ball], w=[ball])

        CH = {}

        class St:
            pass

        def front(c):
            st = St()
            CH[c] = st
            tok0 = c * 128
            st.rhs1, st.b_rhs1 = rhs1_r.next()
            rhs1, b_rhs1 = st.rhs1, st.b_rhs1
            for hl, src_ap, eng in ((0, adt_hi[:, c, :], "pool"), (1, adt_lo[:, c, :], "dve")):
                S.op(eng, lambda e, hl=hl, src_ap=src_ap: e.tensor_tensor(
                    out=rhs1[:, hl, :, :], in0=C.triInclF[:, :].unsqueeze(1).to_broadcast([128, 16, 128]),
                    in1=src_ap.unsqueeze(2).to_broadcast([128, 16, 128]), op=ALU.mult),
                    r=[ball, C.bc], w=[b_rhs1])
            cv, b_cv = cv_r.next()
            S.dma("sp", [(cv[:, :, :], xbc_v[:, :, tok0:tok0 + 128])], w=[b_cv])
            st.zt, st.b_zt = z_t.next()
            zt, b_zt = st.zt, st.b_zt
            S.dma("sp", [(zt[:, :], W["z"][tok0:tok0 + 128, :])], w=[b_zt])
            st.cvb, st.b_cvb = cvb_r.next()
            cvb, b_cvb = st.cvb, st.b_cvb
            S.op("dve", lambda e: e.tensor_copy(out=cvb[:, :, :], in_=cv[:, 8:12, :]), r=[b_cv], w=[b_cvb])
            st.xs, st.b_xs = xs_r.next()
            xs_tok, b_xs = st.xs, st.b_xs
            for half in range(2):
                for kk in range(4):
                    k = half * 4 + kk
                    S.op("pe", lambda e, k=k, kk=kk: e.transpose(out=SBK[:, kk * 128:(kk + 1) * 128], in_=cv[:, k, :],
                                                              identity=C.identF[:, :]),
                         r=[b_cv, C.bc], w=[bb])
                S.op("dve", lambda e, half=half: e.tensor_copy(out=xs_tok[:, half * 512:(half + 1) * 512], in_=SBK),
                     r=[bb], w=[b_xs])
            for gg in range(2):
                S.op("pe", lambda e, gg=gg: e.transpose(out=pBt[:, gg * 128:(gg + 1) * 128], in_=cv[:, 8 + gg, :],
                                                        identity=C.identF[:, :]),
                     r=[b_cv, C.bc], w=[bb])
            st.Bt, st.b_Bt = Bt_r.next()
            B_tok, b_Bt = st.Bt, st.b_Bt
            S.op("dve", lambda e: e.tensor_copy(out=B_tok[:, :], in_=pBt), r=[bb], w=[b_Bt])
            x_bf, b_xbf = xbf_r.next()
            S.op("pool", lambda e: e.tensor_tensor(out=h16(x_bf[:, :]), in0=h16(xs_tok[:, :]),
                                                   in1=bc16(dt_all[:, c, :], 0, 16), op=ALU.mult),
                 r=[b_xs, ball], w=[b_xbf])
            st.xdec, st.b_xdec = xdec_r.next()
            xdec, b_xdec = st.xdec, st.b_xdec
            S.op("dve", lambda e: e.tensor_tensor(out=h16(xdec[:, :]), in0=h16(xs_tok[:, :]),
                                                   in1=bc16(ddec_all[:, c, :], 0, 16), op=ALU.mult),
                 r=[b_xs, ball], w=[b_xdec])
            for g in range(2):
                S.op("pe", lambda e, g=g: e.matmul(pcb[:, g * 128:(g + 1) * 128], lhsT=cvb[:, g, :],
                                                   rhs=cvb[:, 2 + g, :], start=True, stop=True),
                     r=[b_cvb], w=[b_cb])
            st.cbs, st.b_cbs = cbs_r.next()
            cbs, b_cbs = st.cbs, st.b_cbs
            S.op("dve", lambda e: e.tensor_copy(out=cbs[:, :], in_=pcb), r=[b_cb], w=[b_cbs])
            st.M = {}
            st.x_bf, st.b_xbf = x_bf, b_xbf

        def QD(c, qd):
            st = CH[c]
            g = qd // 2
            dps, b_d = Dbanks.next()
            for hl in range(2):
                S.op("pe", lambda e, hl=hl: e.matmul(
                    dps, lhsT=C.triUpB[:, :], rhs=rhs1_of(st)[:, hl, qd * 4:(qd + 1) * 4, :],
                    start=(hl == 0), stop=False),
                    r=[st.b_rhs1, C.bc], w=[b_d])
            S.op("pe", lambda e: e.matmul(dps, lhsT=C.identB[:, :], rhs=nm4[:, :, :], start=False, stop=True),
                 r=[bk, C.bc], w=[b_d])
            lm_t, b_lm = lm.next()
            S.op("act", lambda e: e.activation(out=lm_t[:, :], in_=dps, func=AF.Exp), r=[b_d], w=[b_lm])
            M_t, b_M = Mt.next()
            S.op("dve", lambda e: e.tensor_tensor(
                out=M_t[:, :, :], in0=lm_t[:, :].rearrange("p (h l) -> p h l", h=4),
                in1=pcb_of(st)[:, g * 128:(g + 1) * 128].unsqueeze(1).to_broadcast([128, 4, 128]), op=ALU.mult),
                r=[b_lm, st.b_cbs], w=[b_M])
            st.M[qd] = (M_t, b_M)

        def QY(c, qd):
            st = CH[c]
            M_t, b_M = st.M[qd]
            if qd == 0:
                st.y1, st.b_y1 = y1_r.next()
                S.op("pool", lambda e: e.tensor_tensor(out=h16(st.y1[:, :]), in0=h16(st.xs[:, :]),
                                                       in1=bc16(dsk, 0, 16), op=ALU.mult),
                     r=[st.b_xs, bk], w=[st.b_y1])
            for hh in range(4):
                h = qd * 4 + hh
                S.op("pe", lambda e, hh=hh, h=h: e.matmul(
                    SBK[:, hh * 64:(hh + 1) * 64], lhsT=M_t[:, hh, :], rhs=st.x_bf[:, h * 64:(h + 1) * 64],
                    start=True, stop=True),
                    r=[b_M, st.b_xbf], w=[bb])
            S.op("dve", lambda e: e.tensor_tensor(out=st.y1[:, qd * 256:(qd + 1) * 256],
                                                  in0=st.y1[:, qd * 256:(qd + 1) * 256], in1=SBK[:, 0:256], op=ALU.add),
                 r=[st.b_y1, bb], w=[st.b_y1])

        def F3(c):
            pass

        def mid(c):
            st = CH[c]
            cvb, b_cvb = st.cvb, st.b_cvb
            if c > 0:
                for g in range(2):
                    yps, b_yp = Dbanks.next()
                    S.op("pe", lambda e, g=g, yps=yps: e.matmul(yps, lhsT=cvb[:, 2 + g, :],
                                                                rhs=hsb[:, g * 512:(g + 1) * 512],
                                                                start=True, stop=True),
                         r=[b_cvb, b_hsb], w=[b_yp])
                    S.op("dve", lambda e, g=g, yps=yps: e.tensor_tensor(
                        out=h16(yoffs[:, g * 512:(g + 1) * 512]), in0=h16(yps), in1=bc16(edt_all[:, 0, c, :], 8 * g, 8),
                        op=ALU.mult),
                        r=[b_yp, ball], w=[b_yoffs])
            if c < NT - 1:
                for g in range(2):
                    sps, b_sp = Dbanks.next()
                    S.op("pe", lambda e, g=g, sps=sps: e.matmul(sps, lhsT=st.Bt[:, g * 128:(g + 1) * 128],
                                                                rhs=st.xdec[:, g * 512:(g + 1) * 512],
                                                                start=True, stop=True),
                         r=[st.b_Bt, st.b_xdec], w=[b_sp])
                    hs_g = hstate[:, g * 512:(g + 1) * 512]
                    if c == 0:
                        S.op("dve", lambda e, hs_g=hs_g, sps=sps: e.tensor_copy(out=hs_g, in_=sps),
                             r=[b_sp], w=[b_hst])
                    else:
                        S.op("dve", lambda e, hs_g=hs_g, g=g: e.tensor_tensor(
                            out=h16(hs_g), in0=h16(hs_g), in1=bc16(edt_all[:, 2, c, :], 8 * g, 8), op=ALU.mult),
                            r=[b_hst, ball], w=[b_hst])
                        S.op("dve", lambda e, hs_g=hs_g, sps=sps: e.tensor_tensor(out=hs_g, in0=hs_g, in1=sps,
                                                                                   op=ALU.add),
                             r=[b_hst, b_sp], w=[b_hst])
                    S.op("pool", lambda e, hs_g=hs_g, g=g: e.tensor_copy(out=hsb[:, g * 512:(g + 1) * 512], in_=hs_g),
                         r=[b_hst], w=[b_hsb])

        def backA(c):
            st = CH[c]
            tok0 = c * 128
            y1, b_y1, zt, b_zt = st.y1, st.b_y1, st.zt, st.b_zt
            ss, b_ss = ss_r.next()
            if c > 0:
                S.op("dve", lambda e: e.tensor_tensor(out=y1[:, :], in0=y1[:, :], in1=yoffs[:, :], op=ALU.add),
                     r=[b_y1, b_yoffs], w=[b_y1])
            S.op("pool", lambda e: e.tensor_tensor(out=y1[:, :], in0=y1[:, :], in1=zt[:, :], op=ALU.mult),
                 r=[b_y1, b_zt], w=[b_y1])
            for g in range(2):
                S.op("act", lambda e, g=g: e.activation(out=junk[:, g * 512:(g + 1) * 512],
                                                        in_=y1[:, g * 512:(g + 1) * 512], func=AF.Square,
                                                        accum_out=ss[:, g:g + 1]),
                     r=[b_y1], w=[b_junk, b_ss])
            S.op("act", lambda e: e.activation(out=ss[:, 2:4], in_=ss[:, 0:2], func=AF.Ln, scale=1.0 / 512, bias=EPS),
                 r=[b_ss], w=[b_ss])
            S.op("act", lambda e: e.activation(out=ss[:, 4:6], in_=ss[:, 2:4], func=AF.Exp, scale=-0.5),
                 r=[b_ss], w=[b_ss])
            st.ss, st.b_ss = ss, b_ss

        def backB(c):
            st = CH[c]
            tok0 = c * 128
            y1, b_y1, ss, b_ss = st.y1, st.b_y1, st.ss, st.b_ss
            for g in range(2):
                S.op("dve", lambda e, g=g: e.scalar_tensor_tensor(
                    out=y1[:, g * 512:(g + 1) * 512], in0=y1[:, g * 512:(g + 1) * 512], scalar=ss[:, 4 + g:5 + g],
                    in1=gssd[:, g * 512:(g + 1) * 512], op0=ALU.mult, op1=ALU.mult),
                    r=[b_y1, b_ss, bk], w=[b_y1])
            o_t, b_ot = ot.next()
            for half in range(2):
                for kk in range(4):
                    k = half * 4 + kk
                    S.op("pe", lambda e, k=k, kk=kk: e.transpose(out=SBK[:, kk * 128:(kk + 1) * 128],
                                                              in_=y1[:, k * 128:(k + 1) * 128], identity=C.identF[:, :]),
                         r=[b_y1, C.bc], w=[bb])
                S.op("dve", lambda e, half=half: e.tensor_copy(
                    out=o_t[:, half * 4:(half + 1) * 4, :], in_=SBK.rearrange("p (k t) -> p k t", k=4)),
                    r=[bb], w=[b_ot])
            S.dma("sp", [(ossd_v[:, :, tok0:tok0 + 128], o_t[:, :, :])], r=[b_ot])
            del CH[c]

        front(0)
        yield
        for qd in range(4):
            QD(0, qd)
            QY(0, qd)
            yield
        for c in range(NT):
            n = c + 1
            if n < NT:
                front(n)
                yield
                QD(n, 0)
                yield
            mid(c)
            yield
            if n < NT:
                QD(n, 1)
                QY(n, 0)
                yield
            backA(c)
            yield
            if n < NT:
                QD(n, 2)
                QY(n, 1)
                yield
            backB(c)
            yield
            if n < NT:
                QD(n, 3)
                QY(n, 2)
                yield
                QY(n, 3)
                yield
        yield "done"


def ssd_phase_1bank(C):
    gen = ssd_steps(C)
    for r in gen:
        if r == "done":
            break
    C.S.deferred_flush()
    for _ in gen:
        pass


def outxa_phase(C):
    nc, S, I, W = C.nc, C.S, C.I, C.W
    PP, PB = C.PP, C.PB
    with ExitStack() as es:
        def sb(name, shape, dt):
            return es.enter_context(nc.sbuf_tensor(name + "_p%d" % S.phase, shape, dt))
        kxT, b_kxT = sb("kxT", [128, 8, MEM], BF16), Buf("kxT")
        vx, b_vx = sb("vx", [128, 2, D], BF16), Buf("vx")
        banks = Rot([(PP[i][:, h * 512:(h + 1) * 512], PB[i][h]) for i in range(2) for h in range(2)])
        with ExitStack() as es1:
            def sb1(name, shape, dt):
                return es1.enter_context(nc.sbuf_tensor(name + "_q%d" % S.phase, shape, dt))
            wkv = sb1("wkv", [128, 8, 2 * D], BF16)
            b_wkv = [Buf("wkv0"), Buf("wkv1")]
            load_w_bf16(C, wkv, I["xa_wkv"], 0, 8, [(0, 1024), (1024, 2048)], b_wkv)
            gmT, b_gmT = sb1("gmT", [128, 8], F32), Buf("gmT")
            load_gT(C, gmT, I["mem_g"], b_gmT)
            pool1 = {"ss": Rot([(sb1("ss%d" % i, [128, 4], F32), Buf("ss%d" % i)) for i in range(2)]),
                     "junk": (sb1("junk", [128, D], F32), Buf("junk")),
                     "xn": (sb1("xn", [128, D], F32), Buf("xn"))}
            memT, b_memT = sb1("memT", [128, 8, MEM], BF16), [Buf("memT0"), Buf("memT1")]
            for mb in range(2):
                mt, b_mt = sb1("memt%d" % mb, [128, D], F32), Buf("memt%d" % mb)
                S.dma("sp", [(mt[:, :], I["mem"][mb * 128:(mb + 1) * 128, :])], w=[b_mt])
                norm_T(C, pool1, mt[:, :], b_mt, gmT, b_gmT, memT, mb * 128, b_memT[mb], PP[2 + mb], PB[2 + mb])
            for c in range(8):
                pa, pba = banks.next()
                for k in range(8):
                    S.op("pe", lambda e, k=k, c=c, pa=pa: e.matmul(
                        pa[:, 0:MEM], lhsT=wkv[:, k, c * 128:(c + 1) * 128], rhs=memT[:, k, :],
                        start=(k == 0), stop=(k == 7)), r=[b_wkv[0]] + b_memT, w=[pba])
                S.op("dve", lambda e, c=c, pa=pa: e.tensor_copy(out=kxT[:, c, :], in_=pa[:, 0:MEM]),
                     r=[pba], w=[b_kxT])
            for mb in range(2):
                for n in range(2):
                    pa, pba = banks.next()
                    for k in range(8):
                        S.op("pe", lambda e, k=k, mb=mb, n=n, pa=pa: e.matmul(
                            pa, lhsT=memT[:, k, mb * 128:(mb + 1) * 128],
                            rhs=wkv[:, k, D + n * 512:D + (n + 1) * 512], start=(k == 0), stop=(k == 7)),
                            r=[b_wkv[1], b_memT[mb]], w=[pba])
                    S.op("act", lambda e, mb=mb, n=n, pa=pa: e.activation(out=vx[:, mb, n * 512:(n + 1) * 512],
                                                                          in_=pa, func=AF.Copy),
                         r=[pba], w=[b_vx])
            S.deferred_flush()
        wout = sb("wout", [128, 16, D], BF16)
        b_wout = [Buf("wout0"), Buf("wout1")]
        wq, b_wq = sb("wq", [128, 8, D], BF16), Buf("wq")
        wo, b_wo = sb("wo", [128, 8, D], BF16), Buf("wo")
        gaT, b_gaT = sb("gaT", [128, 8], F32), Buf("gaT")
        gxT, b_gxT = sb("gxT", [128, 8], F32), Buf("gxT")
        gb_mix, b_gbm = sb("gb_mix", [128, D], F32), Buf("gb_mix")
        gb_xa, b_gbx = sb("gb_xa", [128, D], F32), Buf("gb_xa")
        load_gT(C, gaT, I["attn_norm_g"], b_gaT)
        load_gT(C, gxT, I["xa_pre_g"], b_gxT)
        load_bcast(C, gb_mix, I["mix_post_g"], b_gbm, D)
        load_bcast(C, gb_xa, I["xa_post_g"], b_gbx, D)
        for half in range(2):
            S.dma("pool", [(wout[:, k, :], I["w_out"][k * 128:(k + 1) * 128, :]) for k in range(half * 8, half * 8 + 8)],
                  w=[b_wout[half]])
        S.dma("pool", [(wq[:, k, :], I["xa_wq"][k * 128:(k + 1) * 128, :]) for k in range(8)], w=[b_wq])
        S.dma("pool", [(wo[:, k, :], I["xa_wo"][k * 128:(k + 1) * 128, :]) for k in range(8)], w=[b_wo])

        pool = {"ss": Rot([(sb("ss%d" % i, [128, 4], F32), Buf("ss%d" % i)) for i in range(4)]),
                "junk": (sb("junk", [128, D], F32), Buf("junk")),
                "xn": (sb("xn", [128, D], F32), Buf("xn"))}
        t1, b_t1 = sb("t1", [128, D], F32), Buf("t1")
        oa, b_oa = sb("oa", [128, 8, 512], F32), Buf("oa")
        os_, b_os = sb("os", [128, 8, 512], BF16), Buf("os")
        sqk = Rot([(sb("sqk%d" % i, [128, 512], BF16), Buf("sqk%d" % i)) for i in range(3)])
        rsb, b_rsb = sb("rsb", [128, 512], F32), Buf("rsb")
        oan, b_oan = sb("oan", [128, 8, 512], BF16), Buf("oan")
        h2 = [(sb("h2_%d" % t, [128, D], F32), Buf("h2_%d" % t)) for t in range(4)]
        hnT = sb("hnT", [128, 8, 512], BF16)
        b_hnT = [Buf("hnT%d" % t) for t in range(4)]
        qxT, b_qxT = sb("qxT", [128, 8, 512], BF16), [Buf("qxT%d" % c) for c in range(8)]
        pT = sb("pT", [128, 4, 2, 512], BF16)
        b_pT = [[Buf("pT%d_%d" % (hh, mb)) for mb in range(2)] for hh in range(4)]
        rden = Rot([(sb("rden%d" % i, [128, 512], F32), Buf("rden%d" % i)) for i in range(2)])
        oxn, b_oxn = sb("oxn", [128, 8, 512], BF16), [Buf("oxn%d" % c) for c in range(8)]
        oat_v = W["oattT"].rearrange("(k p) t -> p k t", p=128)
        oss_v = W["ossdT"].rearrange("(k p) t -> p k t", p=128)

        def S1(blk):
            tok0 = blk * 512
            S.dma("sp", [(oa[:, :, :], oat_v[:, :, tok0:tok0 + 512])], w=[b_oa])
            S.dma("sp", [(os_[:, :, :], oss_v[:, :, tok0:tok0 + 512])], w=[b_os])
            pa, pba = banks.next()
            for k in range(8):
                sq, b_sq = sqk.next()
                S.op("act", lambda e, k=k, sq=sq: e.activation(out=sq[:, :], in_=oa[:, k, :], func=AF.Square),
                     r=[b_oa], w=[b_sq])
                S.op("pe", lambda e, k=k, sq=sq, pa=pa: e.matmul(pa, lhsT=C.onesB[:, :], rhs=sq[:, :],
                                                                 start=(k == 0), stop=(k == 7)),
                     r=[b_sq, C.bc], w=[pba])
            S.op("act", lambda e, pa=pa: e.activation(out=rsb[:, :], in_=pa, func=AF.Ln, scale=1.0 / D, bias=EPS),
                 r=[pba], w=[b_rsb])
            S.op("act", lambda e: e.activation(out=rsb[:, :], in_=rsb[:, :], func=AF.Exp, scale=-0.5),
                 r=[b_rsb], w=[b_rsb])
            for k in range(8):
                S.op("dve", lambda e, k=k: e.scalar_tensor_tensor(out=oan[:, k, :], in0=oa[:, k, :],
                                                                  scalar=gaT[:, k:k + 1], in1=rsb[:, :],
                                                                  op0=ALU.mult, op1=ALU.mult),
                     r=[b_oa, b_gaT, b_rsb], w=[b_oan])

        NBX = SEQ // 512
        S1(0)
        for blk in range(NBX):
            tok0 = blk * 512
            for t in range(4):
                tt = blk * 4 + t
                pf, pbf = PP[2 + t % 2], PB[2 + t % 2]
                for n in range(2):
                    for k in range(16):
                        src_, bsrc = (oan, b_oan) if k < 8 else (os_, b_os)
                        S.op("pe", lambda e, n=n, k=k, t=t, pf=pf, src_=src_: e.matmul(
                            pf[:, n * 512:(n + 1) * 512], lhsT=src_[:, k % 8, t * 128:(t + 1) * 128],
                            rhs=wout[:, k, n * 512:(n + 1) * 512], start=(k == 0), stop=(k == 15)),
                            r=[bsrc, b_wout[k // 8]], w=[pbf[n]])
                xr, bxr = h2[t]
                S.dma("sp", [(xr[:, :], W["h1"][tt * 128:(tt + 1) * 128, :])], w=[bxr])
                if t > 0:
                    xp_, bxp_ = h2[t - 1]
                    norm_T(C, pool, xp_[:, :], bxp_, gxT, b_gxT, hnT, (t - 1) * 128, b_hnT[t - 1],
                           PP[(t - 1) % 2], PB[(t - 1) % 2])
                post_norm_residual(C, pool, pf, pbf, gb_mix, b_gbm, xr, bxr, t1, b_t1, 1.0)
            xp_, bxp_ = h2[3]
            norm_T(C, pool, xp_[:, :], bxp_, gxT, b_gxT, hnT, 3 * 128, b_hnT[3], PP[1], PB[1])
            for c in range(8):
                pa, pba = banks.next()
                for k in range(8):
                    S.op("pe", lambda e, k=k, c=c, pa=pa: e.matmul(
                        pa, lhsT=wq[:, k, c * 128:(c + 1) * 128], rhs=hnT[:, k, :], start=(k == 0), stop=(k == 7)),
                        r=[b_wq] + b_hnT, w=[pba])
                S.op("act", lambda e, c=c, pa=pa: e.activation(out=qxT[:, c, :], in_=pa, func=AF.Copy,
                                                               scale=1.0 / 16.0),
                     r=[pba], w=[b_qxT[c]])
            if blk + 1 < NBX:
                S1(blk + 1)
            for hh in range(4):
                for mb in range(2):
                    pa, pba = banks.next()
                    for dc in range(2):
                        cch = 2 * hh + dc
                        S.op("pe", lambda e, cch=cch, mb=mb, dc=dc, pa=pa: e.matmul(
                            pa, lhsT=kxT[:, cch, mb * 128:(mb + 1) * 128], rhs=qxT[:, cch, :],
                            start=(dc == 0), stop=(dc == 1)),
                            r=[b_kxT, b_qxT[cch]], w=[pba])
                    S.op("act", lambda e, hh=hh, mb=mb, pa=pa: e.activation(out=pT[:, hh, mb, :], in_=pa, func=AF.Exp),
                         r=[pba], w=[b_pT[hh][mb]])
                pa, pba = banks.next()
                for mb in range(2):
                    S.op("pe", lambda e, hh=hh, mb=mb, pa=pa: e.matmul(pa, lhsT=C.onesB[:, :], rhs=pT[:, hh, mb, :],
                                                                       start=(mb == 0), stop=(mb == 1)),
                         r=[b_pT[hh][mb], C.bc], w=[pba])
                rd, b_rd = rden.next()
                S.op("dve", lambda e, rd=rd, pa=pa: e.reciprocal(out=rd[:, :], in_=pa), r=[pba], w=[b_rd])
                for dc in range(2):
                    cch = 2 * hh + dc
                    pa, pba = banks.next()
                    for mb in range(2):
                        S.op("pe", lambda e, cch=cch, mb=mb, hh=hh, pa=pa: e.matmul(
                            pa, lhsT=vx[:, mb, cch * 128:(cch + 1) * 128], rhs=pT[:, hh, mb, :],
                            start=(mb == 0), stop=(mb == 1)),
                            r=[b_vx, b_pT[hh][mb]], w=[pba])
                    S.op("dve", lambda e, cch=cch, rd=rd, pa=pa: e.tensor_tensor(out=oxn[:, cch, :], in0=pa,
                                                                                in1=rd[:, :], op=ALU.mult),
                         r=[pba, b_rd], w=[b_oxn[cch]])
            for t in range(4):
                tt = blk * 4 + t
                pf, pbf = PP[2 + t % 2], PB[2 + t % 2]
                for n in range(2):
                    for k in range(8):
                        S.op("pe", lambda e, n=n, k=k, t=t, pf=pf: e.matmul(
                            pf[:, n * 512:(n + 1) * 512], lhsT=oxn[:, k, t * 128:(t + 1) * 128],
                            rhs=wo[:, k, n * 512:(n + 1) * 512], start=(k == 0), stop=(k == 7)),
                            r=[b_oxn[k], b_wo], w=[pbf[n]])
                xr, bxr = h2[t]
                post_norm_residual(C, pool, pf, pbf, gb_xa, b_gbx, xr, bxr, t1, b_t1, 1.0)
                S.dma("pool", [(W["h3"][tt * 128:(tt + 1) * 128, :], xr[:, :])], r=[bxr])
        S.deferred_flush()
```

```python
import math
from contextlib import ExitStack

import numpy as np
import concourse.bass as bass
import concourse.mybir as mybir
from concourse.bass_utils import run_bass_kernel_spmd

F32 = mybir.dt.float32
BF16 = mybir.dt.bfloat16
AF = mybir.ActivationFunctionType
ALU = mybir.AluOpType
AX = mybir.AxisListType

ENGS = ("pe", "act", "dve", "pool", "sp")


class Buf:
    def __init__(self, name, excl=False):
        self.name = name
        self.excl = excl
        self.w = None
        self.r = []


class Op:
    __slots__ = ("eng", "fn", "deps", "flag", "val", "dma_sem", "dma_val", "ndma", "tag", "phase")

    def __init__(self, eng, fn):
        self.phase = 0
        self.eng = eng
        self.fn = fn
        self.deps = []
        self.flag = False
        self.val = None
        self.dma_sem = None
        self.dma_val = None
        self.ndma = 0
        self.tag = None


class Sched:
    def __init__(self, nc):
        self.nc = nc
        self.handles = {"pe": nc.tensor, "act": nc.scalar, "dve": nc.vector,
                        "pool": nc.gpsimd, "sp": nc.sync}
        self.sem = {e: nc.alloc_semaphore("s_" + e) for e in ("pe", "act", "dve", "pool")}
        self.count = {e: 0 for e in self.sem}
        self.ops = {e: [] for e in ENGS}
        self.last = {e: None for e in ENGS}
        self.dmasems = {}
        self.waited = {e: {} for e in ENGS}
        self.n_inst = 0
        self.prev_tokens = []
        self.phase = 0
        self.defer = False
        self.replaying = False
        self.deferq = []

    def _dep(self, op, p):
        if p is None or p is op or p.phase != self.phase:
            return
        if p.eng == "pe" and op.eng == "pe" and p.dma_sem is None:
            return
        op.deps.append(p)
        if p.dma_sem is None:
            p.flag = True

    def replay(self, n):
        q = self.deferq
        k = 0
        self.replaying = True
        while q and k < n:
            kind, args, kw = q.pop(0)
            if kind == "op":
                self.op(*args, **kw)
            else:
                self.dma(*args, **kw)
            k += 1
        self.replaying = False
        return len(q)

    def op(self, eng, fn, r=(), w=()):
        if self.defer and not self.replaying:
            self.deferq.append(("op", (eng, fn), {"r": list(r), "w": list(w)}))
            return None
        o = Op(eng, fn)
        o.phase = self.phase
        w = list(w) + [b for b in r if b.excl]
        r = [b for b in r if not b.excl]
        for b in r:
            self._dep(o, b.w)
        for b in w:
            self._dep(o, b.w)
            for x in b.r:
                self._dep(o, x)
        for b in r:
            b.r.append(o)
        for b in w:
            b.w = o
            b.r = []
        self.ops[eng].append(o)
        self.last[eng] = o
        return o

    def dma(self, q, pairs, r=(), w=(), key=None, **kw):
        if self.defer and not self.replaying:
            kw2 = dict(kw)
            kw2.update({"r": list(r), "w": list(w), "key": key})
            self.deferq.append(("dma", (q, pairs), kw2))
            return None
        if key is None:
            key = (w[0].name if w else r[0].name) + ("_ld" if w else "_st")
        if key not in self.dmasems:
            self.dmasems[key] = [self.nc.alloc_semaphore("d_" + str(len(self.dmasems))), 0]
        ent = self.dmasems[key]
        sem = ent[0]

        def fn(e, pairs=pairs, sem=sem, kw=kw):
            for (o_, i_) in pairs:
                e.dma_start(out=o_, in_=i_, **kw).then_inc(sem, 16)
            return None

        o = self.op(q, fn, r=r, w=w)
        ent[1] += 16 * len(pairs)
        o.dma_sem = sem
        o.dma_val = ent[1]
        o.ndma = len(pairs)
        return o

    def deferred_flush(self):
        if not hasattr(self, "phases"):
            self.phases = []
        self.phases.append(self.ops)
        self.ops = {e: [] for e in ENGS}
        self.phase += 1

    def emit_all(self):
        comp = ("pe", "act", "dve", "pool")
        sched = self
        plan = []
        prev = []
        for ops in self.phases:
            for e in comp:
                for o in reversed(ops[e]):
                    if o.dma_sem is None and o.fn is not None:
                        o.flag = True
                        break
            for e in comp:
                c = self.count[e]
                for o in ops[e]:
                    if o.dma_sem is None and o.flag:
                        c += 1
                        o.val = c
                self.count[e] = c
            plan.append((ops, list(prev)))
            prev = [(self.sem[e], self.count[e]) for e in comp if self.count[e] > 0]
            dm = {}
            for e in ENGS:
                for o in ops[e]:
                    if o.dma_sem is not None:
                        dm[id(o.dma_sem)] = (o.dma_sem, max(o.dma_val, dm.get(id(o.dma_sem), (None, 0))[1]))
            self._dm_seen = getattr(self, "_dm_seen", {})
            self._dm_seen.update(dm)
            prev += list(self._dm_seen.values())
        final_toks = list(prev)

        def emit_waits(eh, waited, toks):
            for sem, val in toks:
                k = id(sem)
                if waited.get(k, 0) >= val:
                    continue
                eh.wait_ge(sem, val)
                waited[k] = val

        def emit(ename, eh):
            waited = sched.waited[ename]
            for ops, prevt in plan:
                emit_waits(eh, waited, prevt)
                for o in ops[ename]:
                    toks = []
                    for p in o.deps:
                        if p.dma_sem is not None:
                            toks.append((p.dma_sem, p.dma_val))
                        else:
                            toks.append((sched.sem[p.eng], p.val))
                    emit_waits(eh, waited, toks)
                    ins = o.fn(eh)
                    if o.dma_sem is None and o.flag:
                        ins.then_inc(sched.sem[ename], 1)
            emit_waits(eh, waited, final_toks)

        with self.nc.Block() as block:
            @block.tensor
            def _(e):
                emit("pe", e)

            @block.scalar
            def _(e):
                emit("act", e)

            @block.vector
            def _(e):
                emit("dve", e)

            @block.gpsimd
            def _(e):
                emit("pool", e)

            @block.sync
            def _(e):
                emit("sp", e)

    def flush(self, final=False):
        comp = ("pe", "act", "dve", "pool")
        for e in comp:
            for o in reversed(self.ops[e]):
                if o.dma_sem is None and o.fn is not None:
                    o.flag = True
                    break
        for e in comp:
            c = self.count[e]
            for o in self.ops[e]:
                if o.dma_sem is None and o.flag:
                    c += 1
                    o.val = c
            self.count[e] = c
        sched = self
        prev = list(self.prev_tokens)

        def emit_waits(eh, waited, toks):
            for sem, val in toks:
                k = id(sem)
                if waited.get(k, 0) >= val:
                    continue
                eh.wait_ge(sem, val)
                waited[k] = val

        def emit(ename, eh):
            waited = sched.waited[ename]
            emit_waits(eh, waited, prev)
            for o in sched.ops[ename]:
                toks = []
                for p in o.deps:
                    if p.dma_sem is not None:
                        toks.append((p.dma_sem, p.dma_val))
                    else:
                        assert p.val is not None
                        toks.append((sched.sem[p.eng], p.val))
                emit_waits(eh, waited, toks)
                ins = o.fn(eh)
                sched.n_inst += 1
                if o.dma_sem is None and o.flag:
                    ins.then_inc(sched.sem[ename], 1)

        def run_block(fn_for):
            with self.nc.Block() as block:
                @block.tensor
                def _(e):
                    fn_for("pe", e)

                @block.scalar
                def _(e):
                    fn_for("act", e)

                @block.vector
                def _(e):
                    fn_for("dve", e)

                @block.gpsimd
                def _(e):
                    fn_for("pool", e)

                @block.sync
                def _(e):
                    fn_for("sp", e)

        run_block(emit)
        self.ops = {e: [] for e in ENGS}
        self.phase += 1
        self.prev_tokens = [(self.sem[e], self.count[e]) for e in comp if self.count[e] > 0]
        self.prev_tokens += [(s, c) for (s, c) in self.dmasems.values() if c > 0]
        if final:
            toks = list(self.prev_tokens)

            def fin(ename, eh):
                emit_waits(eh, sched.waited[ename], toks)

            run_block(fin)


D = 1024
SEQ = 4096
NT = SEQ // 128
DFF = 2816
NJ = DFF // 128
D_IN = 5648
MEM = 256
EPS = 1e-6
NEG = -30000.0
N_WARM = 0
SSD_1BANK = False
OVERLAP_SSD = True
SIDE_EVERY = 4


class Ctx:
    pass


def build_nc(debug=False, stages=("ffn1", "inproj", "attn", "ssd", "outxa", "ffn2")):
    nc = bass.Bass("TRN2", target_bir_lowering=False)
    S = Sched(nc)
    C = Ctx()
    C.nc, C.S = nc, S

    def din(name, shape):
        return nc.dram_tensor(name, list(shape), F32, kind="ExternalInput")

    I = {}
    I["x"] = din("x", [SEQ, D])
    I["mem"] = din("mem", [MEM, D])
    for nm, shp in [("ffn1_pre_g", [1, D]), ("ffn1_w_gu", [D, 2 * DFF]), ("ffn1_w_down", [DFF, D]),
                    ("ffn1_post_g", [1, D]), ("mix_pre_g", [1, D]), ("w_in", [D, D_IN]),
                    ("conv_w", [4, 1536]), ("conv_b", [1, 1536]), ("dt_bias", [1, 16]),
                    ("a_log", [1, 16]), ("d_skip", [1, 16]), ("ssd_norm_g", [1, D]),
                    ("attn_norm_g", [1, D]), ("w_out", [2 * D, D]), ("mix_post_g", [1, D]),
                    ("xa_pre_g", [1, D]), ("mem_g", [1, D]), ("xa_wq", [D, D]), ("xa_wkv", [D, 2 * D]),
                    ("xa_wo", [D, D]), ("xa_post_g", [1, D]), ("ffn2_pre_g", [1, D]),
                    ("ffn2_w_gu", [D, 2 * DFF]), ("ffn2_w_down", [DFF, D]), ("ffn2_post_g", [1, D])]:
        I[nm] = din(nm, shp)
    out = nc.dram_tensor("out", [SEQ, D], F32, kind="ExternalOutput")
    kd = "ExternalOutput" if debug else "Internal"
    W = {}
    W["h1"] = nc.dram_tensor("h1", [SEQ, D], F32, kind=kd)
    W["qT"] = nc.dram_tensor("qT", [D, SEQ], BF16, kind=kd)
    W["kT"] = nc.dram_tensor("kT", [D, SEQ], BF16, kind=kd)
    W["v"] = nc.dram_tensor("v", [SEQ, D], BF16, kind=kd)
    W["z"] = nc.dram_tensor("z", [SEQ, D], F32, kind=kd)
    W["xbcT"] = nc.dram_tensor("xbcT", [1536, SEQ], F32, kind=kd)
    W["dtr"] = nc.dram_tensor("dtr", [SEQ, 16], F32, kind=kd)
    W["oattT"] = nc.dram_tensor("oattT", [D, SEQ], F32, kind=kd)
    W["ossdT"] = nc.dram_tensor("ossdT", [D, SEQ], BF16, kind=kd)
    W["h3"] = nc.dram_tensor("h3", [SEQ, D], F32, kind=kd)
    C.I, C.W, C.out = I, W, out

    with ExitStack() as es0:
        def sb0(name, shape, dt):
            return es0.enter_context(nc.sbuf_tensor(name, shape, dt))

        C.PP = [es0.enter_context(nc.psum_tensor("pp%d" % i, [128, 1024], F32)) for i in range(4)]
        C.PB = [[Buf("pb%d_%d" % (i, h), excl=True) for h in range(2)] for i in range(4)]

        cst = {}
        bc = Buf("consts")
        C.bc = bc
        onesF = sb0("onesF", [128, 128], F32)
        negF = sb0("negF", [128, 128], F32)
        bigF = sb0("bigF", [128, 128], F32)
        S.op("pool", lambda e: e.memset(onesF[:, :], 1.0), w=[bc])
        S.op("pool", lambda e: e.memset(negF[:, :], -1.0), w=[bc])
        S.op("pool", lambda e: e.memset(bigF[:, :], NEG), w=[bc])

        scratchF = sb0("cscratch_f", [128, 128], F32)

        def mk(name, src, cm, step, cmp, dt=F32):
            tf = sb0(name + "_f", [128, 128], F32) if dt == F32 else scratchF
            S.op("pool", lambda e: e.affine_select(out=tf[:, :], in_=src[:, :], pattern=[[step, 128]],
                                                   compare_op=cmp, fill=0.0, base=0, channel_multiplier=cm),
                 r=[bc], w=[bc])
            if dt == F32:
                return tf
            tb = sb0(name + "_b", [128, 128], BF16)
            S.op("pool", lambda e: e.tensor_copy(out=tb[:, :], in_=tf[:, :]), r=[bc], w=[bc])
            return tb

        C.onesF = onesF
        C.identF = mk("ident", onesF, 1, -1, ALU.is_equal)
        C.identB = mk("identb", onesF, 1, -1, ALU.is_equal, BF16)
        C.triNegB = mk("trineg", negF, 1, -1, ALU.is_ge, BF16)
        C.mask01 = mk("mask01", onesF, -1, 1, ALU.is_gt)
        C.negMaskB = mk("negmask", bigF, 1, -1, ALU.is_ge, BF16)
        C.triInclF = mk("triincl", onesF, -1, 1, ALU.is_ge)
        C.triUpF = mk("triup", onesF, 1, -1, ALU.is_gt)
        C.triUpB = mk("triupb", onesF, 1, -1, ALU.is_gt, BF16)
        C.negMaskSB = mk("negmasks", bigF, 1, -1, ALU.is_gt, BF16)
        onesNegB = sb0("onesNegB", [128, 128], BF16)
        S.op("pool", lambda e: e.tensor_copy(out=onesNegB[:, :], in_=negF[:, :]), r=[bc], w=[bc])
        C.onesNegB = onesNegB
        onesB = sb0("onesB", [128, 128], BF16)
        S.op("pool", lambda e: e.tensor_copy(out=onesB[:, :], in_=onesF[:, :]), r=[bc], w=[bc])
        C.onesB = onesB
        S.deferred_flush()

        if "ffn1" in stages:
            ffn_phase(C, I["x"], W["h1"], I["ffn1_w_gu"], I["ffn1_w_down"], I["ffn1_pre_g"], I["ffn1_post_g"])
        if "inproj" in stages:
            inproj_phase(C)
        if "attn" in stages and "ssd" in stages and OVERLAP_SSD:
            attn_phase(C, side=ssd_steps(C))
        else:
            if "attn" in stages:
                attn_phase(C)
            if "ssd" in stages:
                ssd_phase_1bank(C) if SSD_1BANK else ssd_phase(C)
        if "outxa" in stages:
            outxa_phase(C)
        if "ffn2" in stages:
            ffn_phase(C, W["h3"], out, I["ffn2_w_gu"], I["ffn2_w_down"], I["ffn2_pre_g"], I["ffn2_post_g"])
        S.deferred_flush()
        S.emit_all()
    return nc


class Rot:
    def __init__(self, items):
        self.items = items
        self.i = 0

    def next(self):
        it = self.items[self.i % len(self.items)]
        self.i += 1
        return it


def norm_A(C, pool, x_ap, bx):
    S = C.S
    ss, b_ss = pool["ss"].next()
    junk, b_junk = pool["junk"]
    xn, b_xn = pool["xn"]
    S.op("act", lambda e: e.activation(out=junk[:, :], in_=x_ap, func=AF.Square, accum_out=ss[:, 0:1]),
         r=[bx], w=[b_junk, b_ss])
    S.op("act", lambda e: e.activation(out=ss[:, 1:2], in_=ss[:, 0:1], func=AF.Ln, scale=1.0 / D, bias=EPS),
         r=[b_ss], w=[b_ss])
    S.op("act", lambda e: e.activation(out=ss[:, 2:3], in_=ss[:, 1:2], func=AF.Exp, scale=-0.5),
         r=[b_ss], w=[b_ss])
    S.op("dve", lambda e: e.tensor_scalar(out=xn[:, :], in0=x_ap, scalar1=ss[:, 2:3], scalar2=None, op0=ALU.mult),
         r=[bx, b_ss], w=[b_xn])


def norm_B(C, pool, gT, b_gT, dstT, col0, b_dst, pp, pb):
    S = C.S
    xn, b_xn = pool["xn"]
    for k in range(8):
        S.op("pe", lambda e, k=k: e.transpose(out=pp[:, k * 128:(k + 1) * 128], in_=xn[:, k * 128:(k + 1) * 128],
                                              identity=C.identF[:, :]),
             r=[b_xn, C.bc], w=[pb[k // 4]])
    for h in range(2):
        S.op("dve", lambda e, h=h: e.tensor_tensor(
            out=dstT[:, h * 4:(h + 1) * 4, col0:col0 + 128],
            in0=pp[:, h * 512:(h + 1) * 512].rearrange("p (k t) -> p k t", k=4),
            in1=gT[:, h * 4:(h + 1) * 4].unsqueeze(2).to_broadcast([128, 4, 128]), op=ALU.mult),
            r=[pb[h], b_gT], w=[b_dst])


def norm_T(C, pool, x_ap, bx, gT, b_gT, dstT, col0, b_dst, pp, pb):
    norm_A(C, pool, x_ap, bx)
    norm_B(C, pool, gT, b_gT, dstT, col0, b_dst, pp, pb)


def load_w_bf16(C, dst, w_dram, row0, nk, col_groups, bufs, dcol0=0):
    S = C.S
    for (c0, c1), b in zip(col_groups, bufs):
        pairs = []
        for k in range(nk):
            for s0 in range(c0, c1, 2048):
                s1 = min(c1, s0 + 2048)
                pairs.append((dst[:, k, dcol0 + s0:dcol0 + s1],
                              w_dram[row0 + k * 128:row0 + (k + 1) * 128, s0:s1]))
        S.dma("pool", pairs, w=[b])


def load_gT(C, dst, g_dram, b, n=8):
    C.S.dma("sp", [(dst[:, 0:n], g_dram[0, :].rearrange("(k p) -> p k", p=128))], w=[b],
            allow_slow_non_contiguous=True)


def load_bcast(C, dst, g_dram, b, n):
    C.S.dma("sp", [(dst[:, 0:n], g_dram[0:1, :].partition_broadcast(128))], w=[b])


def ffn_phase(C, src, dst, w_gu, w_dn, g_pre, g_post):
    nc, S = C.nc, C.S
    PP, PB = C.PP, C.PB
    with ExitStack() as es:
        def sb(name, shape, dt):
            return es.enter_context(nc.sbuf_tensor(name + "_p%d" % S.phase, shape, dt))
        wgu = sb("wgu", [128, 8, 2 * DFF], BF16)
        b_wgu = [Buf("wgu%d" % i) for i in range(8)]
        wd = sb("wd", [128, NJ, D], BF16)
        b_wd = [Buf("wd0"), Buf("wd1")]
        gT = sb("gT", [128, 8], F32); b_gT = Buf("gT")
        gb = sb("gb", [128, D], F32); b_gb = Buf("gb")
        load_gT(C, gT, g_pre, b_gT)
        load_bcast(C, gb, g_post, b_gb, D)
        grp = []
        for q4 in range(4):
            grp += [(q4 * 704, (q4 + 1) * 704), (DFF + q4 * 704, DFF + (q4 + 1) * 704)]
        load_w_bf16(C, wgu, w_gu, 0, 8, grp, [b_wgu[c0 // 704] for (c0, c1) in grp])
        for half in range(2):
            pairs = [(wd[:, j, :], w_dn[j * 128:(j + 1) * 128, :]) for j in range(half * 11, half * 11 + 11)]
            S.dma("pool", pairs, w=[b_wd[half]])

        def wgu_bufs(col):
            return sorted({b_wgu[col // 704], b_wgu[(col + 127) // 704]}, key=id)

        xin = Rot([(sb("xin%d" % i, [128, D], F32), Buf("xin%d" % i)) for i in range(2)])
        xres = Rot([(sb("xres%d" % i, [128, D], F32), Buf("xres%d" % i)) for i in range(1)])
        pool = {"ss": Rot([(sb("ss%d" % i, [128, 4], F32), Buf("ss%d" % i)) for i in range(4)]),
                "junk": (sb("junk", [128, D], BF16), Buf("junk")),
                "xn": (sb("xn", [128, D], F32), Buf("xn"))}
        t1, b_t1 = sb("t1", [128, D], F32), Buf("t1")
        xnT2 = [sb("xnT%d" % i, [128, 8, 512], BF16) for i in range(2)]
        b_xnT2 = [[Buf("xnT%d_%d" % (i, t)) for t in range(4)] for i in range(2)]
        actT = sb("actT", [128, NJ, 512], BF16)
        b_actT = [Buf("actT%d" % j) for j in range(NJ)]
        sg = Rot([(sb("sg%d" % i, [128, 512], F32), Buf("sg%d" % i)) for i in range(2)])

        NB = SEQ // 512
        xcur = {}

        def pre_load(blk, t):
            tt = blk * 4 + t
            xi, bxi = xin.next()
            S.dma("sp", [(xi[:, :], src[tt * 128:(tt + 1) * 128, :])], w=[bxi])
            xcur[(blk, t)] = (xi, bxi)

        def pre_A(blk, t):
            xi, bxi = xcur.pop((blk, t))
            norm_A(C, pool, xi[:, :], bxi)

        def pre_B(blk, t):
            tt = blk * 4 + t
            norm_B(C, pool, gT, b_gT, xnT2[blk % 2], t * 128, b_xnT2[blk % 2][t], PP[2 + tt % 2], PB[2 + tt % 2])

        for t in range(4):
            pre_load(0, t) if t < 2 else None
        for t in range(4):
            if t + 2 < 4:
                pass
            if t >= 2:
                pre_load(0, t)
            pre_A(0, t)
            pre_B(0, t)
        for blk in range(NB):
            xnT = xnT2[blk % 2]
            b_xnT = b_xnT2[blk % 2]
            nxt = blk + 1 if blk + 1 < NB else None
            for j in range(NJ):
                if nxt is not None:
                    for t in range(4):
                        if j == 5 * t:
                            pre_load(nxt, t)
                        if j == 5 * t + 2:
                            pre_A(nxt, t)
                        if j == 5 * t + 4:
                            pre_B(nxt, t)
                pa, pba = PP[j % 2], PB[j % 2]
                for half, cbase in ((0, j * 128), (1, DFF + j * 128)):
                    for k in range(8):
                        S.op("pe", lambda e, k=k, half=half, cbase=cbase, pa=pa, xnT=xnT: e.matmul(
                            pa[:, half * 512:(half + 1) * 512], lhsT=wgu[:, k, cbase:cbase + 128],
                            rhs=xnT[:, k, :], start=(k == 0), stop=(k == 7)),
                            r=wgu_bufs(cbase) + b_xnT, w=[pba[half]])
                sgt, b_sg = sg.next()
                S.op("act", lambda e, pa=pa, sgt=sgt: e.activation(out=sgt[:, :], in_=pa[:, 0:512], func=AF.Silu),
                     r=[pba[0]], w=[b_sg])
                S.op("dve", lambda e, pa=pa, sgt=sgt, j=j: e.tensor_tensor(
                    out=actT[:, j, :], in0=sgt[:, :], in1=pa[:, 512:1024], op=ALU.mult),
                    r=[b_sg, pba[1]], w=[b_actT[j]])
            for t in range(4):
                tt = blk * 4 + t
                pf, pbf = PP[2 + t % 2], PB[2 + t % 2]
                for n in range(2):
                    for j in range(NJ):
                        S.op("pe", lambda e, n=n, j=j, t=t, pf=pf: e.matmul(
                            pf[:, n * 512:(n + 1) * 512], lhsT=actT[:, j, t * 128:(t + 1) * 128],
                            rhs=wd[:, j, n * 512:(n + 1) * 512], start=(j == 0), stop=(j == NJ - 1)),
                            r=[b_actT[j], b_wd[j // 11]], w=[pbf[n]])
                xr, bxr = xres.next()
                S.dma("sp", [(xr[:, :], src[tt * 128:(tt + 1) * 128, :])], w=[bxr])
                post_norm_residual(C, pool, pf, pbf, gb, b_gb, xr, bxr, t1, b_t1, 0.5)
                S.dma("pool", [(dst[tt * 128:(tt + 1) * 128, :], xr[:, :])], r=[bxr])
        S.deferred_flush()


def post_norm_residual(C, pool, pf, pbf, gb, b_gb, xr, bxr, t1, b_t1, coef):
    S = C.S
    ss, b_ss = pool["ss"].next()
    junk, b_junk = pool["junk"]
    S.op("act", lambda e: e.activation(out=junk[:, :], in_=pf[:, :], func=AF.Square, accum_out=ss[:, 0:1]),
         r=pbf, w=[b_junk, b_ss])
    S.op("act", lambda e: e.activation(out=ss[:, 1:2], in_=ss[:, 0:1], func=AF.Ln, scale=1.0 / D, bias=EPS),
         r=[b_ss], w=[b_ss])
    S.op("act", lambda e: e.activation(out=ss[:, 2:3], in_=ss[:, 1:2], func=AF.Exp, scale=-0.5),
         r=[b_ss], w=[b_ss])
    S.op("dve", lambda e: e.scalar_tensor_tensor(out=t1[:, :], in0=pf[:, :], scalar=ss[:, 2:3], in1=gb[:, :],
                                                 op0=ALU.mult, op1=ALU.mult),
         r=pbf + [b_ss, b_gb], w=[b_t1])
    S.op("dve", lambda e: e.scalar_tensor_tensor(out=xr[:, :], in0=t1[:, :], scalar=float(coef), in1=xr[:, :],
                                                 op0=ALU.mult, op1=ALU.add),
         r=[b_t1, bxr], w=[bxr])


PARAM_NAMES = ["ffn1_pre_g", "ffn1_w_gu", "ffn1_w_down", "ffn1_post_g", "mix_pre_g", "w_in", "conv_w", "conv_b",
               "dt_bias", "a_log", "d_skip", "ssd_norm_g", "attn_norm_g", "w_out", "mix_post_g", "xa_pre_g",
               "mem_g", "xa_wq", "xa_wkv", "xa_wo", "xa_post_g", "ffn2_pre_g", "ffn2_w_gu", "ffn2_w_down",
               "ffn2_post_g"]


def core_inputs(inputs, b):
    m = {"x": np.ascontiguousarray(inputs["x"][b], dtype=np.float32),
         "mem": np.ascontiguousarray(inputs["mem"][b], dtype=np.float32)}
    for nm in PARAM_NAMES:
        a = np.asarray(inputs[nm], dtype=np.float32)[0]
        if a.ndim == 1:
            a = a[None, :]
        m[nm] = np.ascontiguousarray(a)
    return m


def kernel(**inputs):
    nc = build_nc(debug=False)
    in_maps = [core_inputs(inputs, b) for b in range(8)]
    res = run_bass_kernel_spmd(nc, in_maps, core_ids=list(range(8)))
    return np.stack([np.asarray(r["out"], dtype=np.float32) for r in res.results], axis=0)


def inproj_phase(C):
    nc, S, I, W = C.nc, C.S, C.I, C.W
    PP, PB = C.PP, C.PB
    with ExitStack() as es:
        def sb(name, shape, dt):
            return es.enter_context(nc.sbuf_tensor(name + "_p%d" % S.phase, shape, dt))
        win = sb("win", [128, 8, D_IN], BF16)
        groups = [(4096, 5120), (0, 1024), (1024, 2048), (5120, 5648), (2048, 3072), (3072, 4096)]
        b_win = {g: Buf("win%d" % i) for i, g in enumerate(groups)}
        gT = sb("gT", [128, 8], F32); b_gT = Buf("gT")
        load_gT(C, gT, I["mix_pre_g"], b_gT)
        load_w_bf16(C, win, I["w_in"], 0, 8, groups, [b_win[g] for g in groups])

        def win_buf(col):
            for g in groups:
                if g[0] <= col < g[1]:
                    return b_win[g]

        xin = Rot([(sb("xin%d" % i, [128, D], F32), Buf("xin%d" % i)) for i in range(4)])
        pool = {"ss": Rot([(sb("ss%d" % i, [128, 4], F32), Buf("ss%d" % i)) for i in range(4)]),
                "junk": (sb("junk", [128, D], F32), Buf("junk")),
                "xn": (sb("xn", [128, D], F32), Buf("xn"))}
        uT2 = [sb("uT%d" % i, [128, 8, 512], BF16) for i in range(2)]
        b_uT2 = [[Buf("uT%d_%d" % (i, t)) for t in range(4)] for i in range(2)]
        bkc = Buf("convconst")
        cw4 = sb("cw4", [128, 12, 4], F32)
        cbt = sb("cbt", [128, 12], F32)
        for k in range(4):
            S.dma("sp", [(cw4[:, :, k], I["conv_w"][k, :].rearrange("(ch p) -> p ch", p=128))], w=[bkc],
                  allow_slow_non_contiguous=True, key="convc")
        S.dma("sp", [(cbt[:, :], I["conv_b"][0, :].rearrange("(ch p) -> p ch", p=128))], w=[bkc],
              allow_slow_non_contiguous=True, key="convc")
        hal = [(sb("hal%d" % i, [128, 4], F32), Buf("hal%d" % i)) for i in range(12)]
        xp_r = Rot([(sb("xp%d" % i, [128, 516], F32), Buf("xp%d" % i)) for i in range(3)])
        acc_r = Rot([(sb("acc%d" % i, [128, 512], F32), Buf("acc%d" % i)) for i in range(3)])
        stb = Rot([(sb("stb%d" % i, [128, 512], BF16), Buf("stb%d" % i)) for i in range(3)])
        stf = Rot([(sb("stf%d" % i, [128, 512], F32), Buf("stf%d" % i)) for i in range(3)])
        vst = Rot([(sb("vst%d" % i, [128, D], BF16), Buf("vst%d" % i)) for i in range(2)])
        zst = Rot([(sb("zst%d" % i, [128, D], F32), Buf("zst%d" % i)) for i in range(2)])
        dst_ = Rot([(sb("dts%d" % i, [128, 16], F32), Buf("dts%d" % i)) for i in range(2)])
        banks = Rot([(PP[i][:, h * 512:(h + 1) * 512], PB[i][h]) for i in range(2) for h in range(2)])
        ev = [0]
        pend = []

        def evac(out_ap, in_ap, r, w, scale=None):
            ev[0] += 1
            if scale is not None or ev[0] % 2 == 0:
                S.op("act", lambda e: e.activation(out=out_ap, in_=in_ap, func=AF.Copy,
                                                   scale=(1.0 if scale is None else scale)), r=r, w=w)
            else:
                S.op("dve", lambda e: e.tensor_copy(out=out_ap, in_=in_ap), r=r, w=w)

        NB = SEQ // 512
        xcur = {}

        def pre_load(blk, t):
            tt = blk * 4 + t
            xi, bxi = xin.next()
            S.dma("sp", [(xi[:, :], W["h1"][tt * 128:(tt + 1) * 128, :])], w=[bxi])
            xcur[(blk, t)] = (xi, bxi)

        def pre_A(blk, t):
            xi, bxi = xcur.pop((blk, t))
            norm_A(C, pool, xi[:, :], bxi)

        def pre_B(blk, t):
            tt = blk * 4 + t
            norm_B(C, pool, gT, b_gT, uT2[blk % 2], t * 128, b_uT2[blk % 2][t], PP[2 + tt % 2], PB[2 + tt % 2])

        for t in range(4):
            pre_load(0, t)
        for t in range(4):
            pre_A(0, t)
            pre_B(0, t)
        order = []
        for i in range(12):
            order += [16 + i, i]
        order += [12, 13, 14, 15]
        for blk in range(NB):
            tok0 = blk * 512
            uT = uT2[blk % 2]
            b_uT = b_uT2[blk % 2]
            nxt = blk + 1 if blk + 1 < NB else None
            for ci, c in enumerate(order):
                if nxt is not None:
                    for t in range(4):
                        if ci == 6 * t:
                            pre_load(nxt, t)
                        if ci == 6 * t + 2:
                            pre_A(nxt, t)
                        if ci == 6 * t + 4:
                            pre_B(nxt, t)
                col0 = c * 128 if c < 16 else 4096 + (c - 16) * 128
                pa, pba = banks.next()
                for k in range(8):
                    S.op("pe", lambda e, k=k, col0=col0, pa=pa, uT=uT: e.matmul(
                        pa, lhsT=win[:, k, col0:col0 + 128], rhs=uT[:, k, :], start=(k == 0), stop=(k == 7)),
                        r=[win_buf(col0)] + b_uT, w=[pba])
                if c < 16:
                    st, b_st = stb.next()
                    evac(st[:, :], pa, [pba], [b_st], scale=(0.125 if c < 8 else 1.0))
                    dram = W["qT"] if c < 8 else W["kT"]
                    r0 = (c % 8) * 128
                    S.dma("pool", [(dram[r0:r0 + 128, tok0:tok0 + 512], st[:, :])], r=[b_st])
                else:
                    cc = c - 16
                    xp, b_xp = xp_r.next()
                    hl, b_hl = hal[cc]
                    if blk == 0:
                        S.op("dve", lambda e, xp=xp: e.memset(xp[:, 0:3], 0.0), w=[b_xp])
                    else:
                        S.op("dve", lambda e, xp=xp, hl=hl: e.tensor_copy(out=xp[:, 0:3], in_=hl[:, 0:3]),
                             r=[b_hl], w=[b_xp])
                    S.op("act", lambda e, xp=xp, pa=pa: e.activation(out=xp[:, 3:515], in_=pa, func=AF.Copy),
                         r=[pba], w=[b_xp])
                    while pend:
                        pend.pop(0)()
                    S.op("dve", lambda e, xp=xp, hl=hl: e.tensor_copy(out=hl[:, 0:3], in_=xp[:, 512:515]),
                         r=[b_xp], w=[b_hl])
                    ac, b_ac = acc_r.next()
                    S.op("dve", lambda e, xp=xp, ac=ac, cc=cc: e.tensor_scalar(
                        out=ac[:, :], in0=xp[:, 0:512], scalar1=cw4[:, cc, 0:1], scalar2=cbt[:, cc:cc + 1],
                        op0=ALU.mult, op1=ALU.add), r=[b_xp, bkc], w=[b_ac])
                    for k in range(1, 4):
                        S.op("dve", lambda e, xp=xp, ac=ac, cc=cc, k=k: e.scalar_tensor_tensor(
                            out=ac[:, :], in0=xp[:, k:k + 512], scalar=cw4[:, cc, k:k + 1], in1=ac[:, :],
                            op0=ALU.mult, op1=ALU.add), r=[b_xp, bkc, b_ac], w=[b_ac])
                    def fin(ac=ac, b_ac=b_ac, cc=cc, tok0=tok0):
                        st, b_st = stf.next()
                        S.op("act", lambda e: e.activation(out=st[:, :], in_=ac[:, :], func=AF.Silu),
                             r=[b_ac], w=[b_st])
                        r0 = cc * 128
                        S.dma("pool", [(W["xbcT"][r0:r0 + 128, tok0:tok0 + 512], st[:, :])], r=[b_st])
                    pend.append(fin)
            while pend:
                pend.pop(0)()
            for t in range(4):
                tt = blk * 4 + t
                vs, b_vs = vst.next()
                zs, b_zs = zst.next()
                for n in range(4):
                    col0 = 2048 + n * 512
                    pa, pba = banks.next()
                    for k in range(8):
                        S.op("pe", lambda e, k=k, col0=col0, pa=pa, t=t, uT=uT: e.matmul(
                            pa, lhsT=uT[:, k, t * 128:(t + 1) * 128], rhs=win[:, k, col0:col0 + 512],
                            start=(k == 0), stop=(k == 7)),
                            r=[win_buf(col0), b_uT[t]], w=[pba])
                    if n < 2:
                        evac(vs[:, n * 512:(n + 1) * 512], pa, [pba], [b_vs])
                    else:
                        S.op("act", lambda e, zs=zs, n=n, pa=pa: e.activation(
                            out=zs[:, (n - 2) * 512:(n - 1) * 512], in_=pa, func=AF.Silu), r=[pba], w=[b_zs])
                S.dma("pool", [(W["v"][tt * 128:(tt + 1) * 128, :], vs[:, :])], r=[b_vs])
                S.dma("pool", [(W["z"][tt * 128:(tt + 1) * 128, :], zs[:, :])], r=[b_zs])
                pa, pba = banks.next()
                for k in range(8):
                    S.op("pe", lambda e, k=k, pa=pa, t=t, uT=uT: e.matmul(
                        pa[:, 0:16], lhsT=uT[:, k, t * 128:(t + 1) * 128], rhs=win[:, k, 5632:5648],
                        start=(k == 0), stop=(k == 7)),
                        r=[win_buf(5632), b_uT[t]], w=[pba])
                ds, b_ds = dst_.next()
                evac(ds[:, :], pa[:, 0:16], [pba], [b_ds])
                S.dma("pool", [(W["dtr"][tt * 128:(tt + 1) * 128, :], ds[:, :])], r=[b_ds])
        S.deferred_flush()


def attn_phase(C, side=None):
    nc, S, W = C.nc, C.S, C.W
    PP, PB = C.PP, C.PB
    with ExitStack() as es:
        def sb(name, shape, dt):
            return es.enter_context(nc.sbuf_tensor(name + "_p%d" % S.phase, shape, dt))
        hk = Rot([(sb("hk%d" % i, [64, SEQ], BF16), Buf("hk%d" % i)) for i in range(2)])
        hq = Rot([(sb("hq%d" % i, [64, SEQ], BF16), Buf("hq%d" % i)) for i in range(2)])
        hv = Rot([(sb("hv%d" % i, [128, NT, 64], BF16), Buf("hv%d" % i)) for i in range(2)])
        ebuf = Rot([(sb("eb%d" % i, [128, 1024], F32), Buf("eb%d" % i)) for i in range(2)])
        spb = Rot([(sb("sp%d" % i, [128, 1024], BF16), Buf("sp%d" % i)) for i in range(4)])
        abuf = Rot([(sb("ab%d" % i, [128, 1024], BF16), Buf("ab%d" % i)) for i in range(3)])
        Rb = [(sb("R%d" % i, [128, 512], BF16), Buf("R%d" % i)) for i in range(2)]
        ost = Rot([(sb("ost%d" % i, [64, 512], F32), Buf("ost%d" % i)) for i in range(2)])
        Aregs = Rot([(PP[i], PB[i]) for i in range(3)])
        zb, b_zb = sb("zb", [128, 512], BF16), Buf("zb")
        S.op("dve", lambda e: e.memset(zb[:, :], 0.0), w=[b_zb])
        Obanks = Rot([(PP[3][:, h * 512:(h + 1) * 512], PB[3][h]) for h in range(1 if side is not None else 2)])

        class T:
            pass

        HD = {}

        def issue_loads(h):
            k_t, b_k = hk.next()
            q_t, b_q = hq.next()
            v_t, b_v = hv.next()
            S.dma("sp", [(k_t[:, :], W["kT"][h * 64:(h + 1) * 64, :])], w=[b_k])
            S.dma("sp", [(q_t[:, :], W["qT"][h * 64:(h + 1) * 64, :])], w=[b_q])
            S.dma("sp", [(v_t[:, :, :], W["v"][:, h * 64:(h + 1) * 64].rearrange("(kb p) d -> p kb d", p=128))],
                  w=[b_v])
            HD[h] = (k_t, b_k, q_t, b_q, v_t, b_v)

        units = []
        for h in range(16):
            uidx = 0
            for G in range(8):
                rcur = 0
                nkb = 4 * G + 4
                chain = []
                for idx, kb in enumerate(range(nkb - 1, -1, -1)):
                    t = T()
                    t.h, t.G, t.kb = h, G, kb
                    t.first = (idx == 0)
                    t.last = (kb == 0)
                    i = kb - 4 * G
                    t.c0 = 128 * i if i >= 0 else 0
                    t.diag = (i >= 0)
                    t.Rin = Rb[rcur]
                    t.Rout = Rb[1 - rcur]
                    rcur = 1 - rcur
                    chain.append(t)
                j = 0
                while j < len(chain):
                    u = T()
                    if chain[j].diag:
                        u.tiles = [chain[j]]
                        j += 1
                    else:
                        u.tiles = [chain[j], chain[j + 1]]
                        j += 2
                    for o, t in enumerate(u.tiles):
                        t.off = 512 * o
                    u.h, u.G = h, G
                    u.uidx = uidx
                    uidx += 1
                    u.lo = u.tiles[0].c0
                    u.hi = 512 * len(u.tiles)
                    units.append(u)

        OB = {}

        def stage1(u):
            if u.uidx == 0 and u.h == 0:
                issue_loads(0)
            if u.uidx == 3 and u.h + 1 < 16:
                issue_loads(u.h + 1)
            u.k_t, u.b_k, u.q_t, u.b_q, u.v_t, u.b_v = HD[u.h]
            if u.tiles[0].first:
                OB[(u.h, u.G)] = Obanks.next()
            u.o_ps, u.b_o = OB[(u.h, u.G)]
            u.A, u.b_A = Aregs.next()
            A = u.A
            for t in u.tiles:
                for _ in range(N_WARM):
                    S.op("pe", lambda e, t=t: e.matmul(A[:, t.off:t.off + 512], lhsT=C.onesB[:, :], rhs=zb[:, :],
                                                       start=True, stop=True),
                         r=[C.bc, b_zb], w=[u.b_A[t.off // 512]])
            for t in u.tiles:
                S.op("pe", lambda e, t=t: e.matmul(
                    A[:, t.off + t.c0:t.off + 512], lhsT=u.k_t[:, t.kb * 128:(t.kb + 1) * 128],
                    rhs=u.q_t[:, t.G * 512 + t.c0:(t.G + 1) * 512], start=True, stop=True),
                    r=[u.b_k, u.b_q], w=[u.b_A[t.off // 512]])
            nb = len(u.tiles)
            u.e, u.b_e = ebuf.next()
            u.sp, u.b_sp = spb.next()
            e_, sp_ = u.e, u.sp
            lo, hi = u.lo, u.hi
            S.op("act", lambda e: e.activation(out=e_[:, lo:hi], in_=A[:, lo:hi], func=AF.Exp),
                 r=u.b_A[0:nb], w=[u.b_e])
            S.op("act", lambda e: e.activation(out=sp_[:, lo:hi], in_=e_[:, lo:hi], func=AF.Ln, bias=1.0),
                 r=[u.b_e], w=[u.b_sp])
            t0 = u.tiles[0]
            if t0.diag:
                c0 = t0.c0
                S.op("dve", lambda e: e.tensor_tensor(out=sp_[:, c0:c0 + 128], in0=sp_[:, c0:c0 + 128],
                                                      in1=C.mask01[:, :], op=ALU.mult),
                     r=[u.b_sp, C.bc], w=[u.b_sp])

        def stage2(u):
            A, sp_ = u.A, u.sp
            for t in u.tiles:
                c0, off = t.c0, t.off
                bA = u.b_A[off // 512]
                Rin, b_Rin = t.Rin
                Rout, b_Rout = t.Rout
                S.op("pe", lambda e, c0=c0, off=off: e.matmul(
                    A[:, off + c0:off + 512], lhsT=C.triNegB[:, :], rhs=sp_[:, off + c0:off + 512],
                    start=False, stop=True, skip_group_check=True),
                    r=[u.b_sp, C.bc], w=[bA])
                if not t.first:
                    S.op("pe", lambda e, c0=c0, off=off, Rin=Rin: e.matmul(
                        A[:, off + c0:off + 512], lhsT=C.onesNegB[:, :], rhs=Rin[:, c0:512],
                        start=False, stop=True, skip_group_check=True),
                        r=[b_Rin, C.bc], w=[bA])
                if t.diag:
                    S.op("pe", lambda e, c0=c0, off=off: e.matmul(
                        A[:, off + c0:off + c0 + 128], lhsT=C.identB[:, :], rhs=C.negMaskB[:, :],
                        start=False, stop=True, skip_group_check=True),
                        r=[C.bc], w=[bA])
                if not t.last:
                    if t.first:
                        S.op("dve", lambda e, c0=c0, Rout=Rout: e.memset(Rout[:, 0:c0], 0.0), w=[b_Rout])
                        S.op("dve", lambda e, c0=c0, off=off, Rout=Rout: e.tensor_copy(
                            out=Rout[:, c0:512], in_=sp_[:, off + c0:off + 512]), r=[u.b_sp], w=[b_Rout])
                    else:
                        if c0 > 0:
                            S.op("dve", lambda e, c0=c0, Rout=Rout: e.memset(Rout[:, 0:c0], 0.0), w=[b_Rout])
                        S.op("dve", lambda e, c0=c0, off=off, Rin=Rin, Rout=Rout: e.tensor_tensor(
                            out=Rout[:, c0:512], in0=Rin[:, c0:512], in1=sp_[:, off + c0:off + 512], op=ALU.add),
                            r=[b_Rin, u.b_sp], w=[b_Rout])

        def stage3(u):
            u.a, u.b_a = abuf.next()
            a_, A = u.a, u.A
            lo, hi = u.lo, u.hi
            S.op("act", lambda e: e.activation(out=a_[:, lo:hi], in_=A[:, lo:hi], func=AF.Exp),
                 r=u.b_A[0:len(u.tiles)], w=[u.b_a])

        def stage4(u):
            a_ = u.a
            for t in u.tiles:
                c0, off = t.c0, t.off
                S.op("pe", lambda e, t=t, c0=c0, off=off: e.matmul(
                    u.o_ps[0:64, c0:512], lhsT=u.v_t[:, t.kb, :], rhs=a_[:, off + c0:off + 512],
                    start=t.first, stop=t.last, skip_group_check=True),
                    r=[u.b_a, u.b_v], w=[u.b_o])
                if t.last:
                    o_s, b_os = ost.next()
                    S.op("dve", lambda e, o_s=o_s: e.tensor_copy(out=o_s[:, :], in_=u.o_ps[0:64, :]),
                         r=[u.b_o], w=[b_os])
                    S.dma("pool", [(W["oattT"][t.h * 64:(t.h + 1) * 64, t.G * 512:(t.G + 1) * 512], o_s[:, :])],
                          r=[b_os])

        n = len(units)
        per_unit = 0
        if side is not None:
            S.defer = True
            for r_ in side:
                if r_ == "done":
                    break
            S.defer = False
            per_unit = -(-len(S.deferq) // max(1, n - 40))
        for i in range(n + 2):
            if i < n:
                stage1(units[i])
            if 0 <= i - 1 < n:
                stage2(units[i - 1])
                stage3(units[i - 1])
            if 0 <= i - 2 < n:
                stage4(units[i - 2])
            if side is not None:
                S.replay(per_unit)
        if side is not None:
            S.replay(10 ** 9)
        S.deferred_flush()
        if side is not None:
            for _ in side:
                pass


def ssd_phase(C):
    nc, S, I, W = C.nc, C.S, C.I, C.W
    PP, PB = C.PP, C.PB
    with ExitStack() as es:
        def sb(name, shape, dt):
            return es.enter_context(nc.sbuf_tensor(name + "_p%d" % S.phase, shape, dt))
        bk = Buf("ssdconst")
        dtb = sb("dtb", [128, 16], F32)
        abc = sb("abc", [128, 16], F32)
        dsk = sb("dsk", [128, 16], F32)
        gssd = sb("gssd", [128, D], F32)
        nm4 = sb("nm4", [128, 4, 128], BF16)
        S.dma("sp", [(dtb[:, :], I["dt_bias"][0:1, :].partition_broadcast(128))], w=[bk], key="ssdc")
        S.dma("sp", [(abc[:, :], I["a_log"][0:1, :].partition_broadcast(128))], w=[bk], key="ssdc")
        S.dma("sp", [(dsk[:, :], I["d_skip"][0:1, :].partition_broadcast(128))], w=[bk], key="ssdc")
        S.dma("sp", [(gssd[:, :], I["ssd_norm_g"][0:1, :].partition_broadcast(128))], w=[bk], key="ssdc")
        S.op("act", lambda e: e.activation(out=abc[:, :], in_=abc[:, :], func=AF.Exp), r=[bk], w=[bk])
        S.op("dve", lambda e: e.tensor_scalar(out=abc[:, :], in0=abc[:, :], scalar1=-1.0, scalar2=None,
                                              op0=ALU.mult), r=[bk], w=[bk])
        for i in range(4):
            S.op("dve", lambda e, i=i: e.tensor_copy(out=nm4[:, i, :], in_=C.negMaskSB[:, :]), r=[C.bc], w=[bk])

        def rot2(name, shape, dt, n=2):
            return Rot([(sb("%s%d" % (name, i), shape, dt), Buf("%s%d" % (name, i))) for i in range(n)])

        cv_r = rot2("cv", [128, 12, 128], F32, 3)
        dtr_t = rot2("dtr", [128, 16], F32)
        z_t = rot2("zt", [128, D], F32, 3)
        cvb_r = rot2("cvb", [128, 4, 128], BF16)
        xs_r = rot2("xs_tok", [128, D], F32)
        Bt_r = rot2("B_tok", [128, 256], BF16)
        sm_r = rot2("sm", [128, 8, 16], F32)
        edt_r = rot2("edt", [128, 48], F32)
        xbf_r = rot2("x_bf", [128, D], BF16)
        xdec_r = rot2("xdec", [128, D], BF16)
        y1_r = rot2("y1", [128, D], F32)
        rhs1_r = rot2("rhs1", [128, 2, 16, 128], BF16)
        smb, b_smb = sb("smb", [128, 2, 16], BF16), Buf("smb")
        cbs_r = rot2("cbs", [128, 256], F32)
        Mt = rot2("Mt", [128, 4, 128], BF16, 4)

        def rhs1_of(st):
            return st.rhs1

        def pcb_of(st):
            return st.cbs
        lm = rot2("lm", [128, 512], F32)
        yoffs, b_yoffs = sb("yoffs", [128, D], F32), Buf("yoffs")
        hstate, b_hst = sb("hstate", [128, D], F32), Buf("hstate")
        hsb, b_hsb = sb("hsb", [128, D], BF16), Buf("hsb")
        junk, b_junk = sb("junk", [128, D], F32), Buf("junk")
        ss_r = rot2("ss", [128, 8], F32)
        ot = rot2("ot", [128, 8, 128], BF16)
        Dbanks = Rot([(PP[2][:, h * 512:(h + 1) * 512], PB[2][h]) for h in range(2)])
        b_small = b_cb = PB[3][0]
        psm = PP[3][:, 0:48]
        pcb = PP[3][:, 128:384]
        pBt = PP[3][:, 512:768]
        xbc_v = W["xbcT"].rearrange("(ch p) t -> p ch t", p=128)
        ossd_v = W["ossdT"].rearrange("(k p) t -> p k t", p=128)

        def bc16(ap16, n0, n):
            return ap16[:, n0:n0 + n].unsqueeze(2).to_broadcast([128, n, 64])

        def h16(ap):
            return ap.rearrange("p (h d) -> p h d", d=64)

        ball = Buf("ssd_all")
        dtr_all = sb("dtr_all", [128, NT, 16], F32)
        dt_all = sb("dt_all", [128, NT, 16], F32)
        adt_all = sb("adt_all", [128, NT, 16], F32)
        adt_hi = sb("adt_hi", [128, NT, 16], BF16)
        adt_lo = sb("adt_lo", [128, NT, 16], F32)
        edt_all = sb("edt_all", [128, 3, NT, 16], F32)
        ddec_all = sb("ddec_all", [128, NT, 16], F32)
        S.dma("sp", [(dtr_all[:, :, :], W["dtr"].rearrange("(c p) h -> p c h", p=128))], w=[ball])

        def bcc(ap16):
            return ap16[:, :].unsqueeze(1).to_broadcast([128, NT, 16])

        S.op("dve", lambda e: e.tensor_tensor(out=dtr_all[:, :, :], in0=dtr_all[:, :, :], in1=bcc(dtb), op=ALU.add),
             r=[ball, bk], w=[ball])
        S.op("act", lambda e: e.activation(out=dtr_all[:, :, :], in_=dtr_all[:, :, :], func=AF.Exp),
             r=[ball], w=[ball])
        S.op("act", lambda e: e.activation(out=dt_all[:, :, :], in_=dtr_all[:, :, :], func=AF.Ln, bias=1.0),
             r=[ball], w=[ball])
        S.op("dve", lambda e: e.tensor_tensor(out=adt_all[:, :, :], in0=dt_all[:, :, :], in1=bcc(abc), op=ALU.mult),
             r=[ball, bk], w=[ball])
        S.op("dve", lambda e: e.tensor_copy(out=adt_hi[:, :, :], in_=adt_all[:, :, :]), r=[ball], w=[ball])
        S.op("dve", lambda e: e.tensor_tensor(out=adt_lo[:, :, :], in0=adt_all[:, :, :], in1=adt_hi[:, :, :],
                                              op=ALU.subtract), r=[ball], w=[ball])
        for i, lt in enumerate((C.triInclF, C.triUpF, C.onesF)):
            pq = PP[i // 2][:, (i % 2) * 512:(i % 2 + 1) * 512]
            S.op("pe", lambda e, lt=lt, pq=pq: e.matmul(pq, lhsT=lt[:, :],
                                                        rhs=adt_all[:, :, :].rearrange("p c h -> p (c h)"),
                                                        start=True, stop=True),
                 r=[ball, C.bc], w=[PB[i // 2][i % 2]])
            S.op("act", lambda e, i=i, pq=pq: e.activation(out=edt_all[:, i, :, :].rearrange("p c h -> p (c h)"),
                                                           in_=pq, func=AF.Exp),
                 r=[PB[i // 2][i % 2]], w=[ball])
        S.op("dve", lambda e: e.tensor_tensor(out=ddec_all[:, :, :], in0=dt_all[:, :, :], in1=edt_all[:, 1, :, :],
                                              op=ALU.mult), r=[ball], w=[ball])

        CH = {}

        class St:
            pass

        def front(c):
            st = St()
            CH[c] = st
            tok0 = c * 128
            st.rhs1, st.b_rhs1 = rhs1_r.next()
            rhs1, b_rhs1 = st.rhs1, st.b_rhs1
            for hl, src_ap, eng in ((0, adt_hi[:, c, :], "pool"), (1, adt_lo[:, c, :], "dve")):
                S.op(eng, lambda e, hl=hl, src_ap=src_ap: e.tensor_tensor(
                    out=rhs1[:, hl, :, :], in0=C.triInclF[:, :].unsqueeze(1).to_broadcast([128, 16, 128]),
                    in1=src_ap.unsqueeze(2).to_broadcast([128, 16, 128]), op=ALU.mult),
                    r=[ball, C.bc], w=[b_rhs1])
            cv, b_cv = cv_r.next()
            S.dma("sp", [(cv[:, :, :], xbc_v[:, :, tok0:tok0 + 128])], w=[b_cv])
            st.zt, st.b_zt = z_t.next()
            zt, b_zt = st.zt, st.b_zt
            S.dma("sp", [(zt[:, :], W["z"][tok0:tok0 + 128, :])], w=[b_zt])
            st.cvb, st.b_cvb = cvb_r.next()
            cvb, b_cvb = st.cvb, st.b_cvb
            S.op("dve", lambda e: e.tensor_copy(out=cvb[:, :, :], in_=cv[:, 8:12, :]), r=[b_cv], w=[b_cvb])
            for k in range(8):
                S.op("pe", lambda e, k=k: e.transpose(out=PP[0][:, k * 128:(k + 1) * 128], in_=cv[:, k, :],
                                                      identity=C.identF[:, :]),
                     r=[b_cv, C.bc], w=[PB[0][k // 4]])
            for g in range(2):
                S.op("pe", lambda e, g=g: e.transpose(out=pBt[:, g * 128:(g + 1) * 128], in_=cv[:, 8 + g, :],
                                                      identity=C.identF[:, :]),
                     r=[b_cv, C.bc], w=[PB[3][1]])
            st.xs, st.b_xs = xs_r.next()
            xs_tok, b_xs = st.xs, st.b_xs
            S.op("act", lambda e: e.activation(out=xs_tok[:, :], in_=PP[0][:, :], func=AF.Copy),
                 r=PB[0], w=[b_xs])
            st.Bt, st.b_Bt = Bt_r.next()
            B_tok, b_Bt = st.Bt, st.b_Bt
            S.op("dve", lambda e: e.tensor_copy(out=B_tok[:, :], in_=pBt), r=[PB[3][1]], w=[b_Bt])
            x_bf, b_xbf = xbf_r.next()
            S.op("pool", lambda e: e.tensor_tensor(out=h16(x_bf[:, :]), in0=h16(xs_tok[:, :]),
                                                   in1=bc16(dt_all[:, c, :], 0, 16), op=ALU.mult),
                 r=[b_xs, ball], w=[b_xbf])
            st.xdec, st.b_xdec = xdec_r.next()
            xdec, b_xdec = st.xdec, st.b_xdec
            S.op("dve", lambda e: e.tensor_tensor(out=h16(xdec[:, :]), in0=h16(xs_tok[:, :]),
                                                   in1=bc16(ddec_all[:, c, :], 0, 16), op=ALU.mult),
                 r=[b_xs, ball], w=[b_xdec])
            for g in range(2):
                S.op("pe", lambda e, g=g: e.matmul(pcb[:, g * 128:(g + 1) * 128], lhsT=cvb[:, g, :],
                                                   rhs=cvb[:, 2 + g, :], start=True, stop=True),
                     r=[b_cvb], w=[b_cb])
            st.cbs, st.b_cbs = cbs_r.next()
            cbs, b_cbs = st.cbs, st.b_cbs
            S.op("act", lambda e: e.activation(out=cbs[:, :], in_=pcb, func=AF.Copy), r=[b_cb], w=[b_cbs])
            st.M = {}
            st.x_bf, st.b_xbf = x_bf, b_xbf

        def QD(c, qd):
            st = CH[c]
            g = qd // 2
            dps, b_d = Dbanks.next()
            for hl in range(2):
                S.op("pe", lambda e, hl=hl: e.matmul(
                    dps, lhsT=C.triUpB[:, :], rhs=rhs1_of(st)[:, hl, qd * 4:(qd + 1) * 4, :],
                    start=(hl == 0), stop=False),
                    r=[st.b_rhs1, C.bc], w=[b_d])
            S.op("pe", lambda e: e.matmul(dps, lhsT=C.identB[:, :], rhs=nm4[:, :, :], start=False, stop=True),
                 r=[bk, C.bc], w=[b_d])
            lm_t, b_lm = lm.next()
            S.op("act", lambda e: e.activation(out=lm_t[:, :], in_=dps, func=AF.Exp), r=[b_d], w=[b_lm])
            M_t, b_M = Mt.next()
            S.op("dve", lambda e: e.tensor_tensor(
                out=M_t[:, :, :], in0=lm_t[:, :].rearrange("p (h l) -> p h l", h=4),
                in1=pcb_of(st)[:, g * 128:(g + 1) * 128].unsqueeze(1).to_broadcast([128, 4, 128]), op=ALU.mult),
                r=[b_lm, st.b_cbs], w=[b_M])
            st.M[qd] = (M_t, b_M)

        def QY(c, qd):
            st = CH[c]
            M_t, b_M = st.M[qd]
            for hh in range(4):
                h = qd * 4 + hh
                S.op("pe", lambda e, hh=hh, h=h: e.matmul(
                    PP[1][:, h * 64:(h + 1) * 64], lhsT=M_t[:, hh, :], rhs=st.x_bf[:, h * 64:(h + 1) * 64],
                    start=True, stop=True),
                    r=[b_M, st.b_xbf], w=[PB[1][h // 8]])

        def F3(c):
            st = CH[c]
            xs_tok, b_xs = st.xs, st.b_xs
            st.y1, st.b_y1 = y1_r.next()
            y1, b_y1 = st.y1, st.b_y1
            S.op("pool", lambda e: e.tensor_tensor(out=h16(y1[:, :]), in0=h16(xs_tok[:, :]),
                                                   in1=bc16(dsk, 0, 16), op=ALU.mult),
                 r=[b_xs, bk], w=[b_y1])
            S.op("dve", lambda e: e.tensor_tensor(out=y1[:, :], in0=y1[:, :], in1=PP[1][:, :], op=ALU.add),
                 r=[b_y1] + PB[1], w=[b_y1])

        def mid(c):
            st = CH[c]
            cvb, b_cvb = st.cvb, st.b_cvb
            if c > 0:
                for g in range(2):
                    yps, b_yp = Dbanks.next()
                    S.op("pe", lambda e, g=g, yps=yps: e.matmul(yps, lhsT=cvb[:, 2 + g, :],
                                                                rhs=hsb[:, g * 512:(g + 1) * 512],
                                                                start=True, stop=True),
                         r=[b_cvb, b_hsb], w=[b_yp])
                    S.op("dve", lambda e, g=g, yps=yps: e.tensor_tensor(
                        out=h16(yoffs[:, g * 512:(g + 1) * 512]), in0=h16(yps), in1=bc16(edt_all[:, 0, c, :], 8 * g, 8),
                        op=ALU.mult),
                        r=[b_yp, ball], w=[b_yoffs])
            if c < NT - 1:
                for g in range(2):
                    sps, b_sp = Dbanks.next()
                    S.op("pe", lambda e, g=g, sps=sps: e.matmul(sps, lhsT=st.Bt[:, g * 128:(g + 1) * 128],
                                                                rhs=st.xdec[:, g * 512:(g + 1) * 512],
                                                                start=True, stop=True),
                         r=[st.b_Bt, st.b_xdec], w=[b_sp])
                    hs_g = hstate[:, g * 512:(g + 1) * 512]
                    if c == 0:
                        S.op("dve", lambda e, hs_g=hs_g, sps=sps: e.tensor_copy(out=hs_g, in_=sps),
                             r=[b_sp], w=[b_hst])
                    else:
                        S.op("dve", lambda e, hs_g=hs_g, g=g: e.tensor_tensor(
                            out=h16(hs_g), in0=h16(hs_g), in1=bc16(edt_all[:, 2, c, :], 8 * g, 8), op=ALU.mult),
                            r=[b_hst, ball], w=[b_hst])
                        S.op("dve", lambda e, hs_g=hs_g, sps=sps: e.tensor_tensor(out=hs_g, in0=hs_g, in1=sps,
                                                                                   op=ALU.add),
                             r=[b_hst, b_sp], w=[b_hst])
                    S.op("act", lambda e, hs_g=hs_g, g=g: e.activation(out=hsb[:, g * 512:(g + 1) * 512], in_=hs_g,
                                                                      func=AF.Copy),
                         r=[b_hst], w=[b_hsb])

        def backA(c):
            st = CH[c]
            tok0 = c * 128
            y1, b_y1, zt, b_zt = st.y1, st.b_y1, st.zt, st.b_zt
            ss, b_ss = ss_r.next()
            if c > 0:
                S.op("dve", lambda e: e.tensor_tensor(out=y1[:, :], in0=y1[:, :], in1=yoffs[:, :], op=ALU.add),
                     r=[b_y1, b_yoffs], w=[b_y1])
            S.op("pool", lambda e: e.tensor_tensor(out=y1[:, :], in0=y1[:, :], in1=zt[:, :], op=ALU.mult),
                 r=[b_y1, b_zt], w=[b_y1])
            for g in range(2):
                S.op("act", lambda e, g=g: e.activation(out=junk[:, g * 512:(g + 1) * 512],
                                                        in_=y1[:, g * 512:(g + 1) * 512], func=AF.Square,
                                                        accum_out=ss[:, g:g + 1]),
                     r=[b_y1], w=[b_junk, b_ss])
            S.op("act", lambda e: e.activation(out=ss[:, 2:4], in_=ss[:, 0:2], func=AF.Ln, scale=1.0 / 512, bias=EPS),
                 r=[b_ss], w=[b_ss])
            S.op("act", lambda e: e.activation(out=ss[:, 4:6], in_=ss[:, 2:4], func=AF.Exp, scale=-0.5),
                 r=[b_ss], w=[b_ss])
            st.ss, st.b_ss = ss, b_ss

        def backB(c):
            st = CH[c]
            tok0 = c * 128
            y1, b_y1, ss, b_ss = st.y1, st.b_y1, st.ss, st.b_ss
            for g in range(2):
                S.op("dve", lambda e, g=g: e.scalar_tensor_tensor(
                    out=y1[:, g * 512:(g + 1) * 512], in0=y1[:, g * 512:(g + 1) * 512], scalar=ss[:, 4 + g:5 + g],
                    in1=gssd[:, g * 512:(g + 1) * 512], op0=ALU.mult, op1=ALU.mult),
                    r=[b_y1, b_ss, bk], w=[b_y1])
            for k in range(8):
                S.op("pe", lambda e, k=k: e.transpose(out=PP[0][:, k * 128:(k + 1) * 128],
                                                      in_=y1[:, k * 128:(k + 1) * 128], identity=C.identF[:, :]),
                     r=[b_y1, C.bc], w=[PB[0][k // 4]])
            o_t, b_ot = ot.next()
            S.op("act", lambda e: e.activation(out=o_t[:, :, :],
                                               in_=PP[0][:, :].rearrange("p (k t) -> p k t", k=8), func=AF.Copy),
                 r=PB[0], w=[b_ot])
            S.dma("act", [(ossd_v[:, :, tok0:tok0 + 128], o_t[:, :, :])], r=[b_ot])
            del CH[c]

        def whole_front(c):
            front(c)
            for qd in range(4):
                QD(c, qd)
                QY(c, qd)
            F3(c)

        whole_front(0)
        for c in range(NT):
            n = c + 1
            if n < NT:
                front(n)
                QD(n, 0)
            mid(c)
            if n < NT:
                QD(n, 1)
                QY(n, 0)
            backA(c)
            if n < NT:
                QD(n, 2)
                QY(n, 1)
            backB(c)
            if n < NT:
                QD(n, 3)
                QY(n, 2)
                QY(n, 3)
                F3(n)
        S.deferred_flush()


def ssd_steps(C):
    nc, S, I, W = C.nc, C.S, C.I, C.W
    PP, PB = C.PP, C.PB
    with ExitStack() as es:
        def sb(name, shape, dt):
            return es.enter_context(nc.sbuf_tensor(name + "_p%d" % S.phase, shape, dt))
        bk = Buf("ssdconst")
        dtb = sb("dtb", [128, 16], F32)
        abc = sb("abc", [128, 16], F32)
        dsk = sb("dsk", [128, 16], F32)
        gssd = sb("gssd", [128, D], F32)
        nm4 = sb("nm4", [128, 4, 128], BF16)
        S.dma("sp", [(dtb[:, :], I["dt_bias"][0:1, :].partition_broadcast(128))], w=[bk], key="ssdc")
        S.dma("sp", [(abc[:, :], I["a_log"][0:1, :].partition_broadcast(128))], w=[bk], key="ssdc")
        S.dma("sp", [(dsk[:, :], I["d_skip"][0:1, :].partition_broadcast(128))], w=[bk], key="ssdc")
        S.dma("sp", [(gssd[:, :], I["ssd_norm_g"][0:1, :].partition_broadcast(128))], w=[bk], key="ssdc")
        S.op("act", lambda e: e.activation(out=abc[:, :], in_=abc[:, :], func=AF.Exp), r=[bk], w=[bk])
        S.op("dve", lambda e: e.tensor_scalar(out=abc[:, :], in0=abc[:, :], scalar1=-1.0, scalar2=None,
                                              op0=ALU.mult), r=[bk], w=[bk])
        for i in range(4):
            S.op("dve", lambda e, i=i: e.tensor_copy(out=nm4[:, i, :], in_=C.negMaskSB[:, :]), r=[C.bc], w=[bk])

        def rot2(name, shape, dt, n=2):
            return Rot([(sb("%s%d" % (name, i), shape, dt), Buf("%s%d" % (name, i))) for i in range(n)])

        cv_r = rot2("cv", [128, 12, 128], F32, 3)
        dtr_t = rot2("dtr", [128, 16], F32)
        z_t = rot2("zt", [128, D], F32, 3)
        cvb_r = rot2("cvb", [128, 4, 128], BF16)
        xs_r = rot2("xs_tok", [128, D], F32)
        Bt_r = rot2("B_tok", [128, 256], BF16)
        sm_r = rot2("sm", [128, 8, 16], F32)
        edt_r = rot2("edt", [128, 48], F32)
        xbf_r = rot2("x_bf", [128, D], BF16)
        xdec_r = rot2("xdec", [128, D], BF16)
        y1_r = rot2("y1", [128, D], F32)
        rhs1_r = rot2("rhs1", [128, 2, 16, 128], BF16)
        smb, b_smb = sb("smb", [128, 2, 16], BF16), Buf("smb")
        cbs_r = rot2("cbs", [128, 256], F32)
        Mt = rot2("Mt", [128, 4, 128], BF16, 4)

        def rhs1_of(st):
            return st.rhs1

        def pcb_of(st):
            return st.cbs
        lm = rot2("lm", [128, 512], F32)
        yoffs, b_yoffs = sb("yoffs", [128, D], F32), Buf("yoffs")
        hstate, b_hst = sb("hstate", [128, D], F32), Buf("hstate")
        hsb, b_hsb = sb("hsb", [128, D], BF16), Buf("hsb")
        junk, b_junk = sb("junk", [128, D], F32), Buf("junk")
        ss_r = rot2("ss", [128, 8], F32)
        ot = rot2("ot", [128, 8, 128], BF16)
        SBK, bb = PP[3][:, 512:1024], PB[3][1]
        Dbanks = Rot([(SBK, bb)])
        b_cb = bb
        pcb = SBK[:, 0:256]
        pBt = SBK[:, 256:512]
        xbc_v = W["xbcT"].rearrange("(ch p) t -> p ch t", p=128)
        ossd_v = W["ossdT"].rearrange("(k p) t -> p k t", p=128)

        def bc16(ap16, n0, n):
            return ap16[:, n0:n0 + n].unsqueeze(2).to_broadcast([128, n, 64])

        def h16(ap):
            return ap.rearrange("p (h d) -> p h d", d=64)

        ball = Buf("ssd_all")
        dtr_all = sb("dtr_all", [128, NT, 16], F32)
        dt_all = sb("dt_all", [128, NT, 16], F32)
        adt_all = sb("adt_all", [128, NT, 16], F32)
        adt_hi = sb("adt_hi", [128, NT, 16], BF16)
        adt_lo = sb("adt_lo", [128, NT, 16], F32)
        edt_all = sb("edt_all", [128, 3, NT, 16], F32)
        ddec_all = sb("ddec_all", [128, NT, 16], F32)
        S.dma("sp", [(dtr_all[:, :, :], W["dtr"].rearrange("(c p) h -> p c h", p=128))], w=[ball])

        def bcc(ap16):
            return ap16[:, :].unsqueeze(1).to_broadcast([128, NT, 16])

        S.op("dve", lambda e: e.tensor_tensor(out=dtr_all[:, :, :], in0=dtr_all[:, :, :], in1=bcc(dtb), op=ALU.add),
             r=[ball, bk], w=[ball])
        S.op("act", lambda e: e.activation(out=dtr_all[:, :, :], in_=dtr_all[:, :, :], func=AF.Exp),
             r=[ball], w=[ball])
        S.op("act", lambda e: e.activation(out=dt_all[:, :, :], in_=dtr_all[:, :, :], func=AF.Ln, bias=1.0),
             r=[ball], w=[ball])
        S.op("dve", lambda e: e.tensor_tensor(out=adt_all[:, :, :], in0=dt_all[:, :, :], in1=bcc(abc), op=ALU.mult),
             r=[ball, bk], w=[ball])
        S.op("dve", lambda e: e.tensor_copy(out=adt_hi[:, :, :], in_=adt_all[:, :, :]), r=[ball], w=[ball])
        S.op("dve", lambda e: e.tensor_tensor(out=adt_lo[:, :, :], in0=adt_all[:, :, :], in1=adt_hi[:, :, :],
                                              op=ALU.subtract), r=[ball], w=[ball])
        for i, lt in enumerate((C.triInclF, C.triUpF, C.onesF)):
            pq = SBK
            S.op("pe", lambda e, lt=lt, pq=pq: e.matmul(pq, lhsT=lt[:, :],
                                                        rhs=adt_all[:, :, :].rearrange("p c h -> p (c h)"),
                                                        start=True, stop=True),
                 r=[ball, C.bc], w=[bb])
            S.op("act", lambda e, i=i, pq=pq: e.activation(out=edt_all[:, i, :, :].rearrange("p c h -> p (c h)"),
                                                           in_=pq, func=AF.Exp),
                 r=[bb], w=[ball])
        S.op("dve", lambda e: e.tensor_tensor(out=ddec_all[:, :, :], in0=dt_all[:, :, :], in1=edt_all[:, 1, :, :],
                                              op=ALU.mult), r=[ball], w=[ball])

        CH = {}

        class St:
            pass

        def front(c):
            st = St()
            CH[c] = st
            tok0 = c * 128
            st.rhs1, st.b_rhs1 = rhs1_r.next()
            rhs1, b_rhs1 = st.rhs1, st.b_rhs1
            for hl, src_ap, eng in ((0, adt_hi[:, c, :], "pool"), (1, adt_lo[:, c, :], "dve")):
                S.op(eng, lambda e, hl=hl, src_ap=src_ap: e.tensor_tensor(
                    out=rhs1[:, hl, :, :], in0=C.triInclF[:, :].unsqueeze(1).to_broadcast([128, 16, 128]),
                    in1=src_ap.unsqueeze(2).to_broadcast([128, 16, 128]), op=ALU.mult),
                    r=[ball, C.bc], w=[b_rhs1])
            cv, b_cv = cv_r.next()
            S.dma("sp", [(cv[:, :, :], xbc_v[:, :, tok0:tok0 + 128])], w=[b_cv])
            st.zt, st.b_zt = z_t.next()
            zt, b_zt = st.zt, st.b_zt
            S.dma("sp", [(zt[:, :], W["z"][tok0:tok0 + 128, :])], w=[b_zt])
            st.cvb, st.b_cvb = cvb_r.next()
            cvb, b_cvb = st.cvb, st.b_cvb
            S.op("dve", lambda e: e.tensor_copy(out=cvb[:, :, :], in_=cv[:, 8:12, :]), r=[b_cv], w=[b_cvb])
            st.xs, st.b_xs = xs_r.next()
            xs_tok, b_xs = st.xs, st.b_xs
            for half in range(2):
                for kk in range(4):
                    k = half * 4 + kk
                    S.op("pe", lambda e, k=k, kk=kk: e.transpose(out=SBK[:, kk * 128:(kk + 1) * 128], in_=cv[:, k, :],
                                                              identity=C.identF[:, :]),
                         r=[b_cv, C.bc], w=[bb])
                S.op("dve", lambda e, half=half: e.tensor_copy(out=xs_tok[:, half * 512:(half + 1) * 512], in_=SBK),
                     r=[bb], w=[b_xs])
            for gg in range(2):
                S.op("pe", lambda e, gg=gg: e.transpose(out=pBt[:, gg * 128:(gg + 1) * 128], in_=cv[:, 8 + gg, :],
                                                        identity=C.identF[:, :]),
                     r=[b_cv, C.bc], w=[bb])
            st.Bt, st.b_Bt = Bt_r.next()
            B_tok, b_Bt = st.Bt, st.b_Bt
            S.op("dve", lambda e: e.tensor_copy(out=B_tok[:, :], in_=pBt), r=[bb], w=[b_Bt])
            x_bf, b_xbf = xbf_r.next()
            S.op("pool", lambda e: e.tensor_tensor(out=h16(x_bf[:, :]), in0=h16(xs_tok[:, :]),
                                                   in1=bc16(dt_all[:, c, :], 0, 16), op=ALU.mult),
                 r=[b_xs, ball], w=[b_xbf])
            st.xdec, st.b_xdec = xdec_r.next()
            xdec, b_xdec = st.xdec, st.b_xdec
            S.op("dve", lambda e: e.tensor_tensor(out=h16(xdec[:, :]), in0=h16(xs_tok[:, :]),
                                                   in1=bc16(ddec_all[:, c, :], 0, 16), op=ALU.mult),
                 r=[b_xs, ball], w=[b_xdec])
            for g in range(2):
                S.op("pe", lambda e, g=g: e.matmul(pcb[:, g * 128:(g + 1) * 128], lhsT=cvb[:, g, :],
                                                   rhs=cvb[:, 2 + g, :], start=True, stop=True),
                     r=[b_cvb], w=[b_cb])
            st.cbs, st.b_cbs = cbs_r.next()
            cbs, b_cbs = st.cbs, st.b_cbs
            S.op("dve", lambda e: e.tensor_copy(out=cbs[:, :], in_=pcb), r=[b_cb], w=[b_cbs])
            st.M = {}
            st.x_bf, st.b_xbf = x_bf, b_xbf

        def QD(c, qd):
            st = CH[c]
            g = qd // 2
            dps, b_d = Dbanks.next()
            for hl in range(2):
                S.op("pe", lambda e, hl=hl: e.matmul(
                    dps, lhsT=C.triUpB[:, :], rhs=rhs1_of(st)[:, hl, qd * 4:(qd + 1) * 4, :],
                    start=(hl == 0), stop=False),
                    r=[st.b_rhs1, C.bc], w=[b_d])
            S.op("pe", lambda e: e.matmul(dps, lhsT=C.identB[:, :], rhs=nm4[:, :, :], start=False, stop=True),
                 r=[bk, C.bc], w=[b_d])
            lm_t, b_lm = lm.next()
            S.op("act", lambda e: e.activation(out=lm_t[:, :], in_=dps, func=AF.Exp), r=[b_d], w=[b_lm])
            M_t, b_M = Mt.next()
            S.op("dve", lambda e: e.tensor_tensor(
                out=M_t[:, :, :], in0=lm_t[:, :].rearrange("p (h l) -> p h l", h=4),
                in1=pcb_of(st)[:, g * 128:(g + 1) * 128].unsqueeze(1).to_broadcast([128, 4, 128]), op=ALU.mult),
                r=[b_lm, st.b_cbs], w=[b_M])
            st.M[qd] = (M_t, b_M)

        def QY(c, qd):
            st = CH[c]
            M_t, b_M = st.M[qd]
            if qd == 0:
                st.y1, st.b_y1 = y1_r.next()
                S.op("pool", lambda e: e.tensor_tensor(out=h16(st.y1[:, :]), in0=h16(st.xs[:, :]),
                                                       in1=bc16(dsk, 0, 16), op=ALU.mult),
                     r=[st.b_xs, bk], w=[st.b_y1])
            for hh in range(4):
                h = qd * 4 + hh
                S.op("pe", lambda e, hh=hh, h=h: e.matmul(
                    SBK[:, hh * 64:(hh + 1) * 64], lhsT=M_t[:, hh, :], rhs=st.x_bf[:, h * 64:(h + 1) * 64],
                    start=True, stop=True),
                    r=[b_M, st.b_xbf], w=[bb])
            S.op("dve", lambda e: e.tensor_tensor(out=st.y1[:, qd * 256:(qd + 1) * 256],
                                                  in0=st.y1[:, qd * 256:(qd + 1) * 256], in1=SBK[:, 0:256], op=ALU.add),
                 r=[st.b_y1, bb], w=[st.b_y1])

        def F3(c):
            pass

        def mid(c):
            st = CH[c]
            cvb, b_cvb = st.cvb, st.b_cvb
            if c > 0:
                for g in range(2):
                    yps, b_yp = Dbanks.next()
                    S.op("pe", lambda e, g=g, yps=yps: e.matmul(yps, lhsT=cvb[:, 2 + g, :],
                                                                rhs=hsb[:, g * 512:(g + 1) * 512],
                                                                start=True, stop=True),
                         r=[b_cvb, b_hsb], w=[b_yp])
                    S.op("dve", lambda e, g=g, yps=yps: e.tensor_tensor(
                        out=h16(yoffs[:, g * 512:(g + 1) * 512]), in0=h16(yps), in1=bc16(edt_all[:, 0, c, :], 8 * g, 8),
                        op=ALU.mult),
                        r=[b_yp, ball], w=[b_yoffs])
            if c < NT - 1:
                for g in range(2):
                    sps, b_sp = Dbanks.next()
                    S.op("pe", lambda e, g=g, sps=sps: e.matmul(sps, lhsT=st.Bt[:, g * 128:(g + 1) * 128],
                                                                rhs=st.xdec[:, g * 512:(g + 1) * 512],
                                                                start=True, stop=True),
                         r=[st.b_Bt, st.b_xdec], w=[b_sp])
                    hs_g = hstate[:, g * 512:(g + 1) * 512]
                    if c == 0:
                        S.op("dve", lambda e, hs_g=hs_g, sps=sps: e.tensor_copy(out=hs_g, in_=sps),
                             r=[b_sp], w=[b_hst])
                    else:
                        S.op("dve", lambda e, hs_g=hs_g, g=g: e.tensor_tensor(
                            out=h16(hs_g), in0=h16(hs_g), in1=bc16(edt_all[:, 2, c, :], 8 * g, 8), op=ALU.mult),
                            r=[b_hst, ball], w=[b_hst])
                        S.op("dve", lambda e, hs_g=hs_g, sps=sps: e.tensor_tensor(out=hs_g, in0=hs_g, in1=sps,
                                                                                   op=ALU.add),
                             r=[b_hst, b_sp], w=[b_hst])
                    S.op("pool", lambda e, hs_g=hs_g, g=g: e.tensor_copy(out=hsb[:, g * 512:(g + 1) * 512], in_=hs_g),
                         r=[b_hst], w=[b_hsb])

        def backA(c):
            st = CH[c]
            tok0 = c * 128
            y1, b_y1, zt, b_zt = st.y1, st.b_y1, st.zt, st.b_zt
            ss, b_ss = ss_r.next()
            if c > 0:
                S.op("dve", lambda e: e.tensor_tensor(out=y1[:, :], in0=y1[:, :], in1=yoffs[:, :], op=ALU.add),
                     r=[b_y1, b_yoffs], w=[b_y1])
            S.op("pool", lambda e: e.tensor_tensor(out=y1[:, :], in0=y1[:, :], in1=zt[:, :], op=ALU.mult),
                 r=[b_y1, b_zt], w=[b_y1])
            for g in range(2):
                S.op("act", lambda e, g=g: e.activation(out=junk[:, g * 512:(g + 1) * 512],
                                                        in_=y1[:, g * 512:(g + 1) * 512], func=AF.Square,
                                                        accum_out=ss[:, g:g + 1]),
                     r=[b_y1], w=[b_junk, b_ss])
            S.op("act", lambda e: e.activation(out=ss[:, 2:4], in_=ss[:, 0:2], func=AF.Ln, scale=1.0 / 512, bias=EPS),
                 r=[b_ss], w=[b_ss])
            S.op("act", lambda e: e.activation(out=ss[:, 4:6], in_=ss[:, 2:4], func=AF.Exp, scale=-0.5),
                 r=[b_ss], w=[b_ss])
            st.ss, st.b_ss = ss, b_ss

        def backB(c):
            st = CH[c]
            tok0 = c * 128
            y1, b_y1, ss, b_ss = st.y1, st.b_y1, st.ss, st.b_ss
            for g in range(2):
                S.op("dve", lambda e, g=g: e.scalar_tensor_tensor(
                    out=y1[:, g * 512:(g + 1) * 512], in0=y1[:, g * 512:(g + 1) * 512], scalar=ss[:, 4 + g:5 + g],
                    in1=gssd[:, g * 512:(g + 1) * 512], op0=ALU.mult, op1=ALU.mult),
                    r=[b_y1, b_ss, bk], w=[b_y1])
            o_t, b_ot = ot.next()
            for half in range(2):
                for kk in range(4):
                    k = half * 4 + kk
                    S.op("pe", lambda e, k=k, kk=kk: e.transpose(out=SBK[:, kk * 128:(kk + 1) * 128],
                                                              in_=y1[:, k * 128:(k + 1) * 128], identity=C.identF[:, :]),
                         r=[b_y1, C.bc], w=[bb])
                S.op("dve", lambda e, half=half: e.tensor_copy(
                    out=o_t[:, half * 4:(half + 1) * 4, :], in_=SBK.rearrange("p (k t) -> p k t", k=4)),
                    r=[bb], w=[b_ot])
            S.dma("sp", [(ossd_v[:, :, tok0:tok0 + 128], o_t[:, :, :])], r=[b_ot])
            del CH[c]

        front(0)
        yield
        for qd in range(4):
            QD(0, qd)
            QY(0, qd)
            yield
        for c in range(NT):
            n = c + 1
            if n < NT:
                front(n)
                yield
                QD(n, 0)
                yield
            mid(c)
            yield
            if n < NT:
                QD(n, 1)
                QY(n, 0)
                yield
            backA(c)
            yield
            if n < NT:
                QD(n, 2)
                QY(n, 1)
                yield
            backB(c)
            yield
            if n < NT:
                QD(n, 3)
                QY(n, 2)
                yield
                QY(n, 3)
                yield
        yield "done"


def ssd_phase_1bank(C):
    gen = ssd_steps(C)
    for r in gen:
        if r == "done":
            break
    C.S.deferred_flush()
    for _ in gen:
        pass


def outxa_phase(C):
    nc, S, I, W = C.nc, C.S, C.I, C.W
    PP, PB = C.PP, C.PB
    with ExitStack() as es:
        def sb(name, shape, dt):
            return es.enter_context(nc.sbuf_tensor(name + "_p%d" % S.phase, shape, dt))
        kxT, b_kxT = sb("kxT", [128, 8, MEM], BF16), Buf("kxT")
        vx, b_vx = sb("vx", [128, 2, D], BF16), Buf("vx")
        banks = Rot([(PP[i][:, h * 512:(h + 1) * 512], PB[i][h]) for i in range(2) for h in range(2)])
        with ExitStack() as es1:
            def sb1(name, shape, dt):
                return es1.enter_context(nc.sbuf_tensor(name + "_q%d" % S.phase, shape, dt))
            wkv = sb1("wkv", [128, 8, 2 * D], BF16)
            b_wkv = [Buf("wkv0"), Buf("wkv1")]
            load_w_bf16(C, wkv, I["xa_wkv"], 0, 8, [(0, 1024), (1024, 2048)], b_wkv)
            gmT, b_gmT = sb1("gmT", [128, 8], F32), Buf("gmT")
            load_gT(C, gmT, I["mem_g"], b_gmT)
            pool1 = {"ss": Rot([(sb1("ss%d" % i, [128, 4], F32), Buf("ss%d" % i)) for i in range(2)]),
                     "junk": (sb1("junk", [128, D], F32), Buf("junk")),
                     "xn": (sb1("xn", [128, D], F32), Buf("xn"))}
            memT, b_memT = sb1("memT", [128, 8, MEM], BF16), [Buf("memT0"), Buf("memT1")]
            for mb in range(2):
                mt, b_mt = sb1("memt%d" % mb, [128, D], F32), Buf("memt%d" % mb)
                S.dma("sp", [(mt[:, :], I["mem"][mb * 128:(mb + 1) * 128, :])], w=[b_mt])
                norm_T(C, pool1, mt[:, :], b_mt, gmT, b_gmT, memT, mb * 128, b_memT[mb], PP[2 + mb], PB[2 + mb])
            for c in range(8):
                pa, pba = banks.next()
                for k in range(8):
                    S.op("pe", lambda e, k=k, c=c, pa=pa: e.matmul(
                        pa[:, 0:MEM], lhsT=wkv[:, k, c * 128:(c + 1) * 128], rhs=memT[:, k, :],
                        start=(k == 0), stop=(k == 7)), r=[b_wkv[0]] + b_memT, w=[pba])
                S.op("dve", lambda e, c=c, pa=pa: e.tensor_copy(out=kxT[:, c, :], in_=pa[:, 0:MEM]),
                     r=[pba], w=[b_kxT])
            for mb in range(2):
                for n in range(2):
                    pa, pba = banks.next()
                    for k in range(8):
                        S.op("pe", lambda e, k=k, mb=mb, n=n, pa=pa: e.matmul(
                            pa, lhsT=memT[:, k, mb * 128:(mb + 1) * 128],
                            rhs=wkv[:, k, D + n * 512:D + (n + 1) * 512], start=(k == 0), stop=(k == 7)),
                            r=[b_wkv[1], b_memT[mb]], w=[pba])
                    S.op("act", lambda e, mb=mb, n=n, pa=pa: e.activation(out=vx[:, mb, n * 512:(n + 1) * 512],
                                                                          in_=pa, func=AF.Copy),
                         r=[pba], w=[b_vx])
            S.deferred_flush()
        wout = sb("wout", [128, 16, D], BF16)
        b_wout = [Buf("wout0"), Buf("wout1")]
        wq, b_wq = sb("wq", [128, 8, D], BF16), Buf("wq")
        wo, b_wo = sb("wo", [128, 8, D], BF16), Buf("wo")
        gaT, b_gaT = sb("gaT", [128, 8], F32), Buf("gaT")
        gxT, b_gxT = sb("gxT", [128, 8], F32), Buf("gxT")
        gb_mix, b_gbm = sb("gb_mix", [128, D], F32), Buf("gb_mix")
        gb_xa, b_gbx = sb("gb_xa", [128, D], F32), Buf("gb_xa")
        load_gT(C, gaT, I["attn_norm_g"], b_gaT)
        load_gT(C, gxT, I["xa_pre_g"], b_gxT)
        load_bcast(C, gb_mix, I["mix_post_g"], b_gbm, D)
        load_bcast(C, gb_xa, I["xa_post_g"], b_gbx, D)
        for half in range(2):
            S.dma("pool", [(wout[:, k, :], I["w_out"][k * 128:(k + 1) * 128, :]) for k in range(half * 8, half * 8 + 8)],
                  w=[b_wout[half]])
        S.dma("pool", [(wq[:, k, :], I["xa_wq"][k * 128:(k + 1) * 128, :]) for k in range(8)], w=[b_wq])
        S.dma("pool", [(wo[:, k, :], I["xa_wo"][k * 128:(k + 1) * 128, :]) for k in range(8)], w=[b_wo])

        pool = {"ss": Rot([(sb("ss%d" % i, [128, 4], F32), Buf("ss%d" % i)) for i in range(4)]),
                "junk": (sb("junk", [128, D], F32), Buf("junk")),
                "xn": (sb("xn", [128, D], F32), Buf("xn"))}
        t1, b_t1 = sb("t1", [128, D], F32), Buf("t1")
        oa, b_oa = sb("oa", [128, 8, 512], F32), Buf("oa")
        os_, b_os = sb("os", [128, 8, 512], BF16), Buf("os")
        sqk = Rot([(sb("sqk%d" % i, [128, 512], BF16), Buf("sqk%d" % i)) for i in range(3)])
        rsb, b_rsb = sb("rsb", [128, 512], F32), Buf("rsb")
        oan, b_oan = sb("oan", [128, 8, 512], BF16), Buf("oan")
        h2 = [(sb("h2_%d" % t, [128, D], F32), Buf("h2_%d" % t)) for t in range(4)]
        hnT = sb("hnT", [128, 8, 512], BF16)
        b_hnT = [Buf("hnT%d" % t) for t in range(4)]
        qxT, b_qxT = sb("qxT", [128, 8, 512], BF16), [Buf("qxT%d" % c) for c in range(8)]
        pT = sb("pT", [128, 4, 2, 512], BF16)
        b_pT = [[Buf("pT%d_%d" % (hh, mb)) for mb in range(2)] for hh in range(4)]
        rden = Rot([(sb("rden%d" % i, [128, 512], F32), Buf("rden%d" % i)) for i in range(2)])
        oxn, b_oxn = sb("oxn", [128, 8, 512], BF16), [Buf("oxn%d" % c) for c in range(8)]
        oat_v = W["oattT"].rearrange("(k p) t -> p k t", p=128)
        oss_v = W["ossdT"].rearrange("(k p) t -> p k t", p=128)

        def S1(blk):
            tok0 = blk * 512
            S.dma("sp", [(oa[:, :, :], oat_v[:, :, tok0:tok0 + 512])], w=[b_oa])
            S.dma("sp", [(os_[:, :, :], oss_v[:, :, tok0:tok0 + 512])], w=[b_os])
            pa, pba = banks.next()
            for k in range(8):
                sq, b_sq = sqk.next()
                S.op("act", lambda e, k=k, sq=sq: e.activation(out=sq[:, :], in_=oa[:, k, :], func=AF.Square),
                     r=[b_oa], w=[b_sq])
                S.op("pe", lambda e, k=k, sq=sq, pa=pa: e.matmul(pa, lhsT=C.onesB[:, :], rhs=sq[:, :],
                                                                 start=(k == 0), stop=(k == 7)),
                     r=[b_sq, C.bc], w=[pba])
            S.op("act", lambda e, pa=pa: e.activation(out=rsb[:, :], in_=pa, func=AF.Ln, scale=1.0 / D, bias=EPS),
                 r=[pba], w=[b_rsb])
            S.op("act", lambda e: e.activation(out=rsb[:, :], in_=rsb[:, :], func=AF.Exp, scale=-0.5),
                 r=[b_rsb], w=[b_rsb])
            for k in range(8):
                S.op("dve", lambda e, k=k: e.scalar_tensor_tensor(out=oan[:, k, :], in0=oa[:, k, :],
                                                                  scalar=gaT[:, k:k + 1], in1=rsb[:, :],
                                                                  op0=ALU.mult, op1=ALU.mult),
                     r=[b_oa, b_gaT, b_rsb], w=[b_oan])

        NBX = SEQ // 512
        S1(0)
        for blk in range(NBX):
            tok0 = blk * 512
            for t in range(4):
                tt = blk * 4 + t
                pf, pbf = PP[2 + t % 2], PB[2 + t % 2]
                for n in range(2):
                    for k in range(16):
                        src_, bsrc = (oan, b_oan) if k < 8 else (os_, b_os)
                        S.op("pe", lambda e, n=n, k=k, t=t, pf=pf, src_=src_: e.matmul(
                            pf[:, n * 512:(n + 1) * 512], lhsT=src_[:, k % 8, t * 128:(t + 1) * 128],
                            rhs=wout[:, k, n * 512:(n + 1) * 512], start=(k == 0), stop=(k == 15)),
                            r=[bsrc, b_wout[k // 8]], w=[pbf[n]])
                xr, bxr = h2[t]
                S.dma("sp", [(xr[:, :], W["h1"][tt * 128:(tt + 1) * 128, :])], w=[bxr])
                if t > 0:
                    xp_, bxp_ = h2[t - 1]
                    norm_T(C, pool, xp_[:, :], bxp_, gxT, b_gxT, hnT, (t - 1) * 128, b_hnT[t - 1],
                           PP[(t - 1) % 2], PB[(t - 1) % 2])
                post_norm_residual(C, pool, pf, pbf, gb_mix, b_gbm, xr, bxr, t1, b_t1, 1.0)
            xp_, bxp_ = h2[3]
            norm_T(C, pool, xp_[:, :], bxp_, gxT, b_gxT, hnT, 3 * 128, b_hnT[3], PP[1], PB[1])
            for c in range(8):
                pa, pba = banks.next()
                for k in range(8):
                    S.op("pe", lambda e, k=k, c=c, pa=pa: e.matmul(
                        pa, lhsT=wq[:, k, c * 128:(c + 1) * 128], rhs=hnT[:, k, :], start=(k == 0), stop=(k == 7)),
                        r=[b_wq] + b_hnT, w=[pba])
                S.op("act", lambda e, c=c, pa=pa: e.activation(out=qxT[:, c, :], in_=pa, func=AF.Copy,
                                                               scale=1.0 / 16.0),
                     r=[pba], w=[b_qxT[c]])
            if blk + 1 < NBX:
                S1(blk + 1)
            def xa_scores(hh):
                for mb in range(2):
                    pa, pba = banks.next()
                    for dc in range(2):
                        cch = 2 * hh + dc
                        S.op("pe", lambda e, cch=cch, mb=mb, dc=dc, pa=pa: e.matmul(
                            pa, lhsT=kxT[:, cch, mb * 128:(mb + 1) * 128], rhs=qxT[:, cch, :],
                            start=(dc == 0), stop=(dc == 1)),
                            r=[b_kxT, b_qxT[cch]], w=[pba])
                    S.op("act", lambda e, hh=hh, mb=mb, pa=pa: e.activation(out=pT[:, hh, mb, :], in_=pa, func=AF.Exp),
                         r=[pba], w=[b_pT[hh][mb]])

            def xa_rest(hh):
                pa, pba = banks.next()
                for mb in range(2):
                    S.op("pe", lambda e, hh=hh, mb=mb, pa=pa: e.matmul(pa, lhsT=C.onesB[:, :], rhs=pT[:, hh, mb, :],
                                                                       start=(mb == 0), stop=(mb == 1)),
                         r=[b_pT[hh][mb], C.bc], w=[pba])
                rd, b_rd = rden.next()
                S.op("dve", lambda e, rd=rd, pa=pa: e.reciprocal(out=rd[:, :], in_=pa), r=[pba], w=[b_rd])
                for dc in range(2):
                    cch = 2 * hh + dc
                    pa, pba = banks.next()
                    for mb in range(2):
                        S.op("pe", lambda e, cch=cch, mb=mb, hh=hh, pa=pa: e.matmul(
                            pa, lhsT=vx[:, mb, cch * 128:(cch + 1) * 128], rhs=pT[:, hh, mb, :],
                            start=(mb == 0), stop=(mb == 1)),
                            r=[b_vx, b_pT[hh][mb]], w=[pba])
                    S.op("dve", lambda e, cch=cch, rd=rd, pa=pa: e.tensor_tensor(out=oxn[:, cch, :], in0=pa,
                                                                                in1=rd[:, :], op=ALU.mult),
                         r=[pba, b_rd], w=[b_oxn[cch]])

            for hh in range(4):
                xa_scores(hh)
                if hh > 0:
                    xa_rest(hh - 1)
            xa_rest(3)
            for t in range(4):
                tt = blk * 4 + t
                pf, pbf = PP[2 + t % 2], PB[2 + t % 2]
                for n in range(2):
                    for k in range(8):
                        S.op("pe", lambda e, n=n, k=k, t=t, pf=pf: e.matmul(
                            pf[:, n * 512:(n + 1) * 512], lhsT=oxn[:, k, t * 128:(t + 1) * 128],
                            rhs=wo[:, k, n * 512:(n + 1) * 512], start=(k == 0), stop=(k == 7)),
                            r=[b_oxn[k], b_wo], w=[pbf[n]])
                xr, bxr = h2[t]
                post_norm_residual(C, pool, pf, pbf, gb_xa, b_gbx, xr, bxr, t1, b_t1, 1.0)
                S.dma("pool", [(W["h3"][tt * 128:(tt + 1) * 128, :], xr[:, :])], r=[bxr])
        S.deferred_flush()
```

```python
import math
from contextlib import ExitStack

import numpy as np
import concourse.bass as bass
import concourse.mybir as mybir
from concourse.bass_utils import run_bass_kernel_spmd

F32 = mybir.dt.float32
BF16 = mybir.dt.bfloat16
AF = mybir.ActivationFunctionType
ALU = mybir.AluOpType
AX = mybir.AxisListType

ENGS = ("pe", "act", "dve", "pool", "sp")


class Buf:
    def __init__(self, name, excl=False):
        self.name = name
        self.excl = excl
        self.w = None
        self.r = []


class Op:
    __slots__ = ("eng", "fn", "deps", "flag", "val", "dma_sem", "dma_val", "ndma", "tag", "phase")

    def __init__(self, eng, fn):
        self.phase = 0
        self.eng = eng
        self.fn = fn
        self.deps = []
        self.flag = False
        self.val = None
        self.dma_sem = None
        self.dma_val = None
        self.ndma = 0
        self.tag = None


class Sched:
    def __init__(self, nc):
        self.nc = nc
        self.handles = {"pe": nc.tensor, "act": nc.scalar, "dve": nc.vector,
                        "pool": nc.gpsimd, "sp": nc.sync}
        self.sem = {e: nc.alloc_semaphore("s_" + e) for e in ("pe", "act", "dve", "pool")}
        self.count = {e: 0 for e in self.sem}
        self.ops = {e: [] for e in ENGS}
        self.last = {e: None for e in ENGS}
        self.dmasems = {}
        self.waited = {e: {} for e in ENGS}
        self.n_inst = 0
        self.prev_tokens = []
        self.phase = 0
        self.defer = False
        self.replaying = False
        self.deferq = []

    def _dep(self, op, p):
        if p is None or p is op or p.phase != self.phase:
            return
        if p.eng == "pe" and op.eng == "pe" and p.dma_sem is None:
            return
        op.deps.append(p)
        if p.dma_sem is None:
            p.flag = True

    def replay(self, n):
        q = self.deferq
        k = 0
        self.replaying = True
        while q and k < n:
            kind, args, kw = q.pop(0)
            if kind == "op":
                self.op(*args, **kw)
            else:
                self.dma(*args, **kw)
            k += 1
        self.replaying = False
        return len(q)

    def op(self, eng, fn, r=(), w=()):
        if self.defer and not self.replaying:
            self.deferq.append(("op", (eng, fn), {"r": list(r), "w": list(w)}))
            return None
        o = Op(eng, fn)
        o.phase = self.phase
        w = list(w) + [b for b in r if b.excl]
        r = [b for b in r if not b.excl]
        for b in r:
            self._dep(o, b.w)
        for b in w:
            self._dep(o, b.w)
            for x in b.r:
                self._dep(o, x)
        for b in r:
            b.r.append(o)
        for b in w:
            b.w = o
            b.r = []
        self.ops[eng].append(o)
        self.last[eng] = o
        return o

    def dma(self, q, pairs, r=(), w=(), key=None, **kw):
        if self.defer and not self.replaying:
            kw2 = dict(kw)
            kw2.update({"r": list(r), "w": list(w), "key": key})
            self.deferq.append(("dma", (q, pairs), kw2))
            return None
        if key is None:
            key = (w[0].name if w else r[0].name) + ("_ld" if w else "_st")
        if key not in self.dmasems:
            self.dmasems[key] = [self.nc.alloc_semaphore("d_" + str(len(self.dmasems))), 0]
        ent = self.dmasems[key]
        sem = ent[0]

        def fn(e, pairs=pairs, sem=sem, kw=kw):
            for (o_, i_) in pairs:
                e.dma_start(out=o_, in_=i_, **kw).then_inc(sem, 16)
            return None

        o = self.op(q, fn, r=r, w=w)
        ent[1] += 16 * len(pairs)
        o.dma_sem = sem
        o.dma_val = ent[1]
        o.ndma = len(pairs)
        return o

    def deferred_flush(self):
        if not hasattr(self, "phases"):
            self.phases = []
        self.phases.append(self.ops)
        self.ops = {e: [] for e in ENGS}
        self.phase += 1

    def emit_all(self):
        comp = ("pe", "act", "dve", "pool")
        sched = self
        plan = []
        prev = []
        for ops in self.phases:
            for e in comp:
                for o in reversed(ops[e]):
                    if o.dma_sem is None and o.fn is not None:
                        o.flag = True
                        break
            for e in comp:
                c = self.count[e]
                for o in ops[e]:
                    if o.dma_sem is None and o.flag:
                        c += 1
                        o.val = c
                self.count[e] = c
            plan.append((ops, list(prev)))
            prev = [(self.sem[e], self.count[e]) for e in comp if self.count[e] > 0]
            dm = {}
            for e in ENGS:
                for o in ops[e]:
                    if o.dma_sem is not None:
                        dm[id(o.dma_sem)] = (o.dma_sem, max(o.dma_val, dm.get(id(o.dma_sem), (None, 0))[1]))
            self._dm_seen = getattr(self, "_dm_seen", {})
            self._dm_seen.update(dm)
            prev += list(self._dm_seen.values())
        final_toks = list(prev)

        def emit_waits(eh, waited, toks):
            for sem, val in toks:
                k = id(sem)
                if waited.get(k, 0) >= val:
                    continue
                eh.wait_ge(sem, val)
                waited[k] = val

        def emit(ename, eh):
            waited = sched.waited[ename]
            for ops, prevt in plan:
                emit_waits(eh, waited, prevt)
                for o in ops[ename]:
                    toks = []
                    for p in o.deps:
                        if p.dma_sem is not None:
                            toks.append((p.dma_sem, p.dma_val))
                        else:
                            toks.append((sched.sem[p.eng], p.val))
                    emit_waits(eh, waited, toks)
                    ins = o.fn(eh)
                    if o.dma_sem is None and o.flag:
                        ins.then_inc(sched.sem[ename], 1)
            emit_waits(eh, waited, final_toks)

        with self.nc.Block() as block:
            @block.tensor
            def _(e):
                emit("pe", e)

            @block.scalar
            def _(e):
                emit("act", e)

            @block.vector
            def _(e):
                emit("dve", e)

            @block.gpsimd
            def _(e):
                emit("pool", e)

            @block.sync
            def _(e):
                emit("sp", e)

    def flush(self, final=False):
        comp = ("pe", "act", "dve", "pool")
        for e in comp:
            for o in reversed(self.ops[e]):
                if o.dma_sem is None and o.fn is not None:
                    o.flag = True
                    break
        for e in comp:
            c = self.count[e]
            for o in self.ops[e]:
                if o.dma_sem is None and o.flag:
                    c += 1
                    o.val = c
            self.count[e] = c
        sched = self
        prev = list(self.prev_tokens)

        def emit_waits(eh, waited, toks):
            for sem, val in toks:
                k = id(sem)
                if waited.get(k, 0) >= val:
                    continue
                eh.wait_ge(sem, val)
                waited[k] = val

        def emit(ename, eh):
            waited = sched.waited[ename]
            emit_waits(eh, waited, prev)
            for o in sched.ops[ename]:
                toks = []
                for p in o.deps:
                    if p.dma_sem is not None:
                        toks.append((p.dma_sem, p.dma_val))
                    else:
                        assert p.val is not None
                        toks.append((sched.sem[p.eng], p.val))
                emit_waits(eh, waited, toks)
                ins = o.fn(eh)
                sched.n_inst += 1
                if o.dma_sem is None and o.flag:
                    ins.then_inc(sched.sem[ename], 1)

        def run_block(fn_for):
            with self.nc.Block() as block:
                @block.tensor
                def _(e):
                    fn_for("pe", e)

                @block.scalar
                def _(e):
                    fn_for("act", e)

                @block.vector
                def _(e):
                    fn_for("dve", e)

                @block.gpsimd
                def _(e):
                    fn_for("pool", e)

                @block.sync
                def _(e):
                    fn_for("sp", e)

        run_block(emit)
        self.ops = {e: [] for e in ENGS}
        self.phase += 1
        self.prev_tokens = [(self.sem[e], self.count[e]) for e in comp if self.count[e] > 0]
        self.prev_tokens += [(s, c) for (s, c) in self.dmasems.values() if c > 0]
        if final:
            toks = list(self.prev_tokens)

            def fin(ename, eh):
                emit_waits(eh, sched.waited[ename], toks)

            run_block(fin)


D = 1024
SEQ = 4096
NT = SEQ // 128
DFF = 2816
NJ = DFF // 128
D_IN = 5648
MEM = 256
EPS = 1e-6
NEG = -30000.0
N_WARM = 0
SSD_1BANK = False
OVERLAP_SSD = True
SIDE_EVERY = 4


class Ctx:
    pass


def build_nc(debug=False, stages=("ffn1", "inproj", "attn", "ssd", "outxa", "ffn2")):
    nc = bass.Bass("TRN2", target_bir_lowering=False)
    S = Sched(nc)
    C = Ctx()
    C.nc, C.S = nc, S

    def din(name, shape):
        return nc.dram_tensor(name, list(shape), F32, kind="ExternalInput")

    I = {}
    I["x"] = din("x", [SEQ, D])
    I["mem"] = din("mem", [MEM, D])
    for nm, shp in [("ffn1_pre_g", [1, D]), ("ffn1_w_gu", [D, 2 * DFF]), ("ffn1_w_down", [DFF, D]),
                    ("ffn1_post_g", [1, D]), ("mix_pre_g", [1, D]), ("w_in", [D, D_IN]),
                    ("conv_w", [4, 1536]), ("conv_b", [1, 1536]), ("dt_bias", [1, 16]),
                    ("a_log", [1, 16]), ("d_skip", [1, 16]), ("ssd_norm_g", [1, D]),
                    ("attn_norm_g", [1, D]), ("w_out", [2 * D, D]), ("mix_post_g", [1, D]),
                    ("xa_pre_g", [1, D]), ("mem_g", [1, D]), ("xa_wq", [D, D]), ("xa_wkv", [D, 2 * D]),
                    ("xa_wo", [D, D]), ("xa_post_g", [1, D]), ("ffn2_pre_g", [1, D]),
                    ("ffn2_w_gu", [D, 2 * DFF]), ("ffn2_w_down", [DFF, D]), ("ffn2_post_g", [1, D])]:
        I[nm] = din(nm, shp)
    out = nc.dram_tensor("out", [SEQ, D], F32, kind="ExternalOutput")
    kd = "ExternalOutput" if debug else "Internal"
    W = {}
    W["h1"] = nc.dram_tensor("h1", [SEQ, D], F32, kind=kd)
    W["qT"] = nc.dram_tensor("qT", [D, SEQ], BF16, kind=kd)
    W["kT"] = nc.dram_tensor("kT", [D, SEQ], BF16, kind=kd)
    W["v"] = nc.dram_tensor("v", [SEQ, D], BF16, kind=kd)
    W["z"] = nc.dram_tensor("z", [SEQ, D], F32, kind=kd)
    W["xbcT"] = nc.dram_tensor("xbcT", [1536, SEQ], F32, kind=kd)
    W["dtr"] = nc.dram_tensor("dtr", [SEQ, 16], F32, kind=kd)
    W["oattT"] = nc.dram_tensor("oattT", [D, SEQ], F32, kind=kd)
    W["ossdT"] = nc.dram_tensor("ossdT", [D, SEQ], BF16, kind=kd)
    W["h3"] = nc.dram_tensor("h3", [SEQ, D], F32, kind=kd)
    C.I, C.W, C.out = I, W, out

    with ExitStack() as es0:
        def sb0(name, shape, dt):
            return es0.enter_context(nc.sbuf_tensor(name, shape, dt))

        C.PP = [es0.enter_context(nc.psum_tensor("pp%d" % i, [128, 1024], F32)) for i in range(4)]
        C.PB = [[Buf("pb%d_%d" % (i, h), excl=True) for h in range(2)] for i in range(4)]

        cst = {}
        bc = Buf("consts")
        C.bc = bc
        onesF = sb0("onesF", [128, 128], F32)
        negF = sb0("negF", [128, 128], F32)
        bigF = sb0("bigF", [128, 128], F32)
        S.op("pool", lambda e: e.memset(onesF[:, :], 1.0), w=[bc])
        S.op("pool", lambda e: e.memset(negF[:, :], -1.0), w=[bc])
        S.op("pool", lambda e: e.memset(bigF[:, :], NEG), w=[bc])

        scratchF = sb0("cscratch_f", [128, 128], F32)

        def mk(name, src, cm, step, cmp, dt=F32):
            tf = sb0(name + "_f", [128, 128], F32) if dt == F32 else scratchF
            S.op("pool", lambda e: e.affine_select(out=tf[:, :], in_=src[:, :], pattern=[[step, 128]],
                                                   compare_op=cmp, fill=0.0, base=0, channel_multiplier=cm),
                 r=[bc], w=[bc])
            if dt == F32:
                return tf
            tb = sb0(name + "_b", [128, 128], BF16)
            S.op("pool", lambda e: e.tensor_copy(out=tb[:, :], in_=tf[:, :]), r=[bc], w=[bc])
            return tb

        C.onesF = onesF
        C.identF = mk("ident", onesF, 1, -1, ALU.is_equal)
        C.identB = mk("identb", onesF, 1, -1, ALU.is_equal, BF16)
        C.triNegB = mk("trineg", negF, 1, -1, ALU.is_ge, BF16)
        C.mask01 = mk("mask01", onesF, -1, 1, ALU.is_gt)
        C.negMaskB = mk("negmask", bigF, 1, -1, ALU.is_ge, BF16)
        C.triInclF = mk("triincl", onesF, -1, 1, ALU.is_ge)
        C.triUpF = mk("triup", onesF, 1, -1, ALU.is_gt)
        C.triUpB = mk("triupb", onesF, 1, -1, ALU.is_gt, BF16)
        C.negMaskSB = mk("negmasks", bigF, 1, -1, ALU.is_gt, BF16)
        onesNegB = sb0("onesNegB", [128, 128], BF16)
        S.op("pool", lambda e: e.tensor_copy(out=onesNegB[:, :], in_=negF[:, :]), r=[bc], w=[bc])
        C.onesNegB = onesNegB
        onesB = sb0("onesB", [128, 128], BF16)
        S.op("pool", lambda e: e.tensor_copy(out=onesB[:, :], in_=onesF[:, :]), r=[bc], w=[bc])
        C.onesB = onesB

        if "ffn1" in stages:
            ffn_phase(C, I["x"], W["h1"], I["ffn1_w_gu"], I["ffn1_w_down"], I["ffn1_pre_g"], I["ffn1_post_g"])
        if "inproj" in stages:
            inproj_phase(C)
        if "attn" in stages and "ssd" in stages and OVERLAP_SSD:
            attn_phase(C, side=ssd_steps(C))
        else:
            if "attn" in stages:
                attn_phase(C)
            if "ssd" in stages:
                ssd_phase_1bank(C) if SSD_1BANK else ssd_phase(C)
        if "outxa" in stages:
            outxa_phase(C)
        if "ffn2" in stages:
            ffn_phase(C, W["h3"], out, I["ffn2_w_gu"], I["ffn2_w_down"], I["ffn2_pre_g"], I["ffn2_post_g"])
        S.deferred_flush()
        S.emit_all()
    return nc


class Rot:
    def __init__(self, items):
        self.items = items
        self.i = 0

    def next(self):
        it = self.items[self.i % len(self.items)]
        self.i += 1
        return it


def norm_A(C, pool, x_ap, bx):
    S = C.S
    ss, b_ss = pool["ss"].next()
    junk, b_junk = pool["junk"]
    xn, b_xn = pool["xn"]
    S.op("act", lambda e: e.activation(out=junk[:, :], in_=x_ap, func=AF.Square, accum_out=ss[:, 0:1]),
         r=[bx], w=[b_junk, b_ss])
    S.op("act", lambda e: e.activation(out=ss[:, 1:2], in_=ss[:, 0:1], func=AF.Ln, scale=1.0 / D, bias=EPS),
         r=[b_ss], w=[b_ss])
    S.op("act", lambda e: e.activation(out=ss[:, 2:3], in_=ss[:, 1:2], func=AF.Exp, scale=-0.5),
         r=[b_ss], w=[b_ss])
    S.op("dve", lambda e: e.tensor_scalar(out=xn[:, :], in0=x_ap, scalar1=ss[:, 2:3], scalar2=None, op0=ALU.mult),
         r=[bx, b_ss], w=[b_xn])


def norm_B(C, pool, gT, b_gT, dstT, col0, b_dst, pp, pb):
    S = C.S
    xn, b_xn = pool["xn"]
    for k in range(8):
        S.op("pe", lambda e, k=k: e.transpose(out=pp[:, k * 128:(k + 1) * 128], in_=xn[:, k * 128:(k + 1) * 128],
                                              identity=C.identF[:, :]),
             r=[b_xn, C.bc], w=[pb[k // 4]])
    for h in range(2):
        S.op("dve", lambda e, h=h: e.tensor_tensor(
            out=dstT[:, h * 4:(h + 1) * 4, col0:col0 + 128],
            in0=pp[:, h * 512:(h + 1) * 512].rearrange("p (k t) -> p k t", k=4),
            in1=gT[:, h * 4:(h + 1) * 4].unsqueeze(2).to_broadcast([128, 4, 128]), op=ALU.mult),
            r=[pb[h], b_gT], w=[b_dst])


def norm_T(C, pool, x_ap, bx, gT, b_gT, dstT, col0, b_dst, pp, pb):
    norm_A(C, pool, x_ap, bx)
    norm_B(C, pool, gT, b_gT, dstT, col0, b_dst, pp, pb)


def load_w_bf16(C, dst, w_dram, row0, nk, col_groups, bufs, dcol0=0):
    S = C.S
    for (c0, c1), b in zip(col_groups, bufs):
        pairs = []
        for k in range(nk):
            for s0 in range(c0, c1, 2048):
                s1 = min(c1, s0 + 2048)
                pairs.append((dst[:, k, dcol0 + s0:dcol0 + s1],
                              w_dram[row0 + k * 128:row0 + (k + 1) * 128, s0:s1]))
        S.dma("pool", pairs, w=[b])


def load_gT(C, dst, g_dram, b, n=8):
    C.S.dma("sp", [(dst[:, 0:n], g_dram[0, :].rearrange("(k p) -> p k", p=128))], w=[b],
            allow_slow_non_contiguous=True)


def load_bcast(C, dst, g_dram, b, n):
    C.S.dma("sp", [(dst[:, 0:n], g_dram[0:1, :].partition_broadcast(128))], w=[b])


def ffn_phase(C, src, dst, w_gu, w_dn, g_pre, g_post):
    nc, S = C.nc, C.S
    PP, PB = C.PP, C.PB
    with ExitStack() as es:
        def sb(name, shape, dt):
            return es.enter_context(nc.sbuf_tensor(name + "_p%d" % S.phase, shape, dt))
        wgu = sb("wgu", [128, 8, 2 * DFF], BF16)
        b_wgu = [Buf("wgu%d" % i) for i in range(8)]
        wd = sb("wd", [128, NJ, D], BF16)
        b_wd = [Buf("wd0"), Buf("wd1")]
        gT = sb("gT", [128, 8], F32); b_gT = Buf("gT")
        gb = sb("gb", [128, D], F32); b_gb = Buf("gb")
        load_gT(C, gT, g_pre, b_gT)
        load_bcast(C, gb, g_post, b_gb, D)
        grp = []
        for q4 in range(4):
            grp += [(q4 * 704, (q4 + 1) * 704), (DFF + q4 * 704, DFF + (q4 + 1) * 704)]
        load_w_bf16(C, wgu, w_gu, 0, 8, grp, [b_wgu[c0 // 704] for (c0, c1) in grp])
        for half in range(2):
            pairs = [(wd[:, j, :], w_dn[j * 128:(j + 1) * 128, :]) for j in range(half * 11, half * 11 + 11)]
            S.dma("pool", pairs, w=[b_wd[half]])

        def wgu_bufs(col):
            return sorted({b_wgu[col // 704], b_wgu[(col + 127) // 704]}, key=id)

        xin = Rot([(sb("xin%d" % i, [128, D], F32), Buf("xin%d" % i)) for i in range(2)])
        xres = Rot([(sb("xres%d" % i, [128, D], F32), Buf("xres%d" % i)) for i in range(1)])
        pool = {"ss": Rot([(sb("ss%d" % i, [128, 4], F32), Buf("ss%d" % i)) for i in range(4)]),
                "junk": (sb("junk", [128, D], BF16), Buf("junk")),
                "xn": (sb("xn", [128, D], F32), Buf("xn"))}
        t1, b_t1 = sb("t1", [128, D], F32), Buf("t1")
        xnT2 = [sb("xnT%d" % i, [128, 8, 512], BF16) for i in range(2)]
        b_xnT2 = [[Buf("xnT%d_%d" % (i, t)) for t in range(4)] for i in range(2)]
        actT = sb("actT", [128, NJ, 512], BF16)
        b_actT = [Buf("actT%d" % j) for j in range(NJ)]
        sg = Rot([(sb("sg%d" % i, [128, 512], F32), Buf("sg%d" % i)) for i in range(2)])

        NB = SEQ // 512
        xcur = {}

        def pre_load(blk, t):
            tt = blk * 4 + t
            xi, bxi = xin.next()
            S.dma("sp", [(xi[:, :], src[tt * 128:(tt + 1) * 128, :])], w=[bxi])
            xcur[(blk, t)] = (xi, bxi)

        def pre_A(blk, t):
            xi, bxi = xcur.pop((blk, t))
            norm_A(C, pool, xi[:, :], bxi)

        def pre_B(blk, t):
            tt = blk * 4 + t
            norm_B(C, pool, gT, b_gT, xnT2[blk % 2], t * 128, b_xnT2[blk % 2][t], PP[2 + tt % 2], PB[2 + tt % 2])

        for t in range(4):
            pre_load(0, t) if t < 2 else None
        for t in range(4):
            if t + 2 < 4:
                pass
            if t >= 2:
                pre_load(0, t)
            pre_A(0, t)
            pre_B(0, t)
        for blk in range(NB):
            xnT = xnT2[blk % 2]
            b_xnT = b_xnT2[blk % 2]
            nxt = blk + 1 if blk + 1 < NB else None
            for j in range(NJ):
                if nxt is not None:
                    for t in range(4):
                        if j == 5 * t:
                            pre_load(nxt, t)
                        if j == 5 * t + 2:
                            pre_A(nxt, t)
                        if j == 5 * t + 4:
                            pre_B(nxt, t)
                pa, pba = PP[j % 2], PB[j % 2]
                for half, cbase in ((0, j * 128), (1, DFF + j * 128)):
                    for k in range(8):
                        S.op("pe", lambda e, k=k, half=half, cbase=cbase, pa=pa, xnT=xnT: e.matmul(
                            pa[:, half * 512:(half + 1) * 512], lhsT=wgu[:, k, cbase:cbase + 128],
                            rhs=xnT[:, k, :], start=(k == 0), stop=(k == 7)),
                            r=wgu_bufs(cbase) + b_xnT, w=[pba[half]])
                sgt, b_sg = sg.next()
                S.op("act", lambda e, pa=pa, sgt=sgt: e.activation(out=sgt[:, :], in_=pa[:, 0:512], func=AF.Silu),
                     r=[pba[0]], w=[b_sg])
                S.op("dve", lambda e, pa=pa, sgt=sgt, j=j: e.tensor_tensor(
                    out=actT[:, j, :], in0=sgt[:, :], in1=pa[:, 512:1024], op=ALU.mult),
                    r=[b_sg, pba[1]], w=[b_actT[j]])
            for t in range(4):
                tt = blk * 4 + t
                pf, pbf = PP[2 + t % 2], PB[2 + t % 2]
                for n in range(2):
                    for j in range(NJ):
                        S.op("pe", lambda e, n=n, j=j, t=t, pf=pf: e.matmul(
                            pf[:, n * 512:(n + 1) * 512], lhsT=actT[:, j, t * 128:(t + 1) * 128],
                            rhs=wd[:, j, n * 512:(n + 1) * 512], start=(j == 0), stop=(j == NJ - 1)),
                            r=[b_actT[j], b_wd[j // 11]], w=[pbf[n]])
                xr, bxr = xres.next()
                S.dma("sp", [(xr[:, :], src[tt * 128:(tt + 1) * 128, :])], w=[bxr])
                post_norm_residual(C, pool, pf, pbf, gb, b_gb, xr, bxr, t1, b_t1, 0.5)
                S.dma("pool", [(dst[tt * 128:(tt + 1) * 128, :], xr[:, :])], r=[bxr])
        S.deferred_flush()


def post_norm_residual(C, pool, pf, pbf, gb, b_gb, xr, bxr, t1, b_t1, coef):
    S = C.S
    ss, b_ss = pool["ss"].next()
    junk, b_junk = pool["junk"]
    S.op("act", lambda e: e.activation(out=junk[:, :], in_=pf[:, :], func=AF.Square, accum_out=ss[:, 0:1]),
         r=pbf, w=[b_junk, b_ss])
    S.op("act", lambda e: e.activation(out=ss[:, 1:2], in_=ss[:, 0:1], func=AF.Ln, scale=1.0 / D, bias=EPS),
         r=[b_ss], w=[b_ss])
    S.op("act", lambda e: e.activation(out=ss[:, 2:3], in_=ss[:, 1:2], func=AF.Exp, scale=-0.5),
         r=[b_ss], w=[b_ss])
    S.op("dve", lambda e: e.scalar_tensor_tensor(out=t1[:, :], in0=pf[:, :], scalar=ss[:, 2:3], in1=gb[:, :],
                                                 op0=ALU.mult, op1=ALU.mult),
         r=pbf + [b_ss, b_gb], w=[b_t1])
    S.op("dve", lambda e: e.scalar_tensor_tensor(out=xr[:, :], in0=t1[:, :], scalar=float(coef), in1=xr[:, :],
                                                 op0=ALU.mult, op1=ALU.add),
         r=[b_t1, bxr], w=[bxr])


PARAM_NAMES = ["ffn1_pre_g", "ffn1_w_gu", "ffn1_w_down", "ffn1_post_g", "mix_pre_g", "w_in", "conv_w", "conv_b",
               "dt_bias", "a_log", "d_skip", "ssd_norm_g", "attn_norm_g", "w_out", "mix_post_g", "xa_pre_g",
               "mem_g", "xa_wq", "xa_wkv", "xa_wo", "xa_post_g", "ffn2_pre_g", "ffn2_w_gu", "ffn2_w_down",
               "ffn2_post_g"]


def core_inputs(inputs, b):
    m = {"x": np.ascontiguousarray(inputs["x"][b], dtype=np.float32),
         "mem": np.ascontiguousarray(inputs["mem"][b], dtype=np.float32)}
    for nm in PARAM_NAMES:
        a = np.asarray(inputs[nm], dtype=np.float32)[0]
        if a.ndim == 1:
            a = a[None, :]
        m[nm] = np.ascontiguousarray(a)
    return m


def kernel(**inputs):
    nc = build_nc(debug=False)
    in_maps = [core_inputs(inputs, b) for b in range(8)]
    res = run_bass_kernel_spmd(nc, in_maps, core_ids=list(range(8)))
    return np.stack([np.asarray(r["out"], dtype=np.float32) for r in res.results], axis=0)


def inproj_phase(C):
    nc, S, I, W = C.nc, C.S, C.I, C.W
    PP, PB = C.PP, C.PB
    with ExitStack() as es:
        def sb(name, shape, dt):
            return es.enter_context(nc.sbuf_tensor(name + "_p%d" % S.phase, shape, dt))
        win = sb("win", [128, 8, D_IN], BF16)
        groups = [(4096, 5120), (0, 1024), (1024, 2048), (5120, 5648), (2048, 3072), (3072, 4096)]
        b_win = {g: Buf("win%d" % i) for i, g in enumerate(groups)}
        gT = sb("gT", [128, 8], F32); b_gT = Buf("gT")
        load_gT(C, gT, I["mix_pre_g"], b_gT)
        load_w_bf16(C, win, I["w_in"], 0, 8, groups, [b_win[g] for g in groups])

        def win_buf(col):
            for g in groups:
                if g[0] <= col < g[1]:
                    return b_win[g]

        xin = Rot([(sb("xin%d" % i, [128, D], F32), Buf("xin%d" % i)) for i in range(4)])
        pool = {"ss": Rot([(sb("ss%d" % i, [128, 4], F32), Buf("ss%d" % i)) for i in range(4)]),
                "junk": (sb("junk", [128, D], F32), Buf("junk")),
                "xn": (sb("xn", [128, D], F32), Buf("xn"))}
        uT2 = [sb("uT%d" % i, [128, 8, 512], BF16) for i in range(2)]
        b_uT2 = [[Buf("uT%d_%d" % (i, t)) for t in range(4)] for i in range(2)]
        bkc = Buf("convconst")
        cw4 = sb("cw4", [128, 12, 4], F32)
        cbt = sb("cbt", [128, 12], F32)
        for k in range(4):
            S.dma("sp", [(cw4[:, :, k], I["conv_w"][k, :].rearrange("(ch p) -> p ch", p=128))], w=[bkc],
                  allow_slow_non_contiguous=True, key="convc")
        S.dma("sp", [(cbt[:, :], I["conv_b"][0, :].rearrange("(ch p) -> p ch", p=128))], w=[bkc],
              allow_slow_non_contiguous=True, key="convc")
        hal = [(sb("hal%d" % i, [128, 4], F32), Buf("hal%d" % i)) for i in range(12)]
        xp_r = Rot([(sb("xp%d" % i, [128, 516], F32), Buf("xp%d" % i)) for i in range(3)])
        acc_r = Rot([(sb("acc%d" % i, [128, 512], F32), Buf("acc%d" % i)) for i in range(3)])
        stb = Rot([(sb("stb%d" % i, [128, 512], BF16), Buf("stb%d" % i)) for i in range(3)])
        stf = Rot([(sb("stf%d" % i, [128, 512], F32), Buf("stf%d" % i)) for i in range(3)])
        vst = Rot([(sb("vst%d" % i, [128, D], BF16), Buf("vst%d" % i)) for i in range(2)])
        zst = Rot([(sb("zst%d" % i, [128, D], F32), Buf("zst%d" % i)) for i in range(2)])
        dst_ = Rot([(sb("dts%d" % i, [128, 16], F32), Buf("dts%d" % i)) for i in range(2)])
        banks = Rot([(PP[i][:, h * 512:(h + 1) * 512], PB[i][h]) for i in range(2) for h in range(2)])
        ev = [0]
        pend = []

        def evac(out_ap, in_ap, r, w, scale=None):
            ev[0] += 1
            if scale is not None or ev[0] % 2 == 0:
                S.op("act", lambda e: e.activation(out=out_ap, in_=in_ap, func=AF.Copy,
                                                   scale=(1.0 if scale is None else scale)), r=r, w=w)
            else:
                S.op("dve", lambda e: e.tensor_copy(out=out_ap, in_=in_ap), r=r, w=w)

        NB = SEQ // 512
        xcur = {}

        def pre_load(blk, t):
            tt = blk * 4 + t
            xi, bxi = xin.next()
            S.dma("sp", [(xi[:, :], W["h1"][tt * 128:(tt + 1) * 128, :])], w=[bxi])
            xcur[(blk, t)] = (xi, bxi)

        def pre_A(blk, t):
            xi, bxi = xcur.pop((blk, t))
            norm_A(C, pool, xi[:, :], bxi)

        def pre_B(blk, t):
            tt = blk * 4 + t
            norm_B(C, pool, gT, b_gT, uT2[blk % 2], t * 128, b_uT2[blk % 2][t], PP[2 + tt % 2], PB[2 + tt % 2])

        for t in range(4):
            pre_load(0, t)
        for t in range(4):
            pre_A(0, t)
            pre_B(0, t)
        order = []
        for i in range(12):
            order += [16 + i, i]
        order += [12, 13, 14, 15]
        for blk in range(NB):
            tok0 = blk * 512
            uT = uT2[blk % 2]
            b_uT = b_uT2[blk % 2]
            nxt = blk + 1 if blk + 1 < NB else None
            for ci, c in enumerate(order):
                if nxt is not None:
                    for t in range(4):
                        if ci == 6 * t:
                            pre_load(nxt, t)
                        if ci == 6 * t + 2:
                            pre_A(nxt, t)
                        if ci == 6 * t + 4:
                            pre_B(nxt, t)
                col0 = c * 128 if c < 16 else 4096 + (c - 16) * 128
                pa, pba = banks.next()
                for k in range(8):
                    S.op("pe", lambda e, k=k, col0=col0, pa=pa, uT=uT: e.matmul(
                        pa, lhsT=win[:, k, col0:col0 + 128], rhs=uT[:, k, :], start=(k == 0), stop=(k == 7)),
                        r=[win_buf(col0)] + b_uT, w=[pba])
                if c < 16:
                    st, b_st = stb.next()
                    evac(st[:, :], pa, [pba], [b_st], scale=(0.125 if c < 8 else 1.0))
                    dram = W["qT"] if c < 8 else W["kT"]
                    r0 = (c % 8) * 128
                    S.dma("pool", [(dram[r0:r0 + 128, tok0:tok0 + 512], st[:, :])], r=[b_st])
                else:
                    cc = c - 16
                    xp, b_xp = xp_r.next()
                    hl, b_hl = hal[cc]
                    if blk == 0:
                        S.op("dve", lambda e, xp=xp: e.memset(xp[:, 0:3], 0.0), w=[b_xp])
                    else:
                        S.op("dve", lambda e, xp=xp, hl=hl: e.tensor_copy(out=xp[:, 0:3], in_=hl[:, 0:3]),
                             r=[b_hl], w=[b_xp])
                    S.op("act", lambda e, xp=xp, pa=pa: e.activation(out=xp[:, 3:515], in_=pa, func=AF.Copy),
                         r=[pba], w=[b_xp])
                    while pend:
                        pend.pop(0)()
                    S.op("dve", lambda e, xp=xp, hl=hl: e.tensor_copy(out=hl[:, 0:3], in_=xp[:, 512:515]),
                         r=[b_xp], w=[b_hl])
                    ac, b_ac = acc_r.next()
                    S.op("dve", lambda e, xp=xp, ac=ac, cc=cc: e.tensor_scalar(
                        out=ac[:, :], in0=xp[:, 0:512], scalar1=cw4[:, cc, 0:1], scalar2=cbt[:, cc:cc + 1],
                        op0=ALU.mult, op1=ALU.add), r=[b_xp, bkc], w=[b_ac])
                    for k in range(1, 4):
                        S.op("dve", lambda e, xp=xp, ac=ac, cc=cc, k=k: e.scalar_tensor_tensor(
                            out=ac[:, :], in0=xp[:, k:k + 512], scalar=cw4[:, cc, k:k + 1], in1=ac[:, :],
                            op0=ALU.mult, op1=ALU.add), r=[b_xp, bkc, b_ac], w=[b_ac])
                    def fin(ac=ac, b_ac=b_ac, cc=cc, tok0=tok0):
                        st, b_st = stf.next()
                        S.op("act", lambda e: e.activation(out=st[:, :], in_=ac[:, :], func=AF.Silu),
                             r=[b_ac], w=[b_st])
                        r0 = cc * 128
                        S.dma("pool", [(W["xbcT"][r0:r0 + 128, tok0:tok0 + 512], st[:, :])], r=[b_st])
                    pend.append(fin)
            while pend:
                pend.pop(0)()
            for t in range(4):
                tt = blk * 4 + t
                vs, b_vs = vst.next()
                zs, b_zs = zst.next()
                for n in range(4):
                    col0 = 2048 + n * 512
                    pa, pba = banks.next()
                    for k in range(8):
                        S.op("pe", lambda e, k=k, col0=col0, pa=pa, t=t, uT=uT: e.matmul(
                            pa, lhsT=uT[:, k, t * 128:(t + 1) * 128], rhs=win[:, k, col0:col0 + 512],
                            start=(k == 0), stop=(k == 7)),
                            r=[win_buf(col0), b_uT[t]], w=[pba])
                    if n < 2:
                        evac(vs[:, n * 512:(n + 1) * 512], pa, [pba], [b_vs])
                    else:
                        S.op("act", lambda e, zs=zs, n=n, pa=pa: e.activation(
                            out=zs[:, (n - 2) * 512:(n - 1) * 512], in_=pa, func=AF.Silu), r=[pba], w=[b_zs])
                S.dma("pool", [(W["v"][tt * 128:(tt + 1) * 128, :], vs[:, :])], r=[b_vs])
                S.dma("pool", [(W["z"][tt * 128:(tt + 1) * 128, :], zs[:, :])], r=[b_zs])
                pa, pba = banks.next()
                for k in range(8):
                    S.op("pe", lambda e, k=k, pa=pa, t=t, uT=uT: e.matmul(
                        pa[:, 0:16], lhsT=uT[:, k, t * 128:(t + 1) * 128], rhs=win[:, k, 5632:5648],
                        start=(k == 0), stop=(k == 7)),
                        r=[win_buf(5632), b_uT[t]], w=[pba])
                ds, b_ds = dst_.next()
                evac(ds[:, :], pa[:, 0:16], [pba], [b_ds])
                S.dma("pool", [(W["dtr"][tt * 128:(tt + 1) * 128, :], ds[:, :])], r=[b_ds])
        S.deferred_flush()


def attn_phase(C, side=None):
    nc, S, W = C.nc, C.S, C.W
    PP, PB = C.PP, C.PB
    with ExitStack() as es:
        def sb(name, shape, dt):
            return es.enter_context(nc.sbuf_tensor(name + "_p%d" % S.phase, shape, dt))
        hk = Rot([(sb("hk%d" % i, [64, SEQ], BF16), Buf("hk%d" % i)) for i in range(2)])
        hq = Rot([(sb("hq%d" % i, [64, SEQ], BF16), Buf("hq%d" % i)) for i in range(2)])
        hv = Rot([(sb("hv%d" % i, [128, NT, 64], BF16), Buf("hv%d" % i)) for i in range(2)])
        ebuf = Rot([(sb("eb%d" % i, [128, 1024], F32), Buf("eb%d" % i)) for i in range(2)])
        spb = Rot([(sb("sp%d" % i, [128, 1024], BF16), Buf("sp%d" % i)) for i in range(4)])
        abuf = Rot([(sb("ab%d" % i, [128, 1024], BF16), Buf("ab%d" % i)) for i in range(3)])
        Rb = [(sb("R%d" % i, [128, 512], BF16), Buf("R%d" % i)) for i in range(2)]
        ost = Rot([(sb("ost%d" % i, [64, 512], F32), Buf("ost%d" % i)) for i in range(2)])
        Aregs = Rot([(PP[i], PB[i]) for i in range(3)])
        zb, b_zb = sb("zb", [128, 512], BF16), Buf("zb")
        S.op("dve", lambda e: e.memset(zb[:, :], 0.0), w=[b_zb])
        Obanks = Rot([(PP[3][:, h * 512:(h + 1) * 512], PB[3][h]) for h in range(1 if side is not None else 2)])

        class T:
            pass

        HD = {}

        def issue_loads(h):
            k_t, b_k = hk.next()
            q_t, b_q = hq.next()
            v_t, b_v = hv.next()
            S.dma("sp", [(k_t[:, :], W["kT"][h * 64:(h + 1) * 64, :])], w=[b_k])
            S.dma("sp", [(q_t[:, :], W["qT"][h * 64:(h + 1) * 64, :])], w=[b_q])
            S.dma("sp", [(v_t[:, :, :], W["v"][:, h * 64:(h + 1) * 64].rearrange("(kb p) d -> p kb d", p=128))],
                  w=[b_v])
            HD[h] = (k_t, b_k, q_t, b_q, v_t, b_v)

        units = []
        for h in range(16):
            uidx = 0
            for G in range(8):
                rcur = 0
                nkb = 4 * G + 4
                chain = []
                for idx, kb in enumerate(range(nkb - 1, -1, -1)):
                    t = T()
                    t.h, t.G, t.kb = h, G, kb
                    t.first = (idx == 0)
                    t.last = (kb == 0)
                    i = kb - 4 * G
                    t.c0 = 128 * i if i >= 0 else 0
                    t.diag = (i >= 0)
                    t.Rin = Rb[rcur]
                    t.Rout = Rb[1 - rcur]
                    rcur = 1 - rcur
                    chain.append(t)
                j = 0
                while j < len(chain):
                    u = T()
                    if chain[j].diag:
                        u.tiles = [chain[j]]
                        j += 1
                    else:
                        u.tiles = [chain[j], chain[j + 1]]
                        j += 2
                    for o, t in enumerate(u.tiles):
                        t.off = 512 * o
                    u.h, u.G = h, G
                    u.uidx = uidx
                    uidx += 1
                    u.lo = u.tiles[0].c0
                    u.hi = 512 * len(u.tiles)
                    units.append(u)

        OB = {}

        def stage1(u):
            if u.uidx == 0 and u.h == 0:
                issue_loads(0)
            if u.uidx == 3 and u.h + 1 < 16:
                issue_loads(u.h + 1)
            u.k_t, u.b_k, u.q_t, u.b_q, u.v_t, u.b_v = HD[u.h]
            if u.tiles[0].first:
                OB[(u.h, u.G)] = Obanks.next()
            u.o_ps, u.b_o = OB[(u.h, u.G)]
            u.A, u.b_A = Aregs.next()
            A = u.A
            for t in u.tiles:
                for _ in range(N_WARM):
                    S.op("pe", lambda e, t=t: e.matmul(A[:, t.off:t.off + 512], lhsT=C.onesB[:, :], rhs=zb[:, :],
                                                       start=True, stop=True),
                         r=[C.bc, b_zb], w=[u.b_A[t.off // 512]])
            for t in u.tiles:
                S.op("pe", lambda e, t=t: e.matmul(
                    A[:, t.off + t.c0:t.off + 512], lhsT=u.k_t[:, t.kb * 128:(t.kb + 1) * 128],
                    rhs=u.q_t[:, t.G * 512 + t.c0:(t.G + 1) * 512], start=True, stop=True),
                    r=[u.b_k, u.b_q], w=[u.b_A[t.off // 512]])
            nb = len(u.tiles)
            u.e, u.b_e = ebuf.next()
            u.sp, u.b_sp = spb.next()
            e_, sp_ = u.e, u.sp
            lo, hi = u.lo, u.hi
            S.op("act", lambda e: e.activation(out=e_[:, lo:hi], in_=A[:, lo:hi], func=AF.Exp),
                 r=u.b_A[0:nb], w=[u.b_e])
            S.op("act", lambda e: e.activation(out=sp_[:, lo:hi], in_=e_[:, lo:hi], func=AF.Ln, bias=1.0),
                 r=[u.b_e], w=[u.b_sp])
            t0 = u.tiles[0]
            if t0.diag:
                c0 = t0.c0
                S.op("dve", lambda e: e.tensor_tensor(out=sp_[:, c0:c0 + 128], in0=sp_[:, c0:c0 + 128],
                                                      in1=C.mask01[:, :], op=ALU.mult),
                     r=[u.b_sp, C.bc], w=[u.b_sp])

        def stage2(u):
            A, sp_ = u.A, u.sp
            for t in u.tiles:
                c0, off = t.c0, t.off
                bA = u.b_A[off // 512]
                Rin, b_Rin = t.Rin
                Rout, b_Rout = t.Rout
                S.op("pe", lambda e, c0=c0, off=off: e.matmul(
                    A[:, off + c0:off + 512], lhsT=C.triNegB[:, :], rhs=sp_[:, off + c0:off + 512],
                    start=False, stop=True, skip_group_check=True),
                    r=[u.b_sp, C.bc], w=[bA])
                if not t.first:
                    S.op("pe", lambda e, c0=c0, off=off, Rin=Rin: e.matmul(
                        A[:, off + c0:off + 512], lhsT=C.onesNegB[:, :], rhs=Rin[:, c0:512],
                        start=False, stop=True, skip_group_check=True),
                        r=[b_Rin, C.bc], w=[bA])
                if t.diag:
                    S.op("pe", lambda e, c0=c0, off=off: e.matmul(
                        A[:, off + c0:off + c0 + 128], lhsT=C.identB[:, :], rhs=C.negMaskB[:, :],
                        start=False, stop=True, skip_group_check=True),
                        r=[C.bc], w=[bA])
                if not t.last:
                    if t.first:
                        S.op("dve", lambda e, c0=c0, Rout=Rout: e.memset(Rout[:, 0:c0], 0.0), w=[b_Rout])
                        S.op("dve", lambda e, c0=c0, off=off, Rout=Rout: e.tensor_copy(
                            out=Rout[:, c0:512], in_=sp_[:, off + c0:off + 512]), r=[u.b_sp], w=[b_Rout])
                    else:
                        if c0 > 0:
                            S.op("dve", lambda e, c0=c0, Rout=Rout: e.memset(Rout[:, 0:c0], 0.0), w=[b_Rout])
                        S.op("dve", lambda e, c0=c0, off=off, Rin=Rin, Rout=Rout: e.tensor_tensor(
                            out=Rout[:, c0:512], in0=Rin[:, c0:512], in1=sp_[:, off + c0:off + 512], op=ALU.add),
                            r=[b_Rin, u.b_sp], w=[b_Rout])

        def stage3(u):
            u.a, u.b_a = abuf.next()
            a_, A = u.a, u.A
            lo, hi = u.lo, u.hi
            S.op("act", lambda e: e.activation(out=a_[:, lo:hi], in_=A[:, lo:hi], func=AF.Exp),
                 r=u.b_A[0:len(u.tiles)], w=[u.b_a])

        def stage4(u):
            a_ = u.a
            for t in u.tiles:
                c0, off = t.c0, t.off
                S.op("pe", lambda e, t=t, c0=c0, off=off: e.matmul(
                    u.o_ps[0:64, c0:512], lhsT=u.v_t[:, t.kb, :], rhs=a_[:, off + c0:off + 512],
                    start=t.first, stop=t.last, skip_group_check=True),
                    r=[u.b_a, u.b_v], w=[u.b_o])
                if t.last:
                    o_s, b_os = ost.next()
                    S.op("dve", lambda e, o_s=o_s: e.tensor_copy(out=o_s[:, :], in_=u.o_ps[0:64, :]),
                         r=[u.b_o], w=[b_os])
                    S.dma("pool", [(W["oattT"][t.h * 64:(t.h + 1) * 64, t.G * 512:(t.G + 1) * 512], o_s[:, :])],
                          r=[b_os])

        n = len(units)
        per_unit = 0
        if side is not None:
            S.defer = True
            for r_ in side:
                if r_ == "done":
                    break
            S.defer = False
            per_unit = -(-len(S.deferq) // max(1, n - 40))
        for i in range(n + 2):
            if i < n:
                stage1(units[i])
            if 0 <= i - 1 < n:
                stage2(units[i - 1])
                stage3(units[i - 1])
            if 0 <= i - 2 < n:
                stage4(units[i - 2])
            if side is not None:
                S.replay(per_unit)
        if side is not None:
            S.replay(10 ** 9)
        S.deferred_flush()
        if side is not None:
            for _ in side:
                pass


def ssd_phase(C):
    nc, S, I, W = C.nc, C.S, C.I, C.W
    PP, PB = C.PP, C.PB
    with ExitStack() as es:
        def sb(name, shape, dt):
            return es.enter_context(nc.sbuf_tensor(name + "_p%d" % S.phase, shape, dt))
        bk = Buf("ssdconst")
        dtb = sb("dtb", [128, 16], F32)
        abc = sb("abc", [128, 16], F32)
        dsk = sb("dsk", [128, 16], F32)
        gssd = sb("gssd", [128, D], F32)
        nm4 = sb("nm4", [128, 4, 128], BF16)
        S.dma("sp", [(dtb[:, :], I["dt_bias"][0:1, :].partition_broadcast(128))], w=[bk], key="ssdc")
        S.dma("sp", [(abc[:, :], I["a_log"][0:1, :].partition_broadcast(128))], w=[bk], key="ssdc")
        S.dma("sp", [(dsk[:, :], I["d_skip"][0:1, :].partition_broadcast(128))], w=[bk], key="ssdc")
        S.dma("sp", [(gssd[:, :], I["ssd_norm_g"][0:1, :].partition_broadcast(128))], w=[bk], key="ssdc")
        S.op("act", lambda e: e.activation(out=abc[:, :], in_=abc[:, :], func=AF.Exp), r=[bk], w=[bk])
        S.op("dve", lambda e: e.tensor_scalar(out=abc[:, :], in0=abc[:, :], scalar1=-1.0, scalar2=None,
                                              op0=ALU.mult), r=[bk], w=[bk])
        for i in range(4):
            S.op("dve", lambda e, i=i: e.tensor_copy(out=nm4[:, i, :], in_=C.negMaskSB[:, :]), r=[C.bc], w=[bk])

        def rot2(name, shape, dt, n=2):
            return Rot([(sb("%s%d" % (name, i), shape, dt), Buf("%s%d" % (name, i))) for i in range(n)])

        cv_r = rot2("cv", [128, 12, 128], F32, 3)
        dtr_t = rot2("dtr", [128, 16], F32)
        z_t = rot2("zt", [128, D], F32, 3)
        cvb_r = rot2("cvb", [128, 4, 128], BF16)
        xs_r = rot2("xs_tok", [128, D], F32)
        Bt_r = rot2("B_tok", [128, 256], BF16)
        sm_r = rot2("sm", [128, 8, 16], F32)
        edt_r = rot2("edt", [128, 48], F32)
        xbf_r = rot2("x_bf", [128, D], BF16)
        xdec_r = rot2("xdec", [128, D], BF16)
        y1_r = rot2("y1", [128, D], F32)
        rhs1_r = rot2("rhs1", [128, 2, 16, 128], BF16)
        smb, b_smb = sb("smb", [128, 2, 16], BF16), Buf("smb")
        cbs_r = rot2("cbs", [128, 256], F32)
        Mt = rot2("Mt", [128, 4, 128], BF16, 4)

        def rhs1_of(st):
            return st.rhs1

        def pcb_of(st):
            return st.cbs
        lm = rot2("lm", [128, 512], F32)
        yoffs, b_yoffs = sb("yoffs", [128, D], F32), Buf("yoffs")
        hstate, b_hst = sb("hstate", [128, D], F32), Buf("hstate")
        hsb, b_hsb = sb("hsb", [128, D], BF16), Buf("hsb")
        junk, b_junk = sb("junk", [128, D], F32), Buf("junk")
        ss_r = rot2("ss", [128, 8], F32)
        ot = rot2("ot", [128, 8, 128], BF16)
        Dbanks = Rot([(PP[2][:, h * 512:(h + 1) * 512], PB[2][h]) for h in range(2)])
        b_small = b_cb = PB[3][0]
        psm = PP[3][:, 0:48]
        pcb = PP[3][:, 128:384]
        pBt = PP[3][:, 512:768]
        xbc_v = W["xbcT"].rearrange("(ch p) t -> p ch t", p=128)
        ossd_v = W["ossdT"].rearrange("(k p) t -> p k t", p=128)

        def bc16(ap16, n0, n):
            return ap16[:, n0:n0 + n].unsqueeze(2).to_broadcast([128, n, 64])

        def h16(ap):
            return ap.rearrange("p (h d) -> p h d", d=64)

        ball = Buf("ssd_all")
        dtr_all = sb("dtr_all", [128, NT, 16], F32)
        dt_all = sb("dt_all", [128, NT, 16], F32)
        adt_all = sb("adt_all", [128, NT, 16], F32)
        adt_hi = sb("adt_hi", [128, NT, 16], BF16)
        adt_lo = sb("adt_lo", [128, NT, 16], F32)
        edt_all = sb("edt_all", [128, 3, NT, 16], F32)
        ddec_all = sb("ddec_all", [128, NT, 16], F32)
        S.dma("sp", [(dtr_all[:, :, :], W["dtr"].rearrange("(c p) h -> p c h", p=128))], w=[ball])

        def bcc(ap16):
            return ap16[:, :].unsqueeze(1).to_broadcast([128, NT, 16])

        S.op("dve", lambda e: e.tensor_tensor(out=dtr_all[:, :, :], in0=dtr_all[:, :, :], in1=bcc(dtb), op=ALU.add),
             r=[ball, bk], w=[ball])
        S.op("act", lambda e: e.activation(out=dtr_all[:, :, :], in_=dtr_all[:, :, :], func=AF.Exp),
             r=[ball], w=[ball])
        S.op("act", lambda e: e.activation(out=dt_all[:, :, :], in_=dtr_all[:, :, :], func=AF.Ln, bias=1.0),
             r=[ball], w=[ball])
        S.op("dve", lambda e: e.tensor_tensor(out=adt_all[:, :, :], in0=dt_all[:, :, :], in1=bcc(abc), op=ALU.mult),
             r=[ball, bk], w=[ball])
        S.op("dve", lambda e: e.tensor_copy(out=adt_hi[:, :, :], in_=adt_all[:, :, :]), r=[ball], w=[ball])
        S.op("dve", lambda e: e.tensor_tensor(out=adt_lo[:, :, :], in0=adt_all[:, :, :], in1=adt_hi[:, :, :],
                                              op=ALU.subtract), r=[ball], w=[ball])
        for i, lt in enumerate((C.triInclF, C.triUpF, C.onesF)):
            pq = PP[i // 2][:, (i % 2) * 512:(i % 2 + 1) * 512]
            S.op("pe", lambda e, lt=lt, pq=pq: e.matmul(pq, lhsT=lt[:, :],
                                                        rhs=adt_all[:, :, :].rearrange("p c h -> p (c h)"),
                                                        start=True, stop=True),
                 r=[ball, C.bc], w=[PB[i // 2][i % 2]])
            S.op("act", lambda e, i=i, pq=pq: e.activation(out=edt_all[:, i, :, :].rearrange("p c h -> p (c h)"),
                                                           in_=pq, func=AF.Exp),
                 r=[PB[i // 2][i % 2]], w=[ball])
        S.op("dve", lambda e: e.tensor_tensor(out=ddec_all[:, :, :], in0=dt_all[:, :, :], in1=edt_all[:, 1, :, :],
                                              op=ALU.mult), r=[ball], w=[ball])

        CH = {}

        class St:
            pass

        def front(c):
            st = St()
            CH[c] = st
            tok0 = c * 128
            st.rhs1, st.b_rhs1 = rhs1_r.next()
            rhs1, b_rhs1 = st.rhs1, st.b_rhs1
            for hl, src_ap, eng in ((0, adt_hi[:, c, :], "pool"), (1, adt_lo[:, c, :], "dve")):
                S.op(eng, lambda e, hl=hl, src_ap=src_ap: e.tensor_tensor(
                    out=rhs1[:, hl, :, :], in0=C.triInclF[:, :].unsqueeze(1).to_broadcast([128, 16, 128]),
                    in1=src_ap.unsqueeze(2).to_broadcast([128, 16, 128]), op=ALU.mult),
                    r=[ball, C.bc], w=[b_rhs1])
            cv, b_cv = cv_r.next()
            S.dma("sp", [(cv[:, :, :], xbc_v[:, :, tok0:tok0 + 128])], w=[b_cv])
            st.zt, st.b_zt = z_t.next()
            zt, b_zt = st.zt, st.b_zt
            S.dma("sp", [(zt[:, :], W["z"][tok0:tok0 + 128, :])], w=[b_zt])
            st.cvb, st.b_cvb = cvb_r.next()
            cvb, b_cvb = st.cvb, st.b_cvb
            S.op("dve", lambda e: e.tensor_copy(out=cvb[:, :, :], in_=cv[:, 8:12, :]), r=[b_cv], w=[b_cvb])
            for k in range(8):
                S.op("pe", lambda e, k=k: e.transpose(out=PP[0][:, k * 128:(k + 1) * 128], in_=cv[:, k, :],
                                                      identity=C.identF[:, :]),
                     r=[b_cv, C.bc], w=[PB[0][k // 4]])
            for g in range(2):
                S.op("pe", lambda e, g=g: e.transpose(out=pBt[:, g * 128:(g + 1) * 128], in_=cv[:, 8 + g, :],
                                                      identity=C.identF[:, :]),
                     r=[b_cv, C.bc], w=[PB[3][1]])
            st.xs, st.b_xs = xs_r.next()
            xs_tok, b_xs = st.xs, st.b_xs
            S.op("act", lambda e: e.activation(out=xs_tok[:, :], in_=PP[0][:, :], func=AF.Copy),
                 r=PB[0], w=[b_xs])
            st.Bt, st.b_Bt = Bt_r.next()
            B_tok, b_Bt = st.Bt, st.b_Bt
            S.op("dve", lambda e: e.tensor_copy(out=B_tok[:, :], in_=pBt), r=[PB[3][1]], w=[b_Bt])
            x_bf, b_xbf = xbf_r.next()
            S.op("pool", lambda e: e.tensor_tensor(out=h16(x_bf[:, :]), in0=h16(xs_tok[:, :]),
                                                   in1=bc16(dt_all[:, c, :], 0, 16), op=ALU.mult),
                 r=[b_xs, ball], w=[b_xbf])
            st.xdec, st.b_xdec = xdec_r.next()
            xdec, b_xdec = st.xdec, st.b_xdec
            S.op("dve", lambda e: e.tensor_tensor(out=h16(xdec[:, :]), in0=h16(xs_tok[:, :]),
                                                   in1=bc16(ddec_all[:, c, :], 0, 16), op=ALU.mult),
                 r=[b_xs, ball], w=[b_xdec])
            for g in range(2):
                S.op("pe", lambda e, g=g: e.matmul(pcb[:, g * 128:(g + 1) * 128], lhsT=cvb[:, g, :],
                                                   rhs=cvb[:, 2 + g, :], start=True, stop=True),
                     r=[b_cvb], w=[b_cb])
            st.cbs, st.b_cbs = cbs_r.next()
            cbs, b_cbs = st.cbs, st.b_cbs
            S.op("act", lambda e: e.activation(out=cbs[:, :], in_=pcb, func=AF.Copy), r=[b_cb], w=[b_cbs])
            st.M = {}
            st.x_bf, st.b_xbf = x_bf, b_xbf

        def QD(c, qd):
            st = CH[c]
            g = qd // 2
            dps, b_d = Dbanks.next()
            for hl in range(2):
                S.op("pe", lambda e, hl=hl: e.matmul(
                    dps, lhsT=C.triUpB[:, :], rhs=rhs1_of(st)[:, hl, qd * 4:(qd + 1) * 4, :],
                    start=(hl == 0), stop=False),
                    r=[st.b_rhs1, C.bc], w=[b_d])
            S.op("pe", lambda e: e.matmul(dps, lhsT=C.identB[:, :], rhs=nm4[:, :, :], start=False, stop=True),
                 r=[bk, C.bc], w=[b_d])
            lm_t, b_lm = lm.next()
            S.op("act", lambda e: e.activation(out=lm_t[:, :], in_=dps, func=AF.Exp), r=[b_d], w=[b_lm])
            M_t, b_M = Mt.next()
            S.op("dve", lambda e: e.tensor_tensor(
                out=M_t[:, :, :], in0=lm_t[:, :].rearrange("p (h l) -> p h l", h=4),
                in1=pcb_of(st)[:, g * 128:(g + 1) * 128].unsqueeze(1).to_broadcast([128, 4, 128]), op=ALU.mult),
                r=[b_lm, st.b_cbs], w=[b_M])
            st.M[qd] = (M_t, b_M)

        def QY(c, qd):
            st = CH[c]
            M_t, b_M = st.M[qd]
            for hh in range(4):
                h = qd * 4 + hh
                S.op("pe", lambda e, hh=hh, h=h: e.matmul(
                    PP[1][:, h * 64:(h + 1) * 64], lhsT=M_t[:, hh, :], rhs=st.x_bf[:, h * 64:(h + 1) * 64],
                    start=True, stop=True),
                    r=[b_M, st.b_xbf], w=[PB[1][h // 8]])

        def F3(c):
            st = CH[c]
            xs_tok, b_xs = st.xs, st.b_xs
            st.y1, st.b_y1 = y1_r.next()
            y1, b_y1 = st.y1, st.b_y1
            S.op("pool", lambda e: e.tensor_tensor(out=h16(y1[:, :]), in0=h16(xs_tok[:, :]),
                                                   in1=bc16(dsk, 0, 16), op=ALU.mult),
                 r=[b_xs, bk], w=[b_y1])
            S.op("dve", lambda e: e.tensor_tensor(out=y1[:, :], in0=y1[:, :], in1=PP[1][:, :], op=ALU.add),
                 r=[b_y1] + PB[1], w=[b_y1])

        def mid(c):
            st = CH[c]
            cvb, b_cvb = st.cvb, st.b_cvb
            if c > 0:
                for g in range(2):
                    yps, b_yp = Dbanks.next()
                    S.op("pe", lambda e, g=g, yps=yps: e.matmul(yps, lhsT=cvb[:, 2 + g, :],
                                                                rhs=hsb[:, g * 512:(g + 1) * 512],
                                                                start=True, stop=True),
                         r=[b_cvb, b_hsb], w=[b_yp])
                    S.op("dve", lambda e, g=g, yps=yps: e.tensor_tensor(
                        out=h16(yoffs[:, g * 512:(g + 1) * 512]), in0=h16(yps), in1=bc16(edt_all[:, 0, c, :], 8 * g, 8),
                        op=ALU.mult),
                        r=[b_yp, ball], w=[b_yoffs])
            if c < NT - 1:
                for g in range(2):
                    sps, b_sp = Dbanks.next()
                    S.op("pe", lambda e, g=g, sps=sps: e.matmul(sps, lhsT=st.Bt[:, g * 128:(g + 1) * 128],
                                                                rhs=st.xdec[:, g * 512:(g + 1) * 512],
                                                                start=True, stop=True),
                         r=[st.b_Bt, st.b_xdec], w=[b_sp])
                    hs_g = hstate[:, g * 512:(g + 1) * 512]
                    if c == 0:
                        S.op("dve", lambda e, hs_g=hs_g, sps=sps: e.tensor_copy(out=hs_g, in_=sps),
                             r=[b_sp], w=[b_hst])
                    else:
                        S.op("dve", lambda e, hs_g=hs_g, g=g: e.tensor_tensor(
                            out=h16(hs_g), in0=h16(hs_g), in1=bc16(edt_all[:, 2, c, :], 8 * g, 8), op=ALU.mult),
                            r=[b_hst, ball], w=[b_hst])
                        S.op("dve", lambda e, hs_g=hs_g, sps=sps: e.tensor_tensor(out=hs_g, in0=hs_g, in1=sps,
                                                                                   op=ALU.add),
                             r=[b_hst, b_sp], w=[b_hst])
                    S.op("act", lambda e, hs_g=hs_g, g=g: e.activation(out=hsb[:, g * 512:(g + 1) * 512], in_=hs_g,
                                                                      func=AF.Copy),
                         r=[b_hst], w=[b_hsb])

        def backA(c):
            st = CH[c]
            tok0 = c * 128
            y1, b_y1, zt, b_zt = st.y1, st.b_y1, st.zt, st.b_zt
            ss, b_ss = ss_r.next()
            if c > 0:
                S.op("dve", lambda e: e.tensor_tensor(out=y1[:, :], in0=y1[:, :], in1=yoffs[:, :], op=ALU.add),
                     r=[b_y1, b_yoffs], w=[b_y1])
            S.op("pool", lambda e: e.tensor_tensor(out=y1[:, :], in0=y1[:, :], in1=zt[:, :], op=ALU.mult),
                 r=[b_y1, b_zt], w=[b_y1])
            for g in range(2):
                S.op("act", lambda e, g=g: e.activation(out=junk[:, g * 512:(g + 1) * 512],
                                                        in_=y1[:, g * 512:(g + 1) * 512], func=AF.Square,
                                                        accum_out=ss[:, g:g + 1]),
                     r=[b_y1], w=[b_junk, b_ss])
            S.op("act", lambda e: e.activation(out=ss[:, 2:4], in_=ss[:, 0:2], func=AF.Ln, scale=1.0 / 512, bias=EPS),
                 r=[b_ss], w=[b_ss])
            S.op("act", lambda e: e.activation(out=ss[:, 4:6], in_=ss[:, 2:4], func=AF.Exp, scale=-0.5),
                 r=[b_ss], w=[b_ss])
            st.ss, st.b_ss = ss, b_ss

        def backB(c):
            st = CH[c]
            tok0 = c * 128
            y1, b_y1, ss, b_ss = st.y1, st.b_y1, st.ss, st.b_ss
            for g in range(2):
                S.op("dve", lambda e, g=g: e.scalar_tensor_tensor(
                    out=y1[:, g * 512:(g + 1) * 512], in0=y1[:, g * 512:(g + 1) * 512], scalar=ss[:, 4 + g:5 + g],
                    in1=gssd[:, g * 512:(g + 1) * 512], op0=ALU.mult, op1=ALU.mult),
                    r=[b_y1, b_ss, bk], w=[b_y1])
            for k in range(8):
                S.op("pe", lambda e, k=k: e.transpose(out=PP[0][:, k * 128:(k + 1) * 128],
                                                      in_=y1[:, k * 128:(k + 1) * 128], identity=C.identF[:, :]),
                     r=[b_y1, C.bc], w=[PB[0][k // 4]])
            o_t, b_ot = ot.next()
            S.op("act", lambda e: e.activation(out=o_t[:, :, :],
                                               in_=PP[0][:, :].rearrange("p (k t) -> p k t", k=8), func=AF.Copy),
                 r=PB[0], w=[b_ot])
            S.dma("act", [(ossd_v[:, :, tok0:tok0 + 128], o_t[:, :, :])], r=[b_ot])
            del CH[c]

        def whole_front(c):
            front(c)
            for qd in range(4):
                QD(c, qd)
                QY(c, qd)
            F3(c)

        whole_front(0)
        for c in range(NT):
            n = c + 1
            if n < NT:
                front(n)
                QD(n, 0)
            mid(c)
            if n < NT:
                QD(n, 1)
                QY(n, 0)
            backA(c)
            if n < NT:
                QD(n, 2)
                QY(n, 1)
            backB(c)
            if n < NT:
                QD(n, 3)
                QY(n, 2)
                QY(n, 3)
                F3(n)
        S.deferred_flush()


def ssd_steps(C):
    nc, S, I, W = C.nc, C.S, C.I, C.W
    PP, PB = C.PP, C.PB
    with ExitStack() as es:
        def sb(name, shape, dt):
            return es.enter_context(nc.sbuf_tensor(name + "_p%d" % S.phase, shape, dt))
        bk = Buf("ssdconst")
        dtb = sb("dtb", [128, 16], F32)
        abc = sb("abc", [128, 16], F32)
        dsk = sb("dsk", [128, 16], F32)
        gssd = sb("gssd", [128, D], F32)
        nm4 = sb("nm4", [128, 4, 128], BF16)
        S.dma("sp", [(dtb[:, :], I["dt_bias"][0:1, :].partition_broadcast(128))], w=[bk], key="ssdc")
        S.dma("sp", [(abc[:, :], I["a_log"][0:1, :].partition_broadcast(128))], w=[bk], key="ssdc")
        S.dma("sp", [(dsk[:, :], I["d_skip"][0:1, :].partition_broadcast(128))], w=[bk], key="ssdc")
        S.dma("sp", [(gssd[:, :], I["ssd_norm_g"][0:1, :].partition_broadcast(128))], w=[bk], key="ssdc")
        S.op("act", lambda e: e.activation(out=abc[:, :], in_=abc[:, :], func=AF.Exp), r=[bk], w=[bk])
        S.op("dve", lambda e: e.tensor_scalar(out=abc[:, :], in0=abc[:, :], scalar1=-1.0, scalar2=None,
                                              op0=ALU.mult), r=[bk], w=[bk])
        for i in range(4):
            S.op("dve", lambda e, i=i: e.tensor_copy(out=nm4[:, i, :], in_=C.negMaskSB[:, :]), r=[C.bc], w=[bk])

        def rot2(name, shape, dt, n=2):
            return Rot([(sb("%s%d" % (name, i), shape, dt), Buf("%s%d" % (name, i))) for i in range(n)])

        cv_r = rot2("cv", [128, 12, 128], F32, 3)
        dtr_t = rot2("dtr", [128, 16], F32)
        z_t = rot2("zt", [128, D], F32, 3)
        cvb_r = rot2("cvb", [128, 4, 128], BF16)
        xs_r = rot2("xs_tok", [128, D], F32)
        Bt_r = rot2("B_tok", [128, 256], BF16)
        sm_r = rot2("sm", [128, 8, 16], F32)
        edt_r = rot2("edt", [128, 48], F32)
        xbf_r = rot2("x_bf", [128, D], BF16)
        xdec_r = rot2("xdec", [128, D], BF16)
        y1_r = rot2("y1", [128, D], F32)
        rhs1_r = rot2("rhs1", [128, 2, 16, 128], BF16)
        smb, b_smb = sb("smb", [128, 2, 16], BF16), Buf("smb")
        cbs_r = rot2("cbs", [128, 256], F32)
        Mt = rot2("Mt", [128, 4, 128], BF16, 4)

        def rhs1_of(st):
            return st.rhs1

        def pcb_of(st):
            return st.cbs
        lm = rot2("lm", [128, 512], F32)
        yoffs, b_yoffs = sb("yoffs", [128, D], F32), Buf("yoffs")
        hstate, b_hst = sb("hstate", [128, D], F32), Buf("hstate")
        hsb, b_hsb = sb("hsb", [128, D], BF16), Buf("hsb")
        junk, b_junk = sb("junk", [128, D], F32), Buf("junk")
        ss_r = rot2("ss", [128, 8], F32)
        ot = rot2("ot", [128, 8, 128], BF16)
        SBK, bb = PP[3][:, 512:1024], PB[3][1]
        Dbanks = Rot([(SBK, bb)])
        b_cb = bb
        pcb = SBK[:, 0:256]
        pBt = SBK[:, 256:512]
        xbc_v = W["xbcT"].rearrange("(ch p) t -> p ch t", p=128)
        ossd_v = W["ossdT"].rearrange("(k p) t -> p k t", p=128)

        def bc16(ap16, n0, n):
            return ap16[:, n0:n0 + n].unsqueeze(2).to_broadcast([128, n, 64])

        def h16(ap):
            return ap.rearrange("p (h d) -> p h d", d=64)

        ball = Buf("ssd_all")
        dtr_all = sb("dtr_all", [128, NT, 16], F32)
        dt_all = sb("dt_all", [128, NT, 16], F32)
        adt_all = sb("adt_all", [128, NT, 16], F32)
        adt_hi = sb("adt_hi", [128, NT, 16], BF16)
        adt_lo = sb("adt_lo", [128, NT, 16], F32)
        edt_all = sb("edt_all", [128, 3, NT, 16], F32)
        ddec_all = sb("ddec_all", [128, NT, 16], F32)
        S.dma("sp", [(dtr_all[:, :, :], W["dtr"].rearrange("(c p) h -> p c h", p=128))], w=[ball])

        def bcc(ap16):
            return ap16[:, :].unsqueeze(1).to_broadcast([128, NT, 16])

        S.op("dve", lambda e: e.tensor_tensor(out=dtr_all[:, :, :], in0=dtr_all[:, :, :], in1=bcc(dtb), op=ALU.add),
             r=[ball, bk], w=[ball])
        S.op("act", lambda e: e.activation(out=dtr_all[:, :, :], in_=dtr_all[:, :, :], func=AF.Exp),
             r=[ball], w=[ball])
        S.op("act", lambda e: e.activation(out=dt_all[:, :, :], in_=dtr_all[:, :, :], func=AF.Ln, bias=1.0),
             r=[ball], w=[ball])
        S.op("dve", lambda e: e.tensor_tensor(out=adt_all[:, :, :], in0=dt_all[:, :, :], in1=bcc(abc), op=ALU.mult),
             r=[ball, bk], w=[ball])
        S.op("dve", lambda e: e.tensor_copy(out=adt_hi[:, :, :], in_=adt_all[:, :, :]), r=[ball], w=[ball])
        S.op("dve", lambda e: e.tensor_tensor(out=adt_lo[:, :, :], in0=adt_all[:, :, :], in1=adt_hi[:, :, :],
                                              op=ALU.subtract), r=[ball], w=[ball])
        for i, lt in enumerate((C.triInclF, C.triUpF, C.onesF)):
            pq = SBK
            S.op("pe", lambda e, lt=lt, pq=pq: e.matmul(pq, lhsT=lt[:, :],
                                                        rhs=adt_all[:, :, :].rearrange("p c h -> p (c h)"),
                                                        start=True, stop=True),
                 r=[ball, C.bc], w=[bb])
            S.op("act", lambda e, i=i, pq=pq: e.activation(out=edt_all[:, i, :, :].rearrange("p c h -> p (c h)"),
                                                           in_=pq, func=AF.Exp),
                 r=[bb], w=[ball])
        S.op("dve", lambda e: e.tensor_tensor(out=ddec_all[:, :, :], in0=dt_all[:, :, :], in1=edt_all[:, 1, :, :],
                                              op=ALU.mult), r=[ball], w=[ball])

        CH = {}

        class St:
            pass

        def front(c):
            st = St()
            CH[c] = st
            tok0 = c * 128
            st.rhs1, st.b_rhs1 = rhs1_r.next()
            rhs1, b_rhs1 = st.rhs1, st.b_rhs1
            for hl, src_ap, eng in ((0, adt_hi[:, c, :], "pool"), (1, adt_lo[:, c, :], "dve")):
                S.op(eng, lambda e, hl=hl, src_ap=src_ap: e.tensor_tensor(
                    out=rhs1[:, hl, :, :], in0=C.triInclF[:, :].unsqueeze(1).to_broadcast([128, 16, 128]),
                    in1=src_ap.unsqueeze(2).to_broadcast([128, 16, 128]), op=ALU.mult),
                    r=[ball, C.bc], w=[b_rhs1])
            cv, b_cv = cv_r.next()
            S.dma("sp", [(cv[:, :, :], xbc_v[:, :, tok0:tok0 + 128])], w=[b_cv])
            st.zt, st.b_zt = z_t.next()
            zt, b_zt = st.zt, st.b_zt
            S.dma("sp", [(zt[:, :], W["z"][tok0:tok0 + 128, :])], w=[b_zt])
            st.cvb, st.b_cvb = cvb_r.next()
            cvb, b_cvb = st.cvb, st.b_cvb
            S.op("dve", lambda e: e.tensor_copy(out=cvb[:, :, :], in_=cv[:, 8:12, :]), r=[b_cv], w=[b_cvb])
            st.xs, st.b_xs = xs_r.next()
            xs_tok, b_xs = st.xs, st.b_xs
            for half in range(2):
                for kk in range(4):
                    k = half * 4 + kk
                    S.op("pe", lambda e, k=k, kk=kk: e.transpose(out=SBK[:, kk * 128:(kk + 1) * 128], in_=cv[:, k, :],
                                                              identity=C.identF[:, :]),
                         r=[b_cv, C.bc], w=[bb])
                S.op("dve", lambda e, half=half: e.tensor_copy(out=xs_tok[:, half * 512:(half + 1) * 512], in_=SBK),
                     r=[bb], w=[b_xs])
            for gg in range(2):
                S.op("pe", lambda e, gg=gg: e.transpose(out=pBt[:, gg * 128:(gg + 1) * 128], in_=cv[:, 8 + gg, :],
                                                        identity=C.identF[:, :]),
                     r=[b_cv, C.bc], w=[bb])
            st.Bt, st.b_Bt = Bt_r.next()
            B_tok, b_Bt = st.Bt, st.b_Bt
            S.op("dve", lambda e: e.tensor_copy(out=B_tok[:, :], in_=pBt), r=[bb], w=[b_Bt])
            x_bf, b_xbf = xbf_r.next()
            S.op("pool", lambda e: e.tensor_tensor(out=h16(x_bf[:, :]), in0=h16(xs_tok[:, :]),
                                                   in1=bc16(dt_all[:, c, :], 0, 16), op=ALU.mult),
                 r=[b_xs, ball], w=[b_xbf])
            st.xdec, st.b_xdec = xdec_r.next()
            xdec, b_xdec = st.xdec, st.b_xdec
            S.op("dve", lambda e: e.tensor_tensor(out=h16(xdec[:, :]), in0=h16(xs_tok[:, :]),
                                                   in1=bc16(ddec_all[:, c, :], 0, 16), op=ALU.mult),
                 r=[b_xs, ball], w=[b_xdec])
            for g in range(2):
                S.op("pe", lambda e, g=g: e.matmul(pcb[:, g * 128:(g + 1) * 128], lhsT=cvb[:, g, :],
                                                   rhs=cvb[:, 2 + g, :], start=True, stop=True),
                     r=[b_cvb], w=[b_cb])
            st.cbs, st.b_cbs = cbs_r.next()
            cbs, b_cbs = st.cbs, st.b_cbs
            S.op("dve", lambda e: e.tensor_copy(out=cbs[:, :], in_=pcb), r=[b_cb], w=[b_cbs])
            st.M = {}
            st.x_bf, st.b_xbf = x_bf, b_xbf

        def QD(c, qd):
            st = CH[c]
            g = qd // 2
            dps, b_d = Dbanks.next()
            for hl in range(2):
                S.op("pe", lambda e, hl=hl: e.matmul(
                    dps, lhsT=C.triUpB[:, :], rhs=rhs1_of(st)[:, hl, qd * 4:(qd + 1) * 4, :],
                    start=(hl == 0), stop=False),
                    r=[st.b_rhs1, C.bc], w=[b_d])
            S.op("pe", lambda e: e.matmul(dps, lhsT=C.identB[:, :], rhs=nm4[:, :, :], start=False, stop=True),
                 r=[bk, C.bc], w=[b_d])
            lm_t, b_lm = lm.next()
            S.op("act", lambda e: e.activation(out=lm_t[:, :], in_=dps, func=AF.Exp), r=[b_d], w=[b_lm])
            M_t, b_M = Mt.next()
            S.op("dve", lambda e: e.tensor_tensor(
                out=M_t[:, :, :], in0=lm_t[:, :].rearrange("p (h l) -> p h l", h=4),
                in1=pcb_of(st)[:, g * 128:(g + 1) * 128].unsqueeze(1).to_broadcast([128, 4, 128]), op=ALU.mult),
                r=[b_lm, st.b_cbs], w=[b_M])
            st.M[qd] = (M_t, b_M)

        def QY(c, qd):
            st = CH[c]
            M_t, b_M = st.M[qd]
            if qd == 0:
                st.y1, st.b_y1 = y1_r.next()
                S.op("pool", lambda e: e.tensor_tensor(out=h16(st.y1[:, :]), in0=h16(st.xs[:, :]),
                                                       in1=bc16(dsk, 0, 16), op=ALU.mult),
                     r=[st.b_xs, bk], w=[st.b_y1])
            for hh in range(4):
                h = qd * 4 + hh
                S.op("pe", lambda e, hh=hh, h=h: e.matmul(
                    SBK[:, hh * 64:(hh + 1) * 64], lhsT=M_t[:, hh, :], rhs=st.x_bf[:, h * 64:(h + 1) * 64],
                    start=True, stop=True),
                    r=[b_M, st.b_xbf], w=[bb])
            S.op("dve", lambda e: e.tensor_tensor(out=st.y1[:, qd * 256:(qd + 1) * 256],
                                                  in0=st.y1[:, qd * 256:(qd + 1) * 256], in1=SBK[:, 0:256], op=ALU.add),
                 r=[st.b_y1, bb], w=[st.b_y1])

        def F3(c):
            pass

        def mid(c):
            st = CH[c]
            cvb, b_cvb = st.cvb, st.b_cvb
            if c > 0:
                for g in range(2):
                    yps, b_yp = Dbanks.next()
                    S.op("pe", lambda e, g=g, yps=yps: e.matmul(yps, lhsT=cvb[:, 2 + g, :],
                                                                rhs=hsb[:, g * 512:(g + 1) * 512],
                                                                start=True, stop=True),
                         r=[b_cvb, b_hsb], w=[b_yp])
                    S.op("dve", lambda e, g=g, yps=yps: e.tensor_tensor(
                        out=h16(yoffs[:, g * 512:(g + 1) * 512]), in0=h16(yps), in1=bc16(edt_all[:, 0, c, :], 8 * g, 8),
                        op=ALU.mult),
                        r=[b_yp, ball], w=[b_yoffs])
            if c < NT - 1:
                for g in range(2):
                    sps, b_sp = Dbanks.next()
                    S.op("pe", lambda e, g=g, sps=sps: e.matmul(sps, lhsT=st.Bt[:, g * 128:(g + 1) * 128],
                                                                rhs=st.xdec[:, g * 512:(g + 1) * 512],
                                                                start=True, stop=True),
                         r=[st.b_Bt, st.b_xdec], w=[b_sp])
                    hs_g = hstate[:, g * 512:(g + 1) * 512]
                    if c == 0:
                        S.op("dve", lambda e, hs_g=hs_g, sps=sps: e.tensor_copy(out=hs_g, in_=sps),
                             r=[b_sp], w=[b_hst])
                    else:
                        S.op("dve", lambda e, hs_g=hs_g, g=g: e.tensor_tensor(
                            out=h16(hs_g), in0=h16(hs_g), in1=bc16(edt_all[:, 2, c, :], 8 * g, 8), op=ALU.mult),
                            r=[b_hst, ball], w=[b_hst])
                        S.op("dve", lambda e, hs_g=hs_g, sps=sps: e.tensor_tensor(out=hs_g, in0=hs_g, in1=sps,
                                                                                   op=ALU.add),
                             r=[b_hst, b_sp], w=[b_hst])
                    S.op("pool", lambda e, hs_g=hs_g, g=g: e.tensor_copy(out=hsb[:, g * 512:(g + 1) * 512], in_=hs_g),
                         r=[b_hst], w=[b_hsb])

        def backA(c):
            st = CH[c]
            tok0 = c * 128
            y1, b_y1, zt, b_zt = st.y1, st.b_y1, st.zt, st.b_zt
            ss, b_ss = ss_r.next()
            if c > 0:
                S.op("dve", lambda e: e.tensor_tensor(out=y1[:, :], in0=y1[:, :], in1=yoffs[:, :], op=ALU.add),
                     r=[b_y1, b_yoffs], w=[b_y1])
            S.op("pool", lambda e: e.tensor_tensor(out=y1[:, :], in0=y1[:, :], in1=zt[:, :], op=ALU.mult),
                 r=[b_y1, b_zt], w=[b_y1])
            for g in range(2):
                S.op("act", lambda e, g=g: e.activation(out=junk[:, g * 512:(g + 1) * 512],
                                                        in_=y1[:, g * 512:(g + 1) * 512], func=AF.Square,
                                                        accum_out=ss[:, g:g + 1]),
                     r=[b_y1], w=[b_junk, b_ss])
            S.op("act", lambda e: e.activation(out=ss[:, 2:4], in_=ss[:, 0:2], func=AF.Ln, scale=1.0 / 512, bias=EPS),
                 r=[b_ss], w=[b_ss])
            S.op("act", lambda e: e.activation(out=ss[:, 4:6], in_=ss[:, 2:4], func=AF.Exp, scale=-0.5),
                 r=[b_ss], w=[b_ss])
            st.ss, st.b_ss = ss, b_ss

        def backB(c):
            st = CH[c]
            tok0 = c * 128
            y1, b_y1, ss, b_ss = st.y1, st.b_y1, st.ss, st.b_ss
            for g in range(2):
                S.op("dve", lambda e, g=g: e.scalar_tensor_tensor(
                    out=y1[:, g * 512:(g + 1) * 512], in0=y1[:, g * 512:(g + 1) * 512], scalar=ss[:, 4 + g:5 + g],
                    in1=gssd[:, g * 512:(g + 1) * 512], op0=ALU.mult, op1=ALU.mult),
                    r=[b_y1, b_ss, bk], w=[b_y1])
            o_t, b_ot = ot.next()
            for half in range(2):
                for kk in range(4):
                    k = half * 4 + kk
                    S.op("pe", lambda e, k=k, kk=kk: e.transpose(out=SBK[:, kk * 128:(kk + 1) * 128],
                                                              in_=y1[:, k * 128:(k + 1) * 128], identity=C.identF[:, :]),
                         r=[b_y1, C.bc], w=[bb])
                S.op("dve", lambda e, half=half: e.tensor_copy(
                    out=o_t[:, half * 4:(half + 1) * 4, :], in_=SBK.rearrange("p (k t) -> p k t", k=4)),
                    r=[bb], w=[b_ot])
            S.dma("sp", [(ossd_v[:, :, tok0:tok0 + 128], o_t[:, :, :])], r=[b_ot])
            del CH[c]

        front(0)
        yield
        for qd in range(4):
            QD(0, qd)
            QY(0, qd)
            yield
        for c in range(NT):
            n = c + 1
            if n < NT:
                front(n)
                yield
                QD(n, 0)
                yield
            mid(c)
            yield
            if n < NT:
                QD(n, 1)
                QY(n, 0)
                yield
            backA(c)
            yield
            if n < NT:
                QD(n, 2)
                QY(n, 1)
                yield
            backB(c)
            yield
            if n < NT:
                QD(n, 3)
                QY(n, 2)
                yield
                QY(n, 3)
                yield
        yield "done"


def ssd_phase_1bank(C):
    gen = ssd_steps(C)
    for r in gen:
        if r == "done":
            break
    C.S.deferred_flush()
    for _ in gen:
        pass


def outxa_phase(C):
    nc, S, I, W = C.nc, C.S, C.I, C.W
    PP, PB = C.PP, C.PB
    with ExitStack() as es:
        def sb(name, shape, dt):
            return es.enter_context(nc.sbuf_tensor(name + "_p%d" % S.phase, shape, dt))
        kxT, b_kxT = sb("kxT", [128, 8, MEM], BF16), Buf("kxT")
        vx, b_vx = sb("vx", [128, 2, D], BF16), Buf("vx")
        banks = Rot([(PP[i][:, h * 512:(h + 1) * 512], PB[i][h]) for i in range(2) for h in range(2)])
        with ExitStack() as es1:
            def sb1(name, shape, dt):
                return es1.enter_context(nc.sbuf_tensor(name + "_q%d" % S.phase, shape, dt))
            wkv = sb1("wkv", [128, 8, 2 * D], BF16)
            b_wkv = [Buf("wkv0"), Buf("wkv1")]
            load_w_bf16(C, wkv, I["xa_wkv"], 0, 8, [(0, 1024), (1024, 2048)], b_wkv)
            gmT, b_gmT = sb1("gmT", [128, 8], F32), Buf("gmT")
            load_gT(C, gmT, I["mem_g"], b_gmT)
            pool1 = {"ss": Rot([(sb1("ss%d" % i, [128, 4], F32), Buf("ss%d" % i)) for i in range(2)]),
                     "junk": (sb1("junk", [128, D], F32), Buf("junk")),
                     "xn": (sb1("xn", [128, D], F32), Buf("xn"))}
            memT, b_memT = sb1("memT", [128, 8, MEM], BF16), [Buf("memT0"), Buf("memT1")]
            for mb in range(2):
                mt, b_mt = sb1("memt%d" % mb, [128, D], F32), Buf("memt%d" % mb)
                S.dma("sp", [(mt[:, :], I["mem"][mb * 128:(mb + 1) * 128, :])], w=[b_mt])
                norm_T(C, pool1, mt[:, :], b_mt, gmT, b_gmT, memT, mb * 128, b_memT[mb], PP[2 + mb], PB[2 + mb])
            for c in range(8):
                pa, pba = banks.next()
                for k in range(8):
                    S.op("pe", lambda e, k=k, c=c, pa=pa: e.matmul(
                        pa[:, 0:MEM], lhsT=wkv[:, k, c * 128:(c + 1) * 128], rhs=memT[:, k, :],
                        start=(k == 0), stop=(k == 7)), r=[b_wkv[0]] + b_memT, w=[pba])
                S.op("dve", lambda e, c=c, pa=pa: e.tensor_copy(out=kxT[:, c, :], in_=pa[:, 0:MEM]),
                     r=[pba], w=[b_kxT])
            for mb in range(2):
                for n in range(2):
                    pa, pba = banks.next()
                    for k in range(8):
                        S.op("pe", lambda e, k=k, mb=mb, n=n, pa=pa: e.matmul(
                            pa, lhsT=memT[:, k, mb * 128:(mb + 1) * 128],
                            rhs=wkv[:, k, D + n * 512:D + (n + 1) * 512], start=(k == 0), stop=(k == 7)),
                            r=[b_wkv[1], b_memT[mb]], w=[pba])
                    S.op("act", lambda e, mb=mb, n=n, pa=pa: e.activation(out=vx[:, mb, n * 512:(n + 1) * 512],
                                                                          in_=pa, func=AF.Copy),
                         r=[pba], w=[b_vx])
            S.deferred_flush()
        wout = sb("wout", [128, 16, D], BF16)
        b_wout = [Buf("wout0"), Buf("wout1")]
        wq, b_wq = sb("wq", [128, 8, D], BF16), Buf("wq")
        wo, b_wo = sb("wo", [128, 8, D], BF16), Buf("wo")
        gaT, b_gaT = sb("gaT", [128, 8], F32), Buf("gaT")
        gxT, b_gxT = sb("gxT", [128, 8], F32), Buf("gxT")
        gb_mix, b_gbm = sb("gb_mix", [128, D], F32), Buf("gb_mix")
        gb_xa, b_gbx = sb("gb_xa", [128, D], F32), Buf("gb_xa")
        load_gT(C, gaT, I["attn_norm_g"], b_gaT)
        load_gT(C, gxT, I["xa_pre_g"], b_gxT)
        load_bcast(C, gb_mix, I["mix_post_g"], b_gbm, D)
        load_bcast(C, gb_xa, I["xa_post_g"], b_gbx, D)
        for half in range(2):
            S.dma("pool", [(wout[:, k, :], I["w_out"][k * 128:(k + 1) * 128, :]) for k in range(half * 8, half * 8 + 8)],
                  w=[b_wout[half]])
        S.dma("pool", [(wq[:, k, :], I["xa_wq"][k * 128:(k + 1) * 128, :]) for k in range(8)], w=[b_wq])
        S.dma("pool", [(wo[:, k, :], I["xa_wo"][k * 128:(k + 1) * 128, :]) for k in range(8)], w=[b_wo])

        pool = {"ss": Rot([(sb("ss%d" % i, [128, 4], F32), Buf("ss%d" % i)) for i in range(4)]),
                "junk": (sb("junk", [128, D], F32), Buf("junk")),
                "xn": (sb("xn", [128, D], F32), Buf("xn"))}
        t1, b_t1 = sb("t1", [128, D], F32), Buf("t1")
        oa, b_oa = sb("oa", [128, 8, 512], F32), Buf("oa")
        os_, b_os = sb("os", [128, 8, 512], BF16), Buf("os")
        sqk = Rot([(sb("sqk%d" % i, [128, 512], BF16), Buf("sqk%d" % i)) for i in range(3)])
        rsb, b_rsb = sb("rsb", [128, 512], F32), Buf("rsb")
        oan, b_oan = sb("oan", [128, 8, 512], BF16), Buf("oan")
        h2 = [(sb("h2_%d" % t, [128, D], F32), Buf("h2_%d" % t)) for t in range(4)]
        hnT = sb("hnT", [128, 8, 512], BF16)
        b_hnT = [Buf("hnT%d" % t) for t in range(4)]
        qxT, b_qxT = sb("qxT", [128, 8, 512], BF16), [Buf("qxT%d" % c) for c in range(8)]
        pT = sb("pT", [128, 4, 2, 512], BF16)
        b_pT = [[Buf("pT%d_%d" % (hh, mb)) for mb in range(2)] for hh in range(4)]
        rden = Rot([(sb("rden%d" % i, [128, 512], F32), Buf("rden%d" % i)) for i in range(2)])
        oxn, b_oxn = sb("oxn", [128, 8, 512], BF16), [Buf("oxn%d" % c) for c in range(8)]
        oat_v = W["oattT"].rearrange("(k p) t -> p k t", p=128)
        oss_v = W["ossdT"].rearrange("(k p) t -> p k t", p=128)

        def S1(blk):
            tok0 = blk * 512
            S.dma("sp", [(oa[:, :, :], oat_v[:, :, tok0:tok0 + 512])], w=[b_oa])
            S.dma("sp", [(os_[:, :, :], oss_v[:, :, tok0:tok0 + 512])], w=[b_os])
            pa, pba = banks.next()
            for k in range(8):
                sq, b_sq = sqk.next()
                S.op("act", lambda e, k=k, sq=sq: e.activation(out=sq[:, :], in_=oa[:, k, :], func=AF.Square),
                     r=[b_oa], w=[b_sq])
                S.op("pe", lambda e, k=k, sq=sq, pa=pa: e.matmul(pa, lhsT=C.onesB[:, :], rhs=sq[:, :],
                                                                 start=(k == 0), stop=(k == 7)),
                     r=[b_sq, C.bc], w=[pba])
            S.op("act", lambda e, pa=pa: e.activation(out=rsb[:, :], in_=pa, func=AF.Ln, scale=1.0 / D, bias=EPS),
                 r=[pba], w=[b_rsb])
            S.op("act", lambda e: e.activation(out=rsb[:, :], in_=rsb[:, :], func=AF.Exp, scale=-0.5),
                 r=[b_rsb], w=[b_rsb])
            for k in range(8):
                S.op("dve", lambda e, k=k: e.scalar_tensor_tensor(out=oan[:, k, :], in0=oa[:, k, :],
                                                                  scalar=gaT[:, k:k + 1], in1=rsb[:, :],
                                                                  op0=ALU.mult, op1=ALU.mult),
                     r=[b_oa, b_gaT, b_rsb], w=[b_oan])

        NBX = SEQ // 512
        S1(0)
        for blk in range(NBX):
            tok0 = blk * 512
            for t in range(4):
                tt = blk * 4 + t
                pf, pbf = PP[2 + t % 2], PB[2 + t % 2]
                for n in range(2):
                    for k in range(16):
                        src_, bsrc = (oan, b_oan) if k < 8 else (os_, b_os)
                        S.op("pe", lambda e, n=n, k=k, t=t, pf=pf, src_=src_: e.matmul(
                            pf[:, n * 512:(n + 1) * 512], lhsT=src_[:, k % 8, t * 128:(t + 1) * 128],
                            rhs=wout[:, k, n * 512:(n + 1) * 512], start=(k == 0), stop=(k == 15)),
                            r=[bsrc, b_wout[k // 8]], w=[pbf[n]])
                xr, bxr = h2[t]
                S.dma("sp", [(xr[:, :], W["h1"][tt * 128:(tt + 1) * 128, :])], w=[bxr])
                if t > 0:
                    xp_, bxp_ = h2[t - 1]
                    norm_T(C, pool, xp_[:, :], bxp_, gxT, b_gxT, hnT, (t - 1) * 128, b_hnT[t - 1],
                           PP[(t - 1) % 2], PB[(t - 1) % 2])
                post_norm_residual(C, pool, pf, pbf, gb_mix, b_gbm, xr, bxr, t1, b_t1, 1.0)
            xp_, bxp_ = h2[3]
            norm_T(C, pool, xp_[:, :], bxp_, gxT, b_gxT, hnT, 3 * 128, b_hnT[3], PP[1], PB[1])
            for c in range(8):
                pa, pba = banks.next()
                for k in range(8):
                    S.op("pe", lambda e, k=k, c=c, pa=pa: e.matmul(
                        pa, lhsT=wq[:, k, c * 128:(c + 1) * 128], rhs=hnT[:, k, :], start=(k == 0), stop=(k == 7)),
                        r=[b_wq] + b_hnT, w=[pba])
                S.op("act", lambda e, c=c, pa=pa: e.activation(out=qxT[:, c, :], in_=pa, func=AF.Copy,
                                                               scale=1.0 / 16.0),
                     r=[pba], w=[b_qxT[c]])
            if blk + 1 < NBX:
                S1(blk + 1)
            for hh in range(4):
                for mb in range(2):
                    pa, pba = banks.next()
                    for dc in range(2):
                        cch = 2 * hh + dc
                        S.op("pe", lambda e, cch=cch, mb=mb, dc=dc, pa=pa: e.matmul(
                            pa, lhsT=kxT[:, cch, mb * 128:(mb + 1) * 128], rhs=qxT[:, cch, :],
                            start=(dc == 0), stop=(dc == 1)),
                            r=[b_kxT, b_qxT[cch]], w=[pba])
                    S.op("act", lambda e, hh=hh, mb=mb, pa=pa: e.activation(out=pT[:, hh, mb, :], in_=pa, func=AF.Exp),
                         r=[pba], w=[b_pT[hh][mb]])
                pa, pba = banks.next()
                for mb in range(2):
                    S.op("pe", lambda e, hh=hh, mb=mb, pa=pa: e.matmul(pa, lhsT=C.onesB[:, :], rhs=pT[:, hh, mb, :],
                                                                       start=(mb == 0), stop=(mb == 1)),
                         r=[b_pT[hh][mb], C.bc], w=[pba])
                rd, b_rd = rden.next()
                S.op("dve", lambda e, rd=rd, pa=pa: e.reciprocal(out=rd[:, :], in_=pa), r=[pba], w=[b_rd])
                for dc in range(2):
                    cch = 2 * hh + dc
                    pa, pba = banks.next()
                    for mb in range(2):
                        S.op("pe", lambda e, cch=cch, mb=mb, hh=hh, pa=pa: e.matmul(
                            pa, lhsT=vx[:, mb, cch * 128:(cch + 1) * 128], rhs=pT[:, hh, mb, :],
                            start=(mb == 0), stop=(mb == 1)),
                            r=[b_vx, b_pT[hh][mb]], w=[pba])
                    S.op("dve", lambda e, cch=cch, rd=rd, pa=pa: e.tensor_tensor(out=oxn[:, cch, :], in0=pa,
                                                                                in1=rd[:, :], op=ALU.mult),
                         r=[pba, b_rd], w=[b_oxn[cch]])
            for t in range(4):
                tt = blk * 4 + t
                pf, pbf = PP[2 + t % 2], PB[2 + t % 2]
                for n in range(2):
                    for k in range(8):
                        S.op("pe", lambda e, n=n, k=k, t=t, pf=pf: e.matmul(
                            pf[:, n * 512:(n + 1) * 512], lhsT=oxn[:, k, t * 128:(t + 1) * 128],
                            rhs=wo[:, k, n * 512:(n + 1) * 512], start=(k == 0), stop=(k == 7)),
                            r=[b_oxn[k], b_wo], w=[pbf[n]])
                xr, bxr = h2[t]
                post_norm_residual(C, pool, pf, pbf, gb_xa, b_gbx, xr, bxr, t1, b_t1, 1.0)
                S.dma("pool", [(W["h3"][tt * 128:(tt + 1) * 128, :], xr[:, :])], r=[bxr])
        S.deferred_flush()
```
